# Optimizing a Trainium2 kernel written in Bass

```python
import math
import jax, jax.numpy as jnp
from jax import lax
import numpy as np

D_MODEL = 1024
BATCH = 4
SEQ = 4096
DEPTH = 2

CHUNK = 64
HEAD_DIM = 64
N_A_LAYERS = DEPTH // 2
N_B_LAYERS = DEPTH - N_A_LAYERS
RMS_EPS = 1e-6
A_HEADS = D_MODEL // HEAD_DIM
A_WIDTH = A_HEADS * HEAD_DIM
A_LEFT_CHUNKS = 8
A_BAND = (A_LEFT_CHUNKS + 1) * CHUNK
A_REL_CLIP = 256
B_Q_HEADS = D_MODEL // HEAD_DIM
B_KV_HEADS = max(1, B_Q_HEADS // 8)
B_GROUP = B_Q_HEADS // B_KV_HEADS
B_WIDTH = B_Q_HEADS * HEAD_DIM
B_KV_WIDTH = B_KV_HEADS * HEAD_DIM
B_WINDOW = 128
B_LEFT_CHUNKS = (B_WINDOW - 1) // CHUNK + 1
B_BAND = (B_LEFT_CHUNKS + 1) * CHUNK
T5_BUCKETS = 32
T5_MAX_DIST = 128

kernel_name = "yoco_chunk_relbias_swa_sink_hybrid"


def rmsnorm(x, g):
    xf = x.astype(jnp.float32)
    y = xf * lax.rsqrt(jnp.mean(xf * xf, axis=-1, keepdims=True) + RMS_EPS)
    return (y * g.astype(jnp.float32)).astype(x.dtype)


def t5_bucket(rel):
    nb = T5_BUCKETS // 2
    max_exact = nb // 2
    ret = jnp.where(rel > 0, nb, 0)
    n = jnp.abs(rel)
    nf = jnp.maximum(n, 1).astype(jnp.float32)
    large = max_exact + (jnp.log(nf / max_exact) / math.log(T5_MAX_DIST / max_exact)
                         * (nb - max_exact)).astype(jnp.int32)
    large = jnp.minimum(large, nb - 1)
    return ret + jnp.where(n < max_exact, n, large)


def mixer_a(x, w_in, rel_bias, w_out):
    b, s, _ = x.shape
    nc = s // CHUNK
    pad = A_LEFT_CHUNKS * CHUNK
    q, k, v, g = jnp.split(x @ w_in, 4, axis=-1)
    q = q.reshape(b, nc, CHUNK, A_HEADS, HEAD_DIM)
    k = jnp.pad(k.reshape(b, s, A_HEADS, HEAD_DIM), ((0, 0), (pad, 0), (0, 0), (0, 0)))
    v = jnp.pad(v.reshape(b, s, A_HEADS, HEAD_DIM), ((0, 0), (pad, 0), (0, 0), (0, 0)))
    dist = jnp.arange(CHUNK)[:, None] + pad - jnp.arange(A_BAND)[None, :]
    idx = jnp.clip(dist, -A_REL_CLIP, A_REL_CLIP) + A_REL_CLIP
    bias = jnp.transpose(rel_bias[idx], (2, 0, 1)).astype(jnp.float32)
    scale = HEAD_DIM ** -0.5

    def one_chunk(args):
        qc, c = args
        kc = lax.dynamic_slice_in_dim(k, c * CHUNK, A_BAND, axis=1)
        vc = lax.dynamic_slice_in_dim(v, c * CHUNK, A_BAND, axis=1)
        logits = jnp.einsum('bqhd,bkhd->bhqk', qc, kc).astype(jnp.float32) * scale + bias
        key_pos = c * CHUNK - pad + jnp.arange(A_BAND)
        logits = jnp.where((key_pos >= 0)[None, None, None, :], logits, -jnp.inf)
        p = jax.nn.softmax(logits, axis=-1).astype(vc.dtype)
        return jnp.einsum('bhqk,bkhd->bqhd', p, vc)

    out = lax.map(one_chunk, (jnp.moveaxis(q, 1, 0), jnp.arange(nc)))
    out = jnp.moveaxis(out, 0, 1).reshape(b, s, A_WIDTH)
    return (out * jax.nn.silu(g)) @ w_out


def mixer_b(x, w_in, sinks, t5_table, k_sh, v_sh, w_out):
    b, s, _ = x.shape
    nc = s // CHUNK
    pad = B_LEFT_CHUNKS * CHUNK
    q, g = jnp.split(x @ w_in, 2, axis=-1)
    q = q.reshape(b, nc, CHUNK, B_KV_HEADS, B_GROUP, HEAD_DIM)
    kp = jnp.pad(k_sh, ((0, 0), (pad, 0), (0, 0), (0, 0)))
    vp = jnp.pad(v_sh, ((0, 0), (pad, 0), (0, 0), (0, 0)))
    band_idx = jnp.arange(nc)[:, None] * CHUNK + jnp.arange(B_BAND)[None, :]
    kb = kp[:, band_idx]
    vb = vp[:, band_idx]
    rel = jnp.arange(B_BAND)[None, :] - pad - jnp.arange(CHUNK)[:, None]
    bias = jnp.transpose(t5_table[t5_bucket(rel)], (2, 0, 1)).astype(jnp.float32)
    bias = bias.reshape(B_KV_HEADS, B_GROUP, 1, CHUNK, B_BAND)
    scale = HEAD_DIM ** -0.5
    logits = jnp.einsum('bcqhgd,bckhd->bhgcqk', q, kb).astype(jnp.float32) * scale + bias
    key_pos = band_idx - pad
    logits = jnp.where((key_pos >= 0)[:, None, :], logits, -jnp.inf)
    sink = jnp.broadcast_to(
        sinks.astype(jnp.float32).reshape(1, B_KV_HEADS, B_GROUP, 1, 1, 1),
        logits.shape[:-1] + (1,))
    p = jax.nn.softmax(jnp.concatenate([logits, sink], axis=-1), axis=-1)[..., :-1]
    out = jnp.einsum('bhgcqk,bckhd->bcqhgd', p.astype(vb.dtype), vb)
    out = out.reshape(b, s, B_WIDTH)
    return (out * jax.nn.silu(g)) @ w_out


def setup_inputs(seed: int = 0) -> dict:
    key = jax.random.key(seed)
    ks = jax.random.split(key, 13)
    f32 = jnp.float32
    nrm = lambda k, shape, sc: (jax.random.normal(k, shape, f32) * sc).astype(f32)
    return {
        "x": nrm(ks[0], (BATCH, SEQ, D_MODEL), 1.0),
        "a_norm": 1.0 + nrm(ks[1], (N_A_LAYERS, D_MODEL), 0.02),
        "a_w_in": nrm(ks[2], (N_A_LAYERS, D_MODEL, 4 * A_WIDTH), D_MODEL ** -0.5),
        "a_rel_bias": nrm(ks[3], (N_A_LAYERS, 2 * A_REL_CLIP + 1, A_HEADS), 0.3),
        "a_w_out": nrm(ks[4], (N_A_LAYERS, A_WIDTH, D_MODEL), A_WIDTH ** -0.5),
        "kv_norm": 1.0 + nrm(ks[5], (D_MODEL,), 0.02),
        "kv_w": nrm(ks[6], (D_MODEL, 2 * B_KV_WIDTH), D_MODEL ** -0.5),
        "t5_bias": nrm(ks[7], (T5_BUCKETS, B_Q_HEADS), 0.3),
        "b_norm": 1.0 + nrm(ks[8], (N_B_LAYERS, D_MODEL), 0.02),
        "b_w_in": nrm(ks[9], (N_B_LAYERS, D_MODEL, 2 * B_WIDTH), D_MODEL ** -0.5),
        "b_sinks": nrm(ks[10], (N_B_LAYERS, B_Q_HEADS), 0.5),
        "b_w_out": nrm(ks[11], (N_B_LAYERS, B_WIDTH, D_MODEL), B_WIDTH ** -0.5),
        "final_norm": 1.0 + nrm(ks[12], (D_MODEL,), 0.02),
    }


def reference(x, a_norm, a_w_in, a_rel_bias, a_w_out, kv_norm, kv_w, t5_bias,
              b_norm, b_w_in, b_sinks, b_w_out, final_norm):
    h = x
    k_sh = v_sh = None
    for layer in range(DEPTH):
        if layer == N_A_LAYERS:
            kv = rmsnorm(h, kv_norm) @ kv_w
            k_sh, v_sh = jnp.split(kv, 2, axis=-1)
            k_sh = k_sh.reshape(h.shape[0], h.shape[1], B_KV_HEADS, HEAD_DIM)
            v_sh = v_sh.reshape(h.shape[0], h.shape[1], B_KV_HEADS, HEAD_DIM)
        if layer < N_A_LAYERS:
            h = h + mixer_a(rmsnorm(h, a_norm[layer]), a_w_in[layer],
                            a_rel_bias[layer], a_w_out[layer])
        else:
            j = layer - N_A_LAYERS
            h = h + mixer_b(rmsnorm(h, b_norm[j]), b_w_in[j], b_sinks[j], t5_bias,
                            k_sh, v_sh, b_w_out[j])
    return rmsnorm(h, final_norm)
```

```python
import math
from contextlib import ExitStack

import numpy as np
import ml_dtypes

import concourse.bass as bass
import concourse.mybir as mybir
from concourse.bass_utils import run_bass_kernel_spmd

F32 = mybir.dt.float32
BF16 = mybir.dt.bfloat16
AF = mybir.ActivationFunctionType
ALU = mybir.AluOpType

N_CORES = 8
D = 1024
KC = 8
SEQ = 4096
NOWN = 2048
HALO = 640
NTA = NOWN + HALO
NU = NOWN + 128
NW = NOWN
NKB_A = NTA // 128
NQB_A = NU // 128
NKB_B = NU // 128
NQB_B = NW // 128
RING_A = 6
RING_B = 4
TILE = 384
EPS = 1e-6

ENGS = ("pe", "act", "dve", "pool", "sp")


class Sched:
    def __init__(self, nc, es):
        self.nc = nc
        self.es = es
        self.sem = {e: es.enter_context(nc.semaphore("s_" + e)) for e in ENGS}
        self.cnt = {e: 0 for e in ENGS}
        self.dsem = {}
        self.dcnt = {}
        self.lastw = {}
        self.readers = {}
        self.pending = {e: [] for e in ENGS}
        self.seen = {e: {} for e in ENGS}
        self.know = {}
        self.order = {}
        self.nwaits = 0

    def op(self, eng, fn, reads=(), writes=(), dma=None, ndma=1):
        deps = {}

        def add(tok):
            if tok is None:
                return
            key, val = tok
            if deps.get(key, 0) < val:
                deps[key] = val

        for r in reads:
            add(self.lastw.get(r))
            if r.startswith("bk") and eng in ("act", "dve"):
                for k, v in self.readers.get(r, {}).items():
                    if k[0] in ("act", "dve") and k[0] != eng:
                        add((k, v))
        for w in writes:
            add(self.lastw.get(w))
            for k, v in self.readers.get(w, {}).items():
                add((k, v))
        if dma is not None:
            if dma not in self.dsem:
                self.dsem[dma] = self.es.enter_context(self.nc.semaphore("d_" + str(dma)))
                self.dcnt[dma] = 0
            self.dcnt[dma] += 16 * ndma
            tok = (("dma", dma), self.dcnt[dma])
        else:
            self.cnt[eng] += 1
            tok = ((eng,), self.cnt[eng])
        waits = []
        for key, val in sorted(deps.items(), key=lambda kv: -self.order.get(kv, 0)):
            if key == ("pe",) and eng == "pe":
                continue
            if self.seen[eng].get(key, 0) >= val:
                continue
            waits.append((key, val))
            for k2, v2 in self.know.get((key, val), {}).items():
                if self.seen[eng].get(k2, 0) < v2:
                    self.seen[eng][k2] = v2
            self.seen[eng][key] = val
        self.nwaits += len(waits)
        self.order[tok] = len(self.order) + 1
        if dma is None:
            kn = dict(self.seen[eng])
            kn[tok[0]] = tok[1]
            self.know[tok] = kn
        else:
            self.know[tok] = dict(self.seen[eng])
        self.pending[eng].append((fn, waits, tok))
        for r in reads:
            d = self.readers.setdefault(r, {})
            if d.get(tok[0], 0) < tok[1]:
                d[tok[0]] = tok[1]
        for w in writes:
            self.lastw[w] = tok
            self.readers[w] = {}
        return tok

    def _semof(self, key):
        if key[0] == "dma":
            return self.dsem[key[1]]
        return self.sem[key[0]]

    def wait_tokens(self, eng, toks):
        self.pending[eng].append((None, list(toks), None))

    def flush(self):
        nc = self.nc
        pend = self.pending
        self.pending = {e: [] for e in ENGS}
        with nc.Block() as block:
            def mk(engname):
                lst = pend[engname]

                def body(e):
                    class _First:
                        def __init__(self, eng):
                            self._eng = eng
                            self.first = None

                        def __getattr__(self, name):
                            real = getattr(self._eng, name)

                            def call(*a, **k):
                                r = real(*a, **k)
                                if self.first is None:
                                    self.first = r
                                return r
                            return call

                    for fn, waits, tok in lst:
                        fuse = fn is not None and len(waits) >= 1
                        for key, val in (waits[:-1] if fuse else waits):
                            e.wait_ge(self._semof(key), val)
                        if fn is None:
                            continue
                        if fuse:
                            px = _First(e)
                            ins = fn(px)
                            px.first._wait_ge(self._semof(waits[-1][0]), waits[-1][1])
                        else:
                            ins = fn(e)
                        key, val = tok
                        if key[0] == "dma":
                            if not isinstance(ins, (list, tuple)):
                                ins = [ins]
                            for i in ins:
                                i.then_inc(self.dsem[key[1]], 16)
                        else:
                            ins.then_inc(self.sem[engname], 1)
                return body

            if pend["pe"]:
                block.tensor(mk("pe"))
            if pend["act"]:
                block.scalar(mk("act"))
            if pend["dve"]:
                block.vector(mk("dve"))
            if pend["pool"]:
                block.gpsimd(mk("pool"))
            if pend["sp"]:
                block.sync(mk("sp"))


def _tiles(n, step):
    return [(a, min(a + step, n)) for a in range(0, n, step)]


def build_nc(debug=()):
    nc = bass.Bass("TRN2", target_bir_lowering=False)

    def din(name, shape, dt=F32):
        return nc.dram_tensor(name, shape, dt, kind="ExternalInput").ap()

    xT = din("xT", [D, NTA])
    wa_qkg = din("wa_qkg", [8, 128, KC * 384])
    wa_v = din("wa_v", [4, 128, KC * 256])
    wa_out = din("wa_out", [128, KC * 1024])
    gains = din("gains", [128, 32])
    biasA = din("biasA", [8, 128, 1280])
    validA = din("validA", [128, NKB_A * 64], BF16)
    kvw = din("kvw", [128, KC * 384])
    wb_qg = din("wb_qg", [8, 128, KC * 256])
    wb_out = din("wb_out", [128, KC * 1024])
    biasB = din("biasB", [8, 128, 512])
    sinkB = din("sinkB", [128, 8])
    validB = din("validB", [128, NKB_B * 64], BF16)
    outT = nc.dram_tensor("outT", [D, NW], F32, kind="ExternalOutput").ap()
    dbg_out = {}
    dbg_shapes = {"xTn": [128, KC * NTA], "attnA": [128, KC * NU], "h1T": [128, KC * NU],
                  "h1n": [128, KC * NU], "attnB": [128, KC * NW], "QT0": [128, NU], "KT0": [128, NTA],
                  "SG0": [128, NU], "V0": [128, NKB_A * 256]}
    for name in debug:
        dbg_out[name] = nc.dram_tensor("dbg_" + name, dbg_shapes[name], F32, kind="ExternalOutput").ap()

    xT_v = xT.rearrange("(kc p) t -> p kc t", p=128)
    outT_v = outT.rearrange("(kc p) t -> p kc t", p=128)

    with ExitStack() as es0:
        S = Sched(nc, es0)
        out_tokens = []

        def sbuf(es, name, shape, dt):
            return es.enter_context(nc.sbuf_tensor(name, shape, dt))

        psum = es0.enter_context(nc.psum_tensor("psum", [128, 8, 512], F32))

        def bk(b):
            return "bk%d" % b

        ones_bf = sbuf(es0, "ones_bf", [128, 128], BF16)
        eps_t = sbuf(es0, "eps_t", [128, 1], F32)
        one_t = sbuf(es0, "one_t", [128, 1], F32)
        eps60 = sbuf(es0, "eps60", [128, 1], F32)
        gains_sb = sbuf(es0, "gains_sb", [128, 32], F32)
        sink_sb = sbuf(es0, "sink_sb", [128, 8], F32)
        esink = sbuf(es0, "esink", [128, 8], F32)
        validB_sb = sbuf(es0, "validB_sb", [128, NKB_B, 64], BF16)
        attnT = sbuf(es0, "attnT", [128, KC, NU], BF16)
        dbgf = sbuf(es0, "dbgf", [128, 512], F32) if debug else None

        def dbg_dump(name, src_fn, ncols, key_reads):
            if name not in dbg_out:
                return
            for (c0, c1) in _tiles(ncols, 512):
                S.op("dve", lambda e, c0=c0, c1=c1: e.tensor_copy(out=dbgf[:, 0:c1 - c0], in_=src_fn(c0, c1)),
                     reads=key_reads, writes=["dbgf"])
                tok = S.op("sp", lambda e, c0=c0, c1=c1: e.dma_start(out=dbg_out[name][:, c0:c1], in_=dbgf[:, 0:c1 - c0]),
                           reads=["dbgf"], dma="dbg")
                out_tokens.append(tok)

        S.op("pool", lambda e: e.memset(ones_bf[:], 1.0), writes=["ones"])
        S.op("pool", lambda e: e.memset(eps_t[:], EPS), writes=["eps"])
        S.op("pool", lambda e: e.memset(one_t[:], 1.0), writes=["one"])
        S.op("pool", lambda e: e.memset(eps60[:], 2.0 ** -60), writes=["eps60"])
        S.op("sp", lambda e: e.dma_start(out=gains_sb[:], in_=gains[:]), writes=["gains"], dma="c0")
        S.op("sp", lambda e: e.dma_start(out=sink_sb[:], in_=sinkB[:]), writes=["sink"], dma="c1")
        S.op("sp", lambda e: e.dma_start(out=validB_sb[:].rearrange("p a b -> p (a b)"), in_=validB[:]),
             writes=["validB"], dma="c2")
        S.op("act", lambda e: e.activation(out=esink[:], in_=sink_sb[:], func=AF.Exp), reads=["sink"], writes=["esink"])

        A_G, KV_G, B_G, F_G = 0, 8, 16, 24

        with ExitStack() as es1:
            xT_bf = sbuf(es1, "xT_bf", [128, KC, NTA], BF16)
            validA_sb = sbuf(es1, "validA_sb", [128, NKB_A, 64], BF16)
            stg = [sbuf(es1, "stg%d" % i, [128, KC * TILE], F32) for i in range(2)]
            sq = sbuf(es1, "sq", [128, KC, TILE], BF16)
            rt = sbuf(es1, "rt", [128, TILE], F32)
            rstd = sbuf(es1, "rstd", [128, TILE], F32)
            bstg = sbuf(es1, "bstg", [128, 1280], F32)
            E_sb = [sbuf(es1, "E%d" % i, [128, 2, 640], BF16) for i in range(2)]
            wqkg_bf = [sbuf(es1, "wqkg%d" % i, [128, KC, 384], BF16) for i in range(2)]
            wv_bf = sbuf(es1, "wv_bf", [128, KC, 256], BF16)
            QT = sbuf(es1, "QT", [128, NU], BF16)
            KT = sbuf(es1, "KT", [128, NTA], BF16)
            SG = sbuf(es1, "SG", [128, NU], BF16)
            V_sb = sbuf(es1, "V_sb", [128, NKB_A, 256], BF16)
            PT = [[sbuf(es1, "PT%d_%d" % (h, s), [128, 640], BF16) for s in range(RING_A)] for h in range(2)]
            Ttmp = [sbuf(es1, "Ttmp%d" % i, [128, 512], F32) for i in range(2)]
            Rt = [sbuf(es1, "Rt%d" % i, [128, 128], F32) for i in range(2)]
            Ot = [sbuf(es1, "Ot%d" % i, [128, 128], F32) for i in range(2)]

            S.op("sp", lambda e: e.dma_start(out=validA_sb[:].rearrange("p a b -> p (a b)"), in_=validA[:]),
                 writes=["validA"], dma="c3")

            stg_ctr = [0]

            def next_stg():
                s = stg_ctr[0] % 2
                stg_ctr[0] += 1
                return s

            XT_TILES = _tiles(NTA, TILE)

            def xk(a, b_):
                return ["xT_%d" % i for i, (t0, t1) in enumerate(XT_TILES) if t0 < b_ and t1 > a]
            ALLX = xk(0, NTA)

            stat_bank = [4]

            def preamble(after_tile=None):
              for ti_, (t0, t1) in enumerate(XT_TILES):
                n = t1 - t0
                s = next_stg()
                sv = stg[s][:, 0:KC * n].rearrange("p (kc t) -> p kc t", kc=KC)
                S.op("sp", lambda e, sv=sv, t0=t0, t1=t1: e.dma_start(out=sv, in_=xT_v[:, :, t0:t1]),
                     writes=["stg%d" % s], dma="stg%d" % s)
                S.op("act", lambda e, sv=sv, n=n: e.activation(out=sq[:, :, 0:n], in_=sv, func=AF.Square),
                     reads=["stg%d" % s], writes=["sq"])
                b = stat_bank[0]
                stat_bank[0] = 4 + (stat_bank[0] - 3) % 4

                def stat_mm(e, b=b, n=n):
                    ins = None
                    for kc in range(KC):
                        ins = e.matmul(psum[:, b, 0:n], lhsT=ones_bf[:, :], rhs=sq[:, kc, 0:n],
                                       start=(kc == 0), stop=(kc == KC - 1))
                    return ins
                S.op("pe", stat_mm, reads=["sq", "ones"], writes=[bk(b)])
                S.op("act", lambda e, b=b, n=n: e.activation(out=rt[:, 0:n], in_=psum[:, b, 0:n], func=AF.Ln,
                                                             scale=1.0 / D, bias=eps_t[:, 0:1]),
                     reads=[bk(b), "eps"], writes=["rt"])
                S.op("act", lambda e, n=n: e.activation(out=rstd[:, 0:n], in_=rt[:, 0:n], func=AF.Exp, scale=-0.5),
                     reads=["rt"], writes=["rstd"])
                S.op("dve", lambda e, sv=sv, t0=t0, t1=t1, n=n: e.tensor_tensor(
                    out=xT_bf[:, :, t0:t1], in0=sv, in1=rstd[:, 0:n].unsqueeze(1).to_broadcast([128, KC, n]), op=ALU.mult),
                    reads=["stg%d" % s, "rstd"], writes=["xT_%d" % ti_])
                if after_tile is not None:
                    after_tile(t1)

            def prefetch_A(hp):
                s = next_stg()
                slot = hp % 2
                S.op("sp", lambda e, s=s, hp=hp: e.dma_start(out=stg[s][:, 0:KC * 384], in_=wa_qkg[hp]),
                     writes=["stg%d" % s], dma="stg%d" % s)

                def cast(e, s=s, slot=slot):
                    ins = None
                    sv = stg[s][:, 0:KC * 384].rearrange("p (kc c) -> p kc c", kc=KC)
                    for kc in range(KC):
                        ins = e.activation(out=wqkg_bf[slot][:, kc, :], in_=sv[:, kc, :], func=AF.Identity,
                                           scale=gains_sb[:, A_G + kc:A_G + kc + 1])
                    return ins
                S.op("act", cast, reads=["stg%d" % s, "gains"], writes=["wqkg%d" % slot])
                if hp % 2 == 0:
                    s2 = next_stg()
                    S.op("sp", lambda e, s2=s2, hp=hp: e.dma_start(out=stg[s2][:, 0:KC * 256], in_=wa_v[hp // 2]),
                         writes=["stg%d" % s2], dma="stg%d" % s2)

                    def castv(e, s2=s2):
                        ins = None
                        sv = stg[s2][:, 0:KC * 256].rearrange("p (kc c) -> p kc c", kc=KC)
                        for kc in range(KC):
                            ins = e.activation(out=wv_bf[:, kc, :], in_=sv[:, kc, :], func=AF.Identity,
                                               scale=gains_sb[:, A_G + kc:A_G + kc + 1])
                        return ins
                    S.op("act", castv, reads=["stg%d" % s2, "gains"], writes=["wv"])
                S.op("sp", lambda e, hp=hp: e.dma_start(out=bstg[:, :], in_=biasA[hp]), writes=["bstg"], dma="bstg")
                S.op("act", lambda e, slot=slot: e.activation(out=E_sb[slot][:].rearrange("p a b -> p (a b)"), in_=bstg[:, :], func=AF.Exp),
                     reads=["bstg"], writes=["E%d" % slot])

                def corners(e, slot=slot):
                    ins = None
                    for h in range(2):
                        e.memset(E_sb[slot][64:128, h, 0:64], 0.0)
                        ins = e.memset(E_sb[slot][0:64, h, 576:640], 0.0)
                    return ins
                S.op("pool", corners, reads=[], writes=["E%d" % slot])

            proj_bank = [0]
            proj_nbanks = [4]

            def next_pbank():
                b = proj_bank[0] % proj_nbanks[0]
                proj_bank[0] = (b + 1) % proj_nbanks[0]
                return b

            def proj_fm(wslot_ap_fn, rhs_fn, ncols_tiles, evac):
                for (c0, c1) in ncols_tiles:
                    b = next_pbank()
                    n = c1 - c0

                    def mm(e, b=b, c0=c0, c1=c1, n=n):
                        ins = None
                        for kc in range(KC):
                            ins = e.matmul(psum[:, b, 0:n], lhsT=wslot_ap_fn(kc), rhs=rhs_fn(kc, c0, c1),
                                           start=(kc == 0), stop=(kc == KC - 1))
                        return ins
                    yield b, c0, c1, n, mm

            def projection_items(hp):
                slot = hp % 2
                wkey = "wqkg%d" % slot
                items = []
                tcount = [0]

                def fm_tile(kind, c0, c1):
                    n = c1 - c0
                    wcol = {"g": 256, "q": 0, "k": 128}[kind]
                    xoff = 0 if kind == "k" else 512

                    def emit():
                        b = next_pbank()

                        def mm(e):
                            ins = None
                            for kc in range(KC):
                                ins = e.matmul(psum[:, b, 0:n], lhsT=wqkg_bf[slot][:, kc, wcol:wcol + 128],
                                               rhs=xT_bf[:, kc, xoff + c0:xoff + c1], start=(kc == 0), stop=(kc == KC - 1))
                            return ins
                        S.op("pe", mm, reads=[wkey] + xk(xoff + c0, xoff + c1), writes=[bk(b)])
                        if kind == "g":
                            tt = tcount[0] % 2
                            tcount[0] += 1
                            S.op("act", lambda e: e.activation(out=Ttmp[tt][:, 0:n], in_=psum[:, b, 0:n], func=AF.Exp, scale=-1.0),
                                 reads=[bk(b)], writes=["Ttmp%d" % tt])
                            S.op("act", lambda e: e.activation(out=Ttmp[tt][:, 0:n], in_=Ttmp[tt][:, 0:n], func=AF.Ln, bias=one_t[:, 0:1]),
                                 reads=["Ttmp%d" % tt, "one"], writes=["Ttmp%d" % tt])
                            S.op("act", lambda e: e.activation(out=Ttmp[tt][:, 0:n], in_=Ttmp[tt][:, 0:n], func=AF.Exp, scale=-1.0),
                                 reads=["Ttmp%d" % tt], writes=["Ttmp%d" % tt])
                            S.op("dve", lambda e: e.tensor_tensor(out=SG[:, c0:c1], in0=psum[:, b, 0:n], in1=Ttmp[tt][:, 0:n], op=ALU.mult),
                                 reads=[bk(b), "Ttmp%d" % tt], writes=["SG"])
                        elif kind == "q":
                            S.op("dve", lambda e: e.tensor_copy(out=QT[:, c0:c1], in_=psum[:, b, 0:n]), reads=[bk(b)], writes=["QT"])
                        else:
                            S.op("dve", lambda e: e.tensor_copy(out=KT[:, c0:c1], in_=psum[:, b, 0:n]), reads=[bk(b)], writes=["KT"])
                    return (xoff + c1, emit)

                def v_tile(tb0):
                    tbs = [tb for tb in (tb0, tb0 + 1) if tb < NKB_A]
                    nt = len(tbs)

                    def emit():
                        b = next_pbank()

                        def mmv(e):
                            ins = None
                            for i, tb in enumerate(tbs):
                                for kc in range(KC):
                                    ins = e.matmul(psum[:, b, i * 256:(i + 1) * 256], lhsT=xT_bf[:, kc, tb * 128:(tb + 1) * 128],
                                                   rhs=wv_bf[:, kc, :], start=(kc == 0), stop=(kc == KC - 1))
                            return ins
                        S.op("pe", mmv, reads=["wv"] + xk(tbs[0] * 128, (tbs[-1] + 1) * 128), writes=[bk(b)])
                        S.op("act", lambda e: e.activation(
                            out=V_sb[:, tb0:tb0 + nt, :], in_=psum[:, b, 0:nt * 256].rearrange("p (a c) -> p a c", a=nt), func=AF.Copy),
                            reads=[bk(b)], writes=["V"])
                    return ((tbs[-1] + 1) * 128, emit)

                for (c0, c1) in _tiles(NU, 512):
                    items.append(fm_tile("g", c0, c1))
                for (c0, c1) in _tiles(NU, 512):
                    items.append(fm_tile("q", c0, c1))
                for (c0, c1) in _tiles(NTA, 512):
                    items.append(fm_tile("k", c0, c1))
                if hp % 2 == 0:
                    for tb0 in range(0, NKB_A, 2):
                        items.append(v_tile(tb0))
                return items

            def projection_A(hp):
                for need, emit in projection_items(hp):
                    emit()

            nd_ctr = [0]

            def attention_A(hp):
                slot = hp % 2
                hpl = hp % 2
                ekey = "E%d" % slot

                def pv(ju):
                    i = nd_ctr[0] % 2
                    nd_ctr[0] += 1
                    nb, db = 4 + i, 6 + i

                    def mm(e, ju=ju, nb=nb, db=db):
                        ins = None
                        kbs = list(range(ju, ju + 5))
                        for idx, kb2 in enumerate(kbs):
                            col = (ju + 4 - kb2) * 128
                            st, sp_ = (idx == 0), (idx == len(kbs) - 1)
                            r = kb2 % RING_A
                            for h in range(2):
                                e.matmul(psum[h * 64:(h + 1) * 64, nb, 0:128],
                                         lhsT=V_sb[:, kb2, hpl * 128 + h * 64:hpl * 128 + (h + 1) * 64],
                                         rhs=PT[h][r][:, col:col + 128], start=st, stop=sp_)
                            for h in range(2):
                                ins = e.matmul(psum[h * 64:(h + 1) * 64, db, 0:128], lhsT=validA_sb[:, kb2, :],
                                               rhs=PT[h][r][:, col:col + 128], start=st, stop=sp_)
                        return ins
                    rd = ["V", "validA"] + ["PT%d_%d" % (h, kb2 % RING_A) for h in range(2) for kb2 in range(ju, ju + 5)]
                    S.op("pe", mm, reads=rd, writes=[bk(nb), bk(db)])
                    S.op("act", lambda e, i=i, db=db: e.activation(out=Rt[i][:, :], in_=psum[:, db, 0:128], func=AF.Ln, bias=eps60[:, 0:1]),
                         reads=[bk(db), "eps60"], writes=["Rt%d" % i])
                    S.op("act", lambda e, i=i: e.activation(out=Rt[i][:, :], in_=Rt[i][:, :], func=AF.Exp, scale=-1.0),
                         reads=["Rt%d" % i], writes=["Rt%d" % i])
                    S.op("dve", lambda e, i=i, nb=nb: e.tensor_tensor(out=Ot[i][:, :], in0=psum[:, nb, 0:128], in1=Rt[i][:, :], op=ALU.mult),
                         reads=[bk(nb), "Rt%d" % i], writes=["Ot%d" % i])
                    S.op("pool", lambda e, i=i, ju=ju: e.tensor_tensor(out=attnT[:, hp, ju * 128:(ju + 1) * 128], in0=Ot[i][:, :],
                                                                       in1=SG[:, ju * 128:(ju + 1) * 128], op=ALU.mult),
                         reads=["Ot%d" % i, "SG"], writes=["attnT"])

                for kb in range(NKB_A):
                    jmin, jmax = max(kb, 4), min(kb + 4, 20)
                    c0, c1 = (jmin - kb) * 128, (jmax - kb + 1) * 128
                    u0 = (jmin - 4) * 128
                    r = kb % RING_A
                    a = c0
                    while a < c1:
                        bend = min(c1, (a // 512 + 1) * 512)
                        for h in range(2):
                            bnk = 2 * h + (a // 512)
                            S.op("pe", lambda e, h=h, kb=kb, a=a, bend=bend, bnk=bnk, c0=c0, u0=u0: e.matmul(
                                psum[:, bnk, a % 512:a % 512 + (bend - a)],
                                lhsT=KT[h * 64:(h + 1) * 64, kb * 128:(kb + 1) * 128],
                                rhs=QT[h * 64:(h + 1) * 64, u0 + (a - c0):u0 + (bend - c0)], start=True, stop=True),
                                reads=["QT", "KT"], writes=[bk(bnk)])
                        a = bend
                    for h in range(2):
                        psS = psum[:, 2 * h:2 * h + 2, :].rearrange("p a b -> p (a b)")
                        S.op("act", lambda e, h=h, r=r, c0=c0, c1=c1, psS=psS: e.activation(
                            out=PT[h][r][:, c0:c1], in_=psS[:, c0:c1], func=AF.Exp, scale=0.125),
                            reads=[bk(2 * h), bk(2 * h + 1)], writes=["PT%d_%d" % (h, r)])
                        S.op("dve", lambda e, h=h, r=r, c0=c0, c1=c1, slot=slot: e.tensor_tensor(
                            out=PT[h][r][:, c0:c1], in0=PT[h][r][:, c0:c1], in1=E_sb[slot][:, h, c0:c1], op=ALU.mult),
                            reads=["PT%d_%d" % (h, r), ekey], writes=["PT%d_%d" % (h, r)])
                    ju = kb - 5
                    if ju >= 0:
                        pv(ju)
                for ju in range(NKB_A - 5, NQB_A):
                    pv(ju)

            prefetch_A(0)
            items0 = projection_items(0)

            prev_t1 = [0]

            def after_tile(t1):
                lim, prev_t1[0] = prev_t1[0], t1
                rest = []
                for need, emit in items0:
                    if need <= lim:
                        emit()
                    else:
                        rest.append((need, emit))
                items0[:] = rest
            preamble(after_tile)
            proj_nbanks[0] = 8
            after_tile(NTA)
            assert not items0
            prefetch_A(1)
            dbg_dump("xTn", lambda c0, c1: xT_bf[:].rearrange("p a b -> p (a b)")[:, c0:c1], KC * NTA, ALLX)
            for hp in range(8):
                if hp >= 1 and hp + 1 < 8:
                    prefetch_A(hp + 1)
                if hp >= 1:
                    projection_A(hp)
                if hp == 0:
                    dbg_dump("QT0", lambda c0, c1: QT[:, c0:c1], NU, ["QT"])
                    dbg_dump("KT0", lambda c0, c1: KT[:, c0:c1], NTA, ["KT"])
                    dbg_dump("SG0", lambda c0, c1: SG[:, c0:c1], NU, ["SG"])
                    dbg_dump("V0", lambda c0, c1: V_sb[:].rearrange("p a b -> p (a b)")[:, c0:c1], NKB_A * 256, ["V"])
                attention_A(hp)
            dbg_dump("attnA", lambda c0, c1: attnT[:].rearrange("p a b -> p (a b)")[:, c0:c1], KC * NU, ["attnT"])
            S.flush()

        with ExitStack() as es2:
            h1T = sbuf(es2, "h1T", [128, KC, NU], F32)
            h1n = sbuf(es2, "h1n", [128, KC, NU], BF16)
            kvw_bf = sbuf(es2, "kvw_bf", [128, KC, 384], BF16)

            def load_wout(es, wsrc, stgs, wout_bf):
                for i in range(4):
                    s = i % 2
                    S.op("sp", lambda e, s=s, i=i: e.dma_start(out=stgs[s][:, 0:2048], in_=wsrc[:, i * 2048:(i + 1) * 2048]),
                         writes=["wstg%d" % s], dma="wstg%d" % s)
                    S.op("act", lambda e, s=s, i=i: e.activation(
                        out=wout_bf[:, :, i * 256:(i + 1) * 256], in_=stgs[s][:, 0:2048].rearrange("p (kc c) -> p kc c", kc=KC), func=AF.Copy),
                        reads=["wstg%d" % s], writes=["wout%d" % i])

            def stats_a(src_ap_fn, src_key, n, sq):
                S.op("act", lambda e, n=n: e.activation(out=sq[:, :, 0:n], in_=src_ap_fn(), func=AF.Square),
                     reads=[src_key], writes=["sq"])

            def stats_b(n, sq, rt, rstd, bank_ctr):
                b = 4 + bank_ctr[0] % 4
                bank_ctr[0] += 1

                def stat_mm(e, b=b, n=n):
                    ins = None
                    for kc in range(KC):
                        ins = e.matmul(psum[:, b, 0:n], lhsT=ones_bf[:, :], rhs=sq[:, kc, 0:n],
                                       start=(kc == 0), stop=(kc == KC - 1))
                    return ins
                S.op("pe", stat_mm, reads=["sq", "ones"], writes=[bk(b)])
                S.op("act", lambda e, b=b, n=n: e.activation(out=rt[:, 0:n], in_=psum[:, b, 0:n], func=AF.Ln,
                                                             scale=1.0 / D, bias=eps_t[:, 0:1]),
                     reads=[bk(b), "eps"], writes=["rt"])
                S.op("act", lambda e, n=n: e.activation(out=rstd[:, 0:n], in_=rt[:, 0:n], func=AF.Exp, scale=-0.5),
                     reads=["rt"], writes=["rstd"])

            with ExitStack() as es2a:
                stg2 = [sbuf(es2a, "stg2_%d" % i, [128, KC * TILE], F32) for i in range(2)]
                wout_bf = sbuf(es2a, "woutA_bf", [128, KC, 1024], BF16)
                sq = sbuf(es2a, "sq2", [128, KC, TILE], BF16)
                rt = sbuf(es2a, "rt2", [128, TILE], F32)
                rstd = sbuf(es2a, "rstd2", [128, TILE], F32)
                load_wout(es2a, wa_out, stg2, wout_bf)
                kvstg = sbuf(es2a, "kvstg", [128, KC * 128], F32)

                def prefetch_kvw():
                    for part in range(3):
                        src = kvw[:].rearrange("p (kc c) -> p kc c", kc=KC)[:, :, part * 128:(part + 1) * 128]
                        S.op("sp", lambda e, src=src: e.dma_start(out=kvstg[:, :].rearrange("p (kc c) -> p kc c", kc=KC), in_=src),
                             writes=["kvstg"], dma="kvstg")

                        def castkv(e, part=part):
                            ins = None
                            sv = kvstg[:, :].rearrange("p (kc c) -> p kc c", kc=KC)
                            for kc in range(KC):
                                ins = e.activation(out=kvw_bf[:, kc, part * 128:(part + 1) * 128], in_=sv[:, kc, :], func=AF.Identity,
                                                   scale=gains_sb[:, KV_G + kc:KV_G + kc + 1])
                            return ins
                        S.op("act", castkv, reads=["kvstg", "gains"], writes=["kvw"])
                bctr = [0]
                wb = [0]
                pendingA = []
                for ti, (u0, u1) in enumerate(_tiles(NU, TILE)):
                    n = u1 - u0
                    s = ti % 2
                    sv = stg2[s][:, 0:KC * n].rearrange("p (kc t) -> p kc t", kc=KC)
                    S.op("sp", lambda e, sv=sv, u0=u0, u1=u1: e.dma_start(out=sv, in_=xT_v[:, :, 512 + u0:512 + u1]),
                         writes=["wstg%d" % s], dma="wstg%d" % s)
                    for oc in range(KC):
                        b = wb[0] % 4
                        wb[0] += 1

                        def mm(e, b=b, oc=oc, u0=u0, u1=u1, n=n):
                            ins = None
                            for kc in range(KC):
                                ins = e.matmul(psum[:, b, 0:n], lhsT=wout_bf[:, kc, oc * 128:(oc + 1) * 128],
                                               rhs=attnT[:, kc, u0:u1], start=(kc == 0), stop=(kc == KC - 1))
                            return ins
                        S.op("pe", mm, reads=["wout%d" % (oc // 2), "attnT"], writes=[bk(b)])
                        S.op("dve", lambda e, b=b, oc=oc, u0=u0, u1=u1, n=n, sv=sv: e.tensor_tensor(
                            out=h1T[:, oc, u0:u1], in0=psum[:, b, 0:n], in1=sv[:, oc, :], op=ALU.add),
                            reads=[bk(b), "wstg%d" % s], writes=["h1T_%d" % ti])
                    def finish(ti=ti, u0=u0, u1=u1, n=n):
                        stats_b(n, sq, rt, rstd, bctr)
                        S.op("dve", lambda e: e.tensor_tensor(
                            out=h1n[:, :, u0:u1], in0=h1T[:, :, u0:u1], in1=rstd[:, 0:n].unsqueeze(1).to_broadcast([128, KC, n]), op=ALU.mult),
                            reads=["h1T_%d" % ti, "rstd"], writes=["h1n"])
                    if ti == 3:
                        prefetch_kvw()
                    if pendingA:
                        pendingA.pop()()
                    stats_a(lambda u0=u0, u1=u1: h1T[:, :, u0:u1], "h1T_%d" % ti, n, sq)
                    pendingA.append(finish)
                pendingA.pop()()
                dbg_dump("h1T", lambda c0, c1: h1T[:].rearrange("p a b -> p (a b)")[:, c0:c1], KC * NU, ["h1n"])
                dbg_dump("h1n", lambda c0, c1: h1n[:].rearrange("p a b -> p (a b)")[:, c0:c1], KC * NU, ["h1n"])
                S.flush()

            with ExitStack() as es2b:
                stgB = [sbuf(es2b, "stgB%d" % i, [128, KC * 256], F32) for i in range(2)]
                wqg_bf = [sbuf(es2b, "wqg%d" % i, [128, KC, 256], BF16) for i in range(2)]
                KshT = [sbuf(es2b, "KshT%d" % g, [128, NU], BF16) for g in range(2)]
                Vsh = sbuf(es2b, "Vsh", [128, NKB_B, 128], BF16)
                QTb = sbuf(es2b, "QTb", [128, NW], BF16)
                SGb = sbuf(es2b, "SGb", [128, NW], BF16)
                PTb = [[sbuf(es2b, "PTb%d_%d" % (h, s), [128, 256], BF16) for s in range(RING_B)] for h in range(2)]
                bstgB = sbuf(es2b, "bstgB", [128, 512], F32)
                EB = [sbuf(es2b, "EB%d" % i, [128, 2, 256], BF16) for i in range(2)]
                TtmpB = [sbuf(es2b, "TtmpB%d" % i, [128, 512], F32) for i in range(2)]
                RtB = [sbuf(es2b, "RtB%d" % i, [128, 128], F32) for i in range(2)]
                OtB = [sbuf(es2b, "OtB%d" % i, [128, 128], F32) for i in range(2)]
                sctr = [0]

                def next_stgB():
                    s = sctr[0] % 2
                    sctr[0] += 1
                    return s

                def prefetch_B(hp):
                    s = next_stgB()
                    slot = hp % 2
                    S.op("sp", lambda e, s=s, hp=hp: e.dma_start(out=stgB[s][:, :], in_=wb_qg[hp]),
                         writes=["stgB%d" % s], dma="stgB%d" % s)

                    def cast(e, s=s, slot=slot):
                        ins = None
                        sv = stgB[s][:, :].rearrange("p (kc c) -> p kc c", kc=KC)
                        for kc in range(KC):
                            ins = e.activation(out=wqg_bf[slot][:, kc, :], in_=sv[:, kc, :], func=AF.Identity,
                                               scale=gains_sb[:, B_G + kc:B_G + kc + 1])
                        return ins
                    S.op("act", cast, reads=["stgB%d" % s, "gains"], writes=["wqg%d" % slot])
                    S.op("sp", lambda e, hp=hp: e.dma_start(out=bstgB[:, :], in_=biasB[hp]), writes=["bstgB"], dma="bstgB")
                    S.op("act", lambda e, slot=slot: e.activation(out=EB[slot][:].rearrange("p a b -> p (a b)"), in_=bstgB[:, :], func=AF.Exp),
                         reads=["bstgB"], writes=["EB%d" % slot])

                    def corners(e, slot=slot):
                        ins = None
                        for h in range(2):
                            e.memset(EB[slot][64:128, h, 0:64], 0.0)
                            ins = e.memset(EB[slot][0:64, h, 192:256], 0.0)
                        return ins
                    S.op("pool", corners, reads=[], writes=["EB%d" % slot])

                pbank = [0]

                def next_pb():
                    b = pbank[0]
                    pbank[0] = (b + 1) % 8
                    return b

                for g in range(2):
                    for (c0, c1) in _tiles(NU, 512):
                        b = next_pb()
                        n = c1 - c0

                        def mm(e, b=b, g=g, c0=c0, c1=c1, n=n):
                            ins = None
                            for kc in range(KC):
                                ins = e.matmul(psum[:, b, 0:n], lhsT=kvw_bf[:, kc, g * 128:(g + 1) * 128], rhs=h1n[:, kc, c0:c1],
                                               start=(kc == 0), stop=(kc == KC - 1))
                            return ins
                        S.op("pe", mm, reads=["kvw", "h1n"], writes=[bk(b)])
                        S.op("dve", lambda e, b=b, g=g, c0=c0, c1=c1, n=n: e.tensor_copy(out=KshT[g][:, c0:c1], in_=psum[:, b, 0:n]),
                             reads=[bk(b)], writes=["KshT%d" % g])
                for tb0 in range(0, NKB_B, 4):
                    tbs = [tb for tb in range(tb0, tb0 + 4) if tb < NKB_B]
                    b = next_pb()

                    def mmv(e, b=b, tbs=tbs):
                        ins = None
                        for i, tb in enumerate(tbs):
                            for kc in range(KC):
                                ins = e.matmul(psum[:, b, i * 128:(i + 1) * 128], lhsT=h1n[:, kc, tb * 128:(tb + 1) * 128],
                                               rhs=kvw_bf[:, kc, 256:384], start=(kc == 0), stop=(kc == KC - 1))
                        return ins
                    S.op("pe", mmv, reads=["kvw", "h1n"], writes=[bk(b)])
                    nt = len(tbs)
                    S.op("act", lambda e, b=b, tb0=tb0, nt=nt: e.activation(
                        out=Vsh[:, tb0:tb0 + nt, :], in_=psum[:, b, 0:nt * 128].rearrange("p (a c) -> p a c", a=nt), func=AF.Copy),
                        reads=[bk(b)], writes=["Vsh"])

                def projection_B(hp):
                    slot = hp % 2
                    wkey = "wqg%d" % slot
                    ti = 0
                    for (c0, c1) in _tiles(NW, 512):
                        b = next_pb()
                        n = c1 - c0

                        def mm(e, b=b, c0=c0, c1=c1, n=n):
                            ins = None
                            for kc in range(KC):
                                ins = e.matmul(psum[:, b, 0:n], lhsT=wqg_bf[slot][:, kc, 128:256], rhs=h1n[:, kc, 128 + c0:128 + c1],
                                               start=(kc == 0), stop=(kc == KC - 1))
                            return ins
                        S.op("pe", mm, reads=[wkey, "h1n"], writes=[bk(b)])
                        tt = ti % 2
                        ti += 1
                        S.op("act", lambda e, b=b, n=n, tt=tt: e.activation(out=TtmpB[tt][:, 0:n], in_=psum[:, b, 0:n], func=AF.Exp, scale=-1.0),
                             reads=[bk(b)], writes=["TtmpB%d" % tt])
                        S.op("act", lambda e, n=n, tt=tt: e.activation(out=TtmpB[tt][:, 0:n], in_=TtmpB[tt][:, 0:n], func=AF.Ln, bias=one_t[:, 0:1]),
                             reads=["TtmpB%d" % tt, "one"], writes=["TtmpB%d" % tt])
                        S.op("act", lambda e, n=n, tt=tt: e.activation(out=TtmpB[tt][:, 0:n], in_=TtmpB[tt][:, 0:n], func=AF.Exp, scale=-1.0),
                             reads=["TtmpB%d" % tt], writes=["TtmpB%d" % tt])
                        S.op("dve", lambda e, b=b, c0=c0, c1=c1, n=n, tt=tt: e.tensor_tensor(
                            out=SGb[:, c0:c1], in0=psum[:, b, 0:n], in1=TtmpB[tt][:, 0:n], op=ALU.mult),
                            reads=[bk(b), "TtmpB%d" % tt], writes=["SGb"])
                    for (c0, c1) in _tiles(NW, 512):
                        b = next_pb()
                        n = c1 - c0

                        def mm(e, b=b, c0=c0, c1=c1, n=n):
                            ins = None
                            for kc in range(KC):
                                ins = e.matmul(psum[:, b, 0:n], lhsT=wqg_bf[slot][:, kc, 0:128], rhs=h1n[:, kc, 128 + c0:128 + c1],
                                               start=(kc == 0), stop=(kc == KC - 1))
                            return ins
                        S.op("pe", mm, reads=[wkey, "h1n"], writes=[bk(b)])
                        S.op("dve", lambda e, b=b, c0=c0, c1=c1, n=n: e.tensor_copy(out=QTb[:, c0:c1], in_=psum[:, b, 0:n]),
                             reads=[bk(b)], writes=["QTb"])

                ndb = [0]

                def attention_B(hp):
                    slot = hp % 2
                    g = hp // 4
                    ekey = "EB%d" % slot

                    def pv(jw):
                        i = ndb[0] % 2
                        ndb[0] += 1
                        nb, db = 4 + i, 6 + i

                        def mm(e, jw=jw, nb=nb, db=db):
                            ins = None
                            kbs = [jw, jw + 1]
                            for idx, kb2 in enumerate(kbs):
                                col = (jw + 1 - kb2) * 128
                                st, sp_ = (idx == 0), (idx == len(kbs) - 1)
                                r = kb2 % RING_B
                                for h in range(2):
                                    e.matmul(psum[h * 64:(h + 1) * 64, nb, 0:128], lhsT=Vsh[:, kb2, g * 64:(g + 1) * 64],
                                             rhs=PTb[h][r][:, col:col + 128], start=st, stop=sp_)
                                for h in range(2):
                                    ins = e.matmul(psum[h * 64:(h + 1) * 64, db, 0:128], lhsT=validB_sb[:, kb2, :],
                                                   rhs=PTb[h][r][:, col:col + 128], start=st, stop=sp_)
                            return ins
                        rd = ["Vsh", "validB"] + ["PTb%d_%d" % (h, kb2 % RING_B) for h in range(2) for kb2 in (jw, jw + 1)]
                        S.op("pe", mm, reads=rd, writes=[bk(nb), bk(db)])
                        S.op("act", lambda e, i=i, db=db: e.activation(out=RtB[i][:, :], in_=psum[:, db, 0:128], func=AF.Ln, bias=esink[:, hp:hp + 1]),
                             reads=[bk(db), "esink"], writes=["RtB%d" % i])
                        S.op("act", lambda e, i=i: e.activation(out=RtB[i][:, :], in_=RtB[i][:, :], func=AF.Exp, scale=-1.0),
                             reads=["RtB%d" % i], writes=["RtB%d" % i])
                        S.op("dve", lambda e, i=i, nb=nb: e.tensor_tensor(out=OtB[i][:, :], in0=psum[:, nb, 0:128], in1=RtB[i][:, :], op=ALU.mult),
                             reads=[bk(nb), "RtB%d" % i], writes=["OtB%d" % i])
                        S.op("pool", lambda e, i=i, jw=jw: e.tensor_tensor(out=attnT[:, hp, jw * 128:(jw + 1) * 128], in0=OtB[i][:, :],
                                                                           in1=SGb[:, jw * 128:(jw + 1) * 128], op=ALU.mult),
                             reads=["OtB%d" % i, "SGb"], writes=["attnT"])

                    for kb in range(NKB_B):
                        jmin, jmax = max(kb - 1, 0), min(kb, NQB_B - 1)
                        c0, c1 = (jmin + 1 - kb) * 128, (jmax + 1 - kb + 1) * 128
                        w0 = jmin * 128
                        r = kb % RING_B
                        for h in range(2):
                            S.op("pe", lambda e, h=h, kb=kb, c0=c0, c1=c1, w0=w0: e.matmul(
                                psum[:, 2 * h, c0:c1], lhsT=KshT[g][h * 64:(h + 1) * 64, kb * 128:(kb + 1) * 128],
                                rhs=QTb[h * 64:(h + 1) * 64, w0:w0 + (c1 - c0)], start=True, stop=True),
                                reads=["QTb", "KshT%d" % g], writes=[bk(2 * h)])
                            S.op("act", lambda e, h=h, r=r, c0=c0, c1=c1: e.activation(
                                out=PTb[h][r][:, c0:c1], in_=psum[:, 2 * h, c0:c1], func=AF.Exp, scale=0.125),
                                reads=[bk(2 * h)], writes=["PTb%d_%d" % (h, r)])
                            S.op("dve", lambda e, h=h, r=r, c0=c0, c1=c1: e.tensor_tensor(
                                out=PTb[h][r][:, c0:c1], in0=PTb[h][r][:, c0:c1], in1=EB[slot][:, h, c0:c1], op=ALU.mult),
                                reads=["PTb%d_%d" % (h, r), ekey], writes=["PTb%d_%d" % (h, r)])
                        jw = kb - 2
                        if jw >= 0:
                            pv(jw)
                    for jw in range(NKB_B - 2, NQB_B):
                        pv(jw)

                prefetch_B(0)
                for hp in range(8):
                    if hp + 1 < 8:
                        prefetch_B(hp + 1)
                    projection_B(hp)
                    attention_B(hp)
                dbg_dump("attnB", lambda c0, c1: attnT[:, c0 // NW, c0 % NW:c0 % NW + (c1 - c0)], KC * NW, ["attnT"])
                S.flush()

            with ExitStack() as es2c:
                TB = 256
                wstg = [sbuf(es2c, "wstgC%d" % i, [128, 2048], F32) for i in range(2)]
                wout_bf = sbuf(es2c, "woutB_bf", [128, KC, 1024], BF16)
                h2 = [sbuf(es2c, "h2_%d" % i, [128, KC, TB], F32) for i in range(3)]
                sq = sbuf(es2c, "sq3", [128, KC, TB], BF16)
                rt = sbuf(es2c, "rt3", [128, TB], F32)
                rstd = sbuf(es2c, "rstd3", [128, TB], F32)
                load_wout(es2c, wb_out, wstg, wout_bf)
                bctr = [0]
                wb = [0]
                pendingB = []
                for ti, (w0, w1) in enumerate(_tiles(NW, TB)):
                    n = w1 - w0
                    s = ti % 3
                    for oc in range(KC):
                        b = wb[0] % 4
                        wb[0] += 1

                        def mm(e, b=b, oc=oc, w0=w0, w1=w1, n=n):
                            ins = None
                            for kc in range(KC):
                                ins = e.matmul(psum[:, b, 0:n], lhsT=wout_bf[:, kc, oc * 128:(oc + 1) * 128],
                                               rhs=attnT[:, kc, w0:w1], start=(kc == 0), stop=(kc == KC - 1))
                            return ins
                        S.op("pe", mm, reads=["wout%d" % (oc // 2), "attnT"], writes=[bk(b)])
                        S.op("dve", lambda e, b=b, oc=oc, w0=w0, w1=w1, n=n, s=s: e.tensor_tensor(
                            out=h2[s][:, oc, 0:n], in0=psum[:, b, 0:n], in1=h1T[:, oc, 128 + w0:128 + w1], op=ALU.add),
                            reads=[bk(b)], writes=["h2_%d" % s])
                    def finishB(s=s, w0=w0, w1=w1, n=n):
                        stats_b(n, sq, rt, rstd, bctr)

                        def fin(e):
                            ins = None
                            for oc in range(KC):
                                ins = e.scalar_tensor_tensor(out=h2[s][:, oc, 0:n], in0=h2[s][:, oc, 0:n],
                                                             scalar=gains_sb[:, F_G + oc:F_G + oc + 1], in1=rstd[:, 0:n],
                                                             op0=ALU.mult, op1=ALU.mult)
                            return ins
                        S.op("dve", fin, reads=["h2_%d" % s, "rstd", "gains"], writes=["h2_%d" % s])
                        tok = S.op("sp", lambda e: e.dma_start(out=outT_v[:, :, w0:w1], in_=h2[s][:, :, 0:n]),
                                   reads=["h2_%d" % s], dma="out%d" % s)
                        out_tokens.append(tok)
                    if pendingB:
                        pendingB.pop()()
                    stats_a(lambda s=s, n=n: h2[s][:, :, 0:n], "h2_%d" % s, n, sq)
                    pendingB.append(finishB)
                pendingB.pop()()
                final = {}
                for key, val in out_tokens:
                    final[key] = max(final.get(key, 0), val)
                S.wait_tokens("sp", list(final.items()))
                S.flush()
    return nc


def _t5_bucket_np(rel):
    nb = 16
    max_exact = 8
    ret = np.where(rel > 0, nb, 0)
    n = np.abs(rel)
    nf = np.maximum(n, 1).astype(np.float32)
    large = max_exact + (np.log(nf / np.float32(max_exact)) / np.float32(math.log(128 / max_exact))
                         * np.float32(nb - max_exact)).astype(np.int32)
    large = np.minimum(large, nb - 1)
    return ret + np.where(n < max_exact, n, large)


def _prep_shared(a_norm, a_w_in, a_rel_bias, a_w_out, kv_norm, kv_w, t5_bias, b_norm, b_w_in, b_sinks, b_w_out, final_norm):
    f = np.float32
    w_in = np.asarray(a_w_in[0], f)
    w4 = w_in.reshape(KC, 128, 4, 8, 128)
    wa_qkg = np.ascontiguousarray(np.transpose(w4[:, :, [0, 1, 3]], (3, 1, 0, 2, 4))).reshape(8, 128, KC * 384)
    wv = w_in[:, 2048:3072].reshape(KC, 128, 4, 256)
    wa_v = np.ascontiguousarray(np.transpose(wv, (2, 1, 0, 3))).reshape(4, 128, KC * 256)
    wa_out = np.ascontiguousarray(np.transpose(np.asarray(a_w_out[0], f).reshape(KC, 128, 4, 256), (1, 2, 0, 3))).reshape(128, KC * 1024)
    wb_out = np.ascontiguousarray(np.transpose(np.asarray(b_w_out[0], f).reshape(KC, 128, 4, 256), (1, 2, 0, 3))).reshape(128, KC * 1024)

    def gcol(v):
        return np.asarray(v, f).reshape(KC, 128).T
    gains = np.ascontiguousarray(np.concatenate([gcol(a_norm[0]), gcol(kv_norm), gcol(b_norm[0]), gcol(final_norm)], axis=1))
    k = np.arange(128)[:, None]
    q = np.arange(640)[None, :]
    idxA = np.clip(q - k, -256, 256) + 256
    rb = np.asarray(a_rel_bias[0], f)
    bA = rb[idxA]
    biasA = np.ascontiguousarray(np.transpose(bA.reshape(128, 640, 8, 2), (2, 0, 3, 1))).reshape(8, 128, 1280)
    kvw_ = np.asarray(kv_w, f).reshape(KC, 128, 256)
    kcat = np.concatenate([kvw_[:, :, 0:64], kvw_[:, :, 0:64], kvw_[:, :, 64:128], kvw_[:, :, 64:128], kvw_[:, :, 128:256]], axis=2)
    kvw = np.ascontiguousarray(np.transpose(kcat, (1, 0, 2))).reshape(128, KC * 384)
    wb = np.asarray(b_w_in[0], f).reshape(KC, 128, 2, 8, 128)
    wb_qg = np.ascontiguousarray(np.transpose(wb, (3, 1, 0, 2, 4))).reshape(8, 128, KC * 256)
    qb = np.arange(256)[None, :]
    bucket = _t5_bucket_np((k - qb).astype(np.int32))
    tb = np.asarray(t5_bias, f)[bucket]
    biasB = np.ascontiguousarray(np.transpose(tb.reshape(128, 256, 8, 2), (2, 0, 3, 1))).reshape(8, 128, 512)
    sk = np.asarray(b_sinks[0], f).reshape(8, 2)
    sinkB = np.ascontiguousarray(np.repeat(sk.T, 64, axis=0))
    return dict(wa_qkg=wa_qkg, wa_v=wa_v, wa_out=wa_out, gains=gains, biasA=biasA, kvw=kvw, wb_qg=wb_qg,
                wb_out=wb_out, biasB=biasB, sinkB=sinkB)


def _prep_core(x, c):
    b, half = c // 2, c % 2
    T0 = half * NOWN
    lo = T0 - HALO
    xe = np.zeros((NTA, D), np.float32)
    src0 = max(lo, 0)
    xe[src0 - lo:, :] = x[b, src0:T0 + NOWN, :]
    xTc = np.ascontiguousarray(xe.T)
    tpos = lo + np.arange(NTA)
    vA = (tpos >= 0).astype(np.float32).reshape(NKB_A, 128).T
    validA = np.ascontiguousarray(np.repeat(vA[:, :, None], 64, axis=2)).reshape(128, NKB_A * 64).astype(ml_dtypes.bfloat16)
    upos = T0 - 128 + np.arange(NU)
    vB = (upos >= 0).astype(np.float32).reshape(NKB_B, 128).T
    validB = np.ascontiguousarray(np.repeat(vB[:, :, None], 64, axis=2)).reshape(128, NKB_B * 64).astype(ml_dtypes.bfloat16)
    return dict(xT=xTc, validA=validA, validB=validB)


_NC_CACHE = {}


def kernel(x, a_norm, a_w_in, a_rel_bias, a_w_out, kv_norm, kv_w, t5_bias, b_norm, b_w_in, b_sinks, b_w_out, final_norm,
           _debug=()):
    x = np.asarray(x, np.float32)
    shared = _prep_shared(np.asarray(a_norm), np.asarray(a_w_in), np.asarray(a_rel_bias), np.asarray(a_w_out),
                          np.asarray(kv_norm), np.asarray(kv_w), np.asarray(t5_bias), np.asarray(b_norm),
                          np.asarray(b_w_in), np.asarray(b_sinks), np.asarray(b_w_out), np.asarray(final_norm))
    in_maps = []
    for c in range(N_CORES):
        m = dict(shared)
        m.update(_prep_core(x, c))
        in_maps.append(m)
    key = tuple(_debug)
    if key not in _NC_CACHE:
        _NC_CACHE[key] = build_nc(debug=key)
    nc = _NC_CACHE[key]
    res = run_bass_kernel_spmd(nc, in_maps, core_ids=list(range(N_CORES)))
    out = np.empty((4, SEQ, D), np.float32)
    for c in range(N_CORES):
        b, half = c // 2, c % 2
        out[b, half * NOWN:(half + 1) * NOWN, :] = np.asarray(res.results[c]["outT"]).T
    if _debug:
        return out, res.results
    return out
```

```python
import math
from contextlib import ExitStack

import numpy as np
import ml_dtypes

import concourse.bass as bass
import concourse.mybir as mybir
from concourse.bass_utils import run_bass_kernel_spmd

F32 = mybir.dt.float32
BF16 = mybir.dt.bfloat16
AF = mybir.ActivationFunctionType
ALU = mybir.AluOpType

N_CORES = 8
D = 1024
KC = 8
SEQ = 4096
NOWN = 2048
HALO = 640
NTA = NOWN + HALO
NU = NOWN + 128
NW = NOWN
NKB_A = NTA // 128
NQB_A = NU // 128
NKB_B = NU // 128
NQB_B = NW // 128
RING_A = 6
RING_B = 4
TILE = 384
EPS = 1e-6

ENGS = ("pe", "act", "dve", "pool", "sp")


class Sched:
    def __init__(self, nc, es):
        self.nc = nc
        self.es = es
        self.sem = {e: es.enter_context(nc.semaphore("s_" + e)) for e in ENGS}
        self.cnt = {e: 0 for e in ENGS}
        self.dsem = {}
        self.dcnt = {}
        self.lastw = {}
        self.readers = {}
        self.pending = {e: [] for e in ENGS}
        self.seen = {e: {} for e in ENGS}
        self.know = {}
        self.order = {}
        self.nwaits = 0

    def op(self, eng, fn, reads=(), writes=(), dma=None, ndma=1):
        deps = {}

        def add(tok):
            if tok is None:
                return
            key, val = tok
            if deps.get(key, 0) < val:
                deps[key] = val

        for r in reads:
            add(self.lastw.get(r))
            if r.startswith("bk") and eng in ("act", "dve"):
                for k, v in self.readers.get(r, {}).items():
                    if k[0] in ("act", "dve") and k[0] != eng:
                        add((k, v))
        for w in writes:
            add(self.lastw.get(w))
            for k, v in self.readers.get(w, {}).items():
                add((k, v))
        if dma is not None:
            if dma not in self.dsem:
                self.dsem[dma] = self.es.enter_context(self.nc.semaphore("d_" + str(dma)))
                self.dcnt[dma] = 0
            self.dcnt[dma] += 16 * ndma
            tok = (("dma", dma), self.dcnt[dma])
        else:
            self.cnt[eng] += 1
            tok = ((eng,), self.cnt[eng])
        waits = []
        for key, val in sorted(deps.items(), key=lambda kv: -self.order.get(kv, 0)):
            if key == ("pe",) and eng == "pe":
                continue
            if self.seen[eng].get(key, 0) >= val:
                continue
            waits.append((key, val))
            for k2, v2 in self.know.get((key, val), {}).items():
                if self.seen[eng].get(k2, 0) < v2:
                    self.seen[eng][k2] = v2
            self.seen[eng][key] = val
        self.nwaits += len(waits)
        self.order[tok] = len(self.order) + 1
        if dma is None:
            kn = dict(self.seen[eng])
            kn[tok[0]] = tok[1]
            self.know[tok] = kn
        else:
            self.know[tok] = dict(self.seen[eng])
        self.pending[eng].append((fn, waits, tok))
        for r in reads:
            d = self.readers.setdefault(r, {})
            if d.get(tok[0], 0) < tok[1]:
                d[tok[0]] = tok[1]
        for w in writes:
            self.lastw[w] = tok
            self.readers[w] = {}
        return tok

    def _semof(self, key):
        if key[0] == "dma":
            return self.dsem[key[1]]
        return self.sem[key[0]]

    def wait_tokens(self, eng, toks):
        self.pending[eng].append((None, list(toks), None))

    def flush(self):
        nc = self.nc
        pend = self.pending
        self.pending = {e: [] for e in ENGS}
        with nc.Block() as block:
            def mk(engname):
                lst = pend[engname]

                def body(e):
                    class _First:
                        def __init__(self, eng):
                            self._eng = eng
                            self.first = None

                        def __getattr__(self, name):
                            real = getattr(self._eng, name)

                            def call(*a, **k):
                                r = real(*a, **k)
                                if self.first is None:
                                    self.first = r
                                return r
                            return call

                    for fn, waits, tok in lst:
                        fuse = fn is not None and len(waits) >= 1
                        for key, val in (waits[:-1] if fuse else waits):
                            e.wait_ge(self._semof(key), val)
                        if fn is None:
                            continue
                        if fuse:
                            px = _First(e)
                            ins = fn(px)
                            px.first._wait_ge(self._semof(waits[-1][0]), waits[-1][1])
                        else:
                            ins = fn(e)
                        key, val = tok
                        if key[0] == "dma":
                            if not isinstance(ins, (list, tuple)):
                                ins = [ins]
                            for i in ins:
                                i.then_inc(self.dsem[key[1]], 16)
                        else:
                            ins.then_inc(self.sem[engname], 1)
                return body

            if pend["pe"]:
                block.tensor(mk("pe"))
            if pend["act"]:
                block.scalar(mk("act"))
            if pend["dve"]:
                block.vector(mk("dve"))
            if pend["pool"]:
                block.gpsimd(mk("pool"))
            if pend["sp"]:
                block.sync(mk("sp"))


def _tiles(n, step):
    return [(a, min(a + step, n)) for a in range(0, n, step)]


def build_nc(debug=()):
    nc = bass.Bass("TRN2", target_bir_lowering=False)

    def din(name, shape, dt=F32):
        return nc.dram_tensor(name, shape, dt, kind="ExternalInput").ap()

    xT = din("xT", [D, NTA])
    wa_qkg = din("wa_qkg", [8, 128, KC * 384])
    wa_v = din("wa_v", [4, 128, KC * 256])
    wa_out = din("wa_out", [128, KC * 1024])
    gains = din("gains", [128, 32])
    biasA = din("biasA", [8, 128, 1280])
    validA = din("validA", [128, NKB_A * 64], BF16)
    kvw = din("kvw", [128, KC * 384])
    wb_qg = din("wb_qg", [8, 128, KC * 256])
    wb_out = din("wb_out", [128, KC * 1024])
    biasB = din("biasB", [8, 128, 512])
    sinkB = din("sinkB", [128, 8])
    validB = din("validB", [128, NKB_B * 64], BF16)
    outT = nc.dram_tensor("outT", [D, NW], F32, kind="ExternalOutput").ap()
    dbg_out = {}
    dbg_shapes = {"xTn": [128, KC * NTA], "attnA": [128, KC * NU], "h1T": [128, KC * NU],
                  "h1n": [128, KC * NU], "attnB": [128, KC * NW], "QT0": [128, NU], "KT0": [128, NTA],
                  "SG0": [128, NU], "V0": [128, NKB_A * 256]}
    for name in debug:
        dbg_out[name] = nc.dram_tensor("dbg_" + name, dbg_shapes[name], F32, kind="ExternalOutput").ap()

    xT_v = xT.rearrange("(kc p) t -> p kc t", p=128)
    outT_v = outT.rearrange("(kc p) t -> p kc t", p=128)

    with ExitStack() as es0:
        S = Sched(nc, es0)
        out_tokens = []

        def sbuf(es, name, shape, dt):
            return es.enter_context(nc.sbuf_tensor(name, shape, dt))

        psum = es0.enter_context(nc.psum_tensor("psum", [128, 8, 512], F32))

        def bk(b):
            return "bk%d" % b

        ones_bf = sbuf(es0, "ones_bf", [128, 128], BF16)
        eps_t = sbuf(es0, "eps_t", [128, 1], F32)
        one_t = sbuf(es0, "one_t", [128, 1], F32)
        eps60 = sbuf(es0, "eps60", [128, 1], F32)
        gains_sb = sbuf(es0, "gains_sb", [128, 32], F32)
        sink_sb = sbuf(es0, "sink_sb", [128, 8], F32)
        esink = sbuf(es0, "esink", [128, 8], F32)
        validB_sb = sbuf(es0, "validB_sb", [128, NKB_B, 64], BF16)
        attnT = sbuf(es0, "attnT", [128, KC, NU], BF16)
        dbgf = sbuf(es0, "dbgf", [128, 512], F32) if debug else None

        def dbg_dump(name, src_fn, ncols, key_reads):
            if name not in dbg_out:
                return
            for (c0, c1) in _tiles(ncols, 512):
                S.op("dve", lambda e, c0=c0, c1=c1: e.tensor_copy(out=dbgf[:, 0:c1 - c0], in_=src_fn(c0, c1)),
                     reads=key_reads, writes=["dbgf"])
                tok = S.op("sp", lambda e, c0=c0, c1=c1: e.dma_start(out=dbg_out[name][:, c0:c1], in_=dbgf[:, 0:c1 - c0]),
                           reads=["dbgf"], dma="dbg")
                out_tokens.append(tok)

        S.op("pool", lambda e: e.memset(ones_bf[:], 1.0), writes=["ones"])
        S.op("pool", lambda e: e.memset(eps_t[:], EPS), writes=["eps"])
        S.op("pool", lambda e: e.memset(one_t[:], 1.0), writes=["one"])
        S.op("pool", lambda e: e.memset(eps60[:], 2.0 ** -60), writes=["eps60"])
        S.op("sp", lambda e: e.dma_start(out=gains_sb[:], in_=gains[:]), writes=["gains"], dma="c0")
        S.op("sp", lambda e: e.dma_start(out=sink_sb[:], in_=sinkB[:]), writes=["sink"], dma="c1")
        S.op("sp", lambda e: e.dma_start(out=validB_sb[:].rearrange("p a b -> p (a b)"), in_=validB[:]),
             writes=["validB"], dma="c2")
        S.op("act", lambda e: e.activation(out=esink[:], in_=sink_sb[:], func=AF.Exp), reads=["sink"], writes=["esink"])

        A_G, KV_G, B_G, F_G = 0, 8, 16, 24

        with ExitStack() as es1:
            xT_bf = sbuf(es1, "xT_bf", [128, KC, NTA], BF16)
            validA_sb = sbuf(es1, "validA_sb", [128, NKB_A, 64], BF16)
            stg = [sbuf(es1, "stg%d" % i, [128, KC * TILE], F32) for i in range(2)]
            sq = sbuf(es1, "sq", [128, KC, TILE], BF16)
            rt = sbuf(es1, "rt", [128, TILE], F32)
            rstd = sbuf(es1, "rstd", [128, TILE], F32)
            bstg = sbuf(es1, "bstg", [128, 1280], F32)
            E_sb = [sbuf(es1, "E%d" % i, [128, 2, 640], BF16) for i in range(2)]
            wqkg_bf = [sbuf(es1, "wqkg%d" % i, [128, KC, 384], BF16) for i in range(2)]
            wv_bf = sbuf(es1, "wv_bf", [128, KC, 256], BF16)
            QT = sbuf(es1, "QT", [128, NU], BF16)
            KT = sbuf(es1, "KT", [128, NTA], BF16)
            SG = sbuf(es1, "SG", [128, NU], BF16)
            V_sb = sbuf(es1, "V_sb", [128, NKB_A, 256], BF16)
            PT = [[sbuf(es1, "PT%d_%d" % (h, s), [128, 640], BF16) for s in range(RING_A)] for h in range(2)]
            Ttmp = [sbuf(es1, "Ttmp%d" % i, [128, 512], F32) for i in range(2)]
            Rt = [sbuf(es1, "Rt%d" % i, [128, 128], F32) for i in range(2)]
            Ot = [sbuf(es1, "Ot%d" % i, [128, 128], F32) for i in range(2)]

            S.op("sp", lambda e: e.dma_start(out=validA_sb[:].rearrange("p a b -> p (a b)"), in_=validA[:]),
                 writes=["validA"], dma="c3")

            stg_ctr = [0]

            def next_stg():
                s = stg_ctr[0] % 2
                stg_ctr[0] += 1
                return s

            XT_TILES = _tiles(NTA, TILE)

            def xk(a, b_):
                return ["xT_%d" % i for i, (t0, t1) in enumerate(XT_TILES) if t0 < b_ and t1 > a]
            ALLX = xk(0, NTA)

            stat_bank = [4]

            def preamble(after_tile=None):
              for ti_, (t0, t1) in enumerate(XT_TILES):
                n = t1 - t0
                s = next_stg()
                sv = stg[s][:, 0:KC * n].rearrange("p (kc t) -> p kc t", kc=KC)
                S.op("sp", lambda e, sv=sv, t0=t0, t1=t1: e.dma_start(out=sv, in_=xT_v[:, :, t0:t1]),
                     writes=["stg%d" % s], dma="stg%d" % s)
                S.op("act", lambda e, sv=sv, n=n: e.activation(out=sq[:, :, 0:n], in_=sv, func=AF.Square),
                     reads=["stg%d" % s], writes=["sq"])
                b = stat_bank[0]
                stat_bank[0] = 4 + (stat_bank[0] - 3) % 4

                def stat_mm(e, b=b, n=n):
                    ins = None
                    for kc in range(KC):
                        ins = e.matmul(psum[:, b, 0:n], lhsT=ones_bf[:, :], rhs=sq[:, kc, 0:n],
                                       start=(kc == 0), stop=(kc == KC - 1))
                    return ins
                S.op("pe", stat_mm, reads=["sq", "ones"], writes=[bk(b)])
                S.op("act", lambda e, b=b, n=n: e.activation(out=rt[:, 0:n], in_=psum[:, b, 0:n], func=AF.Ln,
                                                             scale=1.0 / D, bias=eps_t[:, 0:1]),
                     reads=[bk(b), "eps"], writes=["rt"])
                S.op("act", lambda e, n=n: e.activation(out=rstd[:, 0:n], in_=rt[:, 0:n], func=AF.Exp, scale=-0.5),
                     reads=["rt"], writes=["rstd"])
                S.op("dve", lambda e, sv=sv, t0=t0, t1=t1, n=n: e.tensor_tensor(
                    out=xT_bf[:, :, t0:t1], in0=sv, in1=rstd[:, 0:n].unsqueeze(1).to_broadcast([128, KC, n]), op=ALU.mult),
                    reads=["stg%d" % s, "rstd"], writes=["xT_%d" % ti_])
                if after_tile is not None:
                    after_tile(t1)

            def prefetch_A(hp):
                s = next_stg()
                slot = hp % 2
                S.op("sp", lambda e, s=s, hp=hp: e.dma_start(out=stg[s][:, 0:KC * 384], in_=wa_qkg[hp]),
                     writes=["stg%d" % s], dma="stg%d" % s)

                def cast(e, s=s, slot=slot):
                    ins = None
                    sv = stg[s][:, 0:KC * 384].rearrange("p (kc c) -> p kc c", kc=KC)
                    for kc in range(KC):
                        ins = e.activation(out=wqkg_bf[slot][:, kc, :], in_=sv[:, kc, :], func=AF.Identity,
                                           scale=gains_sb[:, A_G + kc:A_G + kc + 1])
                    return ins
                S.op("act", cast, reads=["stg%d" % s, "gains"], writes=["wqkg%d" % slot])
                if hp % 2 == 0:
                    s2 = next_stg()
                    S.op("sp", lambda e, s2=s2, hp=hp: e.dma_start(out=stg[s2][:, 0:KC * 256], in_=wa_v[hp // 2]),
                         writes=["stg%d" % s2], dma="stg%d" % s2)

                    def castv(e, s2=s2):
                        ins = None
                        sv = stg[s2][:, 0:KC * 256].rearrange("p (kc c) -> p kc c", kc=KC)
                        for kc in range(KC):
                            ins = e.activation(out=wv_bf[:, kc, :], in_=sv[:, kc, :], func=AF.Identity,
                                               scale=gains_sb[:, A_G + kc:A_G + kc + 1])
                        return ins
                    S.op("act", castv, reads=["stg%d" % s2, "gains"], writes=["wv"])
                S.op("sp", lambda e, hp=hp: e.dma_start(out=bstg[:, :], in_=biasA[hp]), writes=["bstg"], dma="bstg")
                S.op("act", lambda e, slot=slot: e.activation(out=E_sb[slot][:].rearrange("p a b -> p (a b)"), in_=bstg[:, :], func=AF.Exp),
                     reads=["bstg"], writes=["E%d" % slot])

                def corners(e, slot=slot):
                    ins = None
                    for h in range(2):
                        e.memset(E_sb[slot][64:128, h, 0:64], 0.0)
                        ins = e.memset(E_sb[slot][0:64, h, 576:640], 0.0)
                    return ins
                S.op("pool", corners, reads=[], writes=["E%d" % slot])

            proj_bank = [0]
            proj_nbanks = [4]

            def next_pbank():
                b = proj_bank[0] % proj_nbanks[0]
                proj_bank[0] = (b + 1) % proj_nbanks[0]
                return b

            def proj_fm(wslot_ap_fn, rhs_fn, ncols_tiles, evac):
                for (c0, c1) in ncols_tiles:
                    b = next_pbank()
                    n = c1 - c0

                    def mm(e, b=b, c0=c0, c1=c1, n=n):
                        ins = None
                        for kc in range(KC):
                            ins = e.matmul(psum[:, b, 0:n], lhsT=wslot_ap_fn(kc), rhs=rhs_fn(kc, c0, c1),
                                           start=(kc == 0), stop=(kc == KC - 1))
                        return ins
                    yield b, c0, c1, n, mm

            def projection_items(hp):
                slot = hp % 2
                wkey = "wqkg%d" % slot
                items = []
                tcount = [0]

                def fm_tile(kind, c0, c1):
                    n = c1 - c0
                    wcol = {"g": 256, "q": 0, "k": 128}[kind]
                    xoff = 0 if kind == "k" else 512

                    def emit():
                        b = next_pbank()

                        def mm(e):
                            ins = None
                            for kc in range(KC):
                                ins = e.matmul(psum[:, b, 0:n], lhsT=wqkg_bf[slot][:, kc, wcol:wcol + 128],
                                               rhs=xT_bf[:, kc, xoff + c0:xoff + c1], start=(kc == 0), stop=(kc == KC - 1))
                            return ins
                        S.op("pe", mm, reads=[wkey] + xk(xoff + c0, xoff + c1), writes=[bk(b)])
                        if kind == "g":
                            tt = tcount[0] % 2
                            tcount[0] += 1
                            S.op("act", lambda e: e.activation(out=Ttmp[tt][:, 0:n], in_=psum[:, b, 0:n], func=AF.Exp, scale=-1.0),
                                 reads=[bk(b)], writes=["Ttmp%d" % tt])
                            S.op("act", lambda e: e.activation(out=Ttmp[tt][:, 0:n], in_=Ttmp[tt][:, 0:n], func=AF.Ln, bias=one_t[:, 0:1]),
                                 reads=["Ttmp%d" % tt, "one"], writes=["Ttmp%d" % tt])
                            S.op("act", lambda e: e.activation(out=Ttmp[tt][:, 0:n], in_=Ttmp[tt][:, 0:n], func=AF.Exp, scale=-1.0),
                                 reads=["Ttmp%d" % tt], writes=["Ttmp%d" % tt])
                            S.op("dve", lambda e: e.tensor_tensor(out=SG[:, c0:c1], in0=psum[:, b, 0:n], in1=Ttmp[tt][:, 0:n], op=ALU.mult),
                                 reads=[bk(b), "Ttmp%d" % tt], writes=["SG"])
                        elif kind == "q":
                            S.op("dve", lambda e: e.tensor_copy(out=QT[:, c0:c1], in_=psum[:, b, 0:n]), reads=[bk(b)], writes=["QT"])
                        else:
                            S.op("dve", lambda e: e.tensor_copy(out=KT[:, c0:c1], in_=psum[:, b, 0:n]), reads=[bk(b)], writes=["KT"])
                    return (xoff + c1, emit)

                def v_tile(tb0):
                    tbs = [tb for tb in (tb0, tb0 + 1) if tb < NKB_A]
                    nt = len(tbs)

                    def emit():
                        b = next_pbank()

                        def mmv(e):
                            ins = None
                            for i, tb in enumerate(tbs):
                                for kc in range(KC):
                                    ins = e.matmul(psum[:, b, i * 256:(i + 1) * 256], lhsT=xT_bf[:, kc, tb * 128:(tb + 1) * 128],
                                                   rhs=wv_bf[:, kc, :], start=(kc == 0), stop=(kc == KC - 1))
                            return ins
                        S.op("pe", mmv, reads=["wv"] + xk(tbs[0] * 128, (tbs[-1] + 1) * 128), writes=[bk(b)])
                        S.op("act", lambda e: e.activation(
                            out=V_sb[:, tb0:tb0 + nt, :], in_=psum[:, b, 0:nt * 256].rearrange("p (a c) -> p a c", a=nt), func=AF.Copy),
                            reads=[bk(b)], writes=["V"])
                    return ((tbs[-1] + 1) * 128, emit)

                for (c0, c1) in _tiles(NU, 512):
                    items.append(fm_tile("g", c0, c1))
                for (c0, c1) in _tiles(NU, 512):
                    items.append(fm_tile("q", c0, c1))
                for (c0, c1) in _tiles(NTA, 512):
                    items.append(fm_tile("k", c0, c1))
                if hp % 2 == 0:
                    for tb0 in range(0, NKB_A, 2):
                        items.append(v_tile(tb0))
                return items

            def projection_A(hp):
                for need, emit in projection_items(hp):
                    emit()

            nd_ctr = [0]

            def attention_A(hp):
                slot = hp % 2
                hpl = hp % 2
                ekey = "E%d" % slot

                def pv(ju):
                    i = nd_ctr[0] % 2
                    nd_ctr[0] += 1
                    nb, db = 4 + i, 6 + i

                    def mm(e, ju=ju, nb=nb, db=db):
                        ins = None
                        kbs = list(range(ju, ju + 5))
                        for kind in ("den", "num"):
                            for h in range(2):
                                for idx, kb2 in enumerate(kbs):
                                    col = (ju + 4 - kb2) * 128
                                    st, sp_ = (idx == 0), (idx == len(kbs) - 1)
                                    r = kb2 % RING_A
                                    if kind == "den":
                                        e.matmul(psum[h * 64:(h + 1) * 64, db, 0:128], lhsT=validA_sb[:, kb2, :],
                                                 rhs=PT[h][r][:, col:col + 128], start=st, stop=sp_)
                                    else:
                                        ins = e.matmul(psum[h * 64:(h + 1) * 64, nb, 0:128],
                                                       lhsT=V_sb[:, kb2, hpl * 128 + h * 64:hpl * 128 + (h + 1) * 64],
                                                       rhs=PT[h][r][:, col:col + 128], start=st, stop=sp_)
                        return ins
                    rd = ["V", "validA"] + ["PT%d_%d" % (h, kb2 % RING_A) for h in range(2) for kb2 in range(ju, ju + 5)]
                    S.op("pe", mm, reads=rd, writes=[bk(nb), bk(db)])
                    S.op("act", lambda e, i=i, db=db: e.activation(out=Rt[i][:, :], in_=psum[:, db, 0:128], func=AF.Ln, bias=eps60[:, 0:1]),
                         reads=[bk(db), "eps60"], writes=["Rt%d" % i])
                    S.op("act", lambda e, i=i: e.activation(out=Rt[i][:, :], in_=Rt[i][:, :], func=AF.Exp, scale=-1.0),
                         reads=["Rt%d" % i], writes=["Rt%d" % i])
                    S.op("dve", lambda e, i=i, nb=nb: e.tensor_tensor(out=Ot[i][:, :], in0=psum[:, nb, 0:128], in1=Rt[i][:, :], op=ALU.mult),
                         reads=[bk(nb), "Rt%d" % i], writes=["Ot%d" % i])
                    S.op("pool", lambda e, i=i, ju=ju: e.tensor_tensor(out=attnT[:, hp, ju * 128:(ju + 1) * 128], in0=Ot[i][:, :],
                                                                       in1=SG[:, ju * 128:(ju + 1) * 128], op=ALU.mult),
                         reads=["Ot%d" % i, "SG"], writes=["attnT"])

                for kb in range(NKB_A):
                    jmin, jmax = max(kb, 4), min(kb + 4, 20)
                    c0, c1 = (jmin - kb) * 128, (jmax - kb + 1) * 128
                    u0 = (jmin - 4) * 128
                    r = kb % RING_A
                    for h in range(2):
                        def mm(e, h=h, kb=kb, c0=c0, c1=c1, u0=u0):
                            ins = None
                            a = c0
                            while a < c1:
                                bnk = 2 * h + (a // 512)
                                bend = min(c1, (a // 512 + 1) * 512)
                                ins = e.matmul(psum[:, bnk, a % 512:a % 512 + (bend - a)],
                                               lhsT=KT[h * 64:(h + 1) * 64, kb * 128:(kb + 1) * 128],
                                               rhs=QT[h * 64:(h + 1) * 64, u0 + (a - c0):u0 + (bend - c0)], start=True, stop=True)
                                a = bend
                            return ins
                        S.op("pe", mm, reads=["QT", "KT"], writes=[bk(2 * h), bk(2 * h + 1)])
                        psS = psum[:, 2 * h:2 * h + 2, :].rearrange("p a b -> p (a b)")
                        S.op("act", lambda e, h=h, r=r, c0=c0, c1=c1, psS=psS: e.activation(
                            out=PT[h][r][:, c0:c1], in_=psS[:, c0:c1], func=AF.Exp, scale=0.125),
                            reads=[bk(2 * h), bk(2 * h + 1)], writes=["PT%d_%d" % (h, r)])
                        S.op("dve", lambda e, h=h, r=r, c0=c0, c1=c1, slot=slot: e.tensor_tensor(
                            out=PT[h][r][:, c0:c1], in0=PT[h][r][:, c0:c1], in1=E_sb[slot][:, h, c0:c1], op=ALU.mult),
                            reads=["PT%d_%d" % (h, r), ekey], writes=["PT%d_%d" % (h, r)])
                    ju = kb - 5
                    if ju >= 0:
                        pv(ju)
                for ju in range(NKB_A - 5, NQB_A):
                    pv(ju)

            prefetch_A(0)
            items0 = projection_items(0)

            prev_t1 = [0]

            def after_tile(t1):
                lim, prev_t1[0] = prev_t1[0], t1
                rest = []
                for need, emit in items0:
                    if need <= lim:
                        emit()
                    else:
                        rest.append((need, emit))
                items0[:] = rest
            preamble(after_tile)
            proj_nbanks[0] = 8
            after_tile(NTA)
            assert not items0
            prefetch_A(1)
            dbg_dump("xTn", lambda c0, c1: xT_bf[:].rearrange("p a b -> p (a b)")[:, c0:c1], KC * NTA, ALLX)
            for hp in range(8):
                if hp >= 1 and hp + 1 < 8:
                    prefetch_A(hp + 1)
                if hp >= 1:
                    projection_A(hp)
                if hp == 0:
                    dbg_dump("QT0", lambda c0, c1: QT[:, c0:c1], NU, ["QT"])
                    dbg_dump("KT0", lambda c0, c1: KT[:, c0:c1], NTA, ["KT"])
                    dbg_dump("SG0", lambda c0, c1: SG[:, c0:c1], NU, ["SG"])
                    dbg_dump("V0", lambda c0, c1: V_sb[:].rearrange("p a b -> p (a b)")[:, c0:c1], NKB_A * 256, ["V"])
                attention_A(hp)
            dbg_dump("attnA", lambda c0, c1: attnT[:].rearrange("p a b -> p (a b)")[:, c0:c1], KC * NU, ["attnT"])
            S.flush()

        with ExitStack() as es2:
            h1T = sbuf(es2, "h1T", [128, KC, NU], F32)
            h1n = sbuf(es2, "h1n", [128, KC, NU], BF16)
            kvw_bf = sbuf(es2, "kvw_bf", [128, KC, 384], BF16)

            def load_wout(es, wsrc, stgs, wout_bf):
                for i in range(4):
                    s = i % 2
                    S.op("sp", lambda e, s=s, i=i: e.dma_start(out=stgs[s][:, 0:2048], in_=wsrc[:, i * 2048:(i + 1) * 2048]),
                         writes=["wstg%d" % s], dma="wstg%d" % s)
                    S.op("act", lambda e, s=s, i=i: e.activation(
                        out=wout_bf[:, :, i * 256:(i + 1) * 256], in_=stgs[s][:, 0:2048].rearrange("p (kc c) -> p kc c", kc=KC), func=AF.Copy),
                        reads=["wstg%d" % s], writes=["wout%d" % i])

            def stats_a(src_ap_fn, src_key, n, sq):
                S.op("act", lambda e, n=n: e.activation(out=sq[:, :, 0:n], in_=src_ap_fn(), func=AF.Square),
                     reads=[src_key], writes=["sq"])

            def stats_b(n, sq, rt, rstd, bank_ctr):
                b = 4 + bank_ctr[0] % 4
                bank_ctr[0] += 1

                def stat_mm(e, b=b, n=n):
                    ins = None
                    for kc in range(KC):
                        ins = e.matmul(psum[:, b, 0:n], lhsT=ones_bf[:, :], rhs=sq[:, kc, 0:n],
                                       start=(kc == 0), stop=(kc == KC - 1))
                    return ins
                S.op("pe", stat_mm, reads=["sq", "ones"], writes=[bk(b)])
                S.op("act", lambda e, b=b, n=n: e.activation(out=rt[:, 0:n], in_=psum[:, b, 0:n], func=AF.Ln,
                                                             scale=1.0 / D, bias=eps_t[:, 0:1]),
                     reads=[bk(b), "eps"], writes=["rt"])
                S.op("act", lambda e, n=n: e.activation(out=rstd[:, 0:n], in_=rt[:, 0:n], func=AF.Exp, scale=-0.5),
                     reads=["rt"], writes=["rstd"])

            with ExitStack() as es2a:
                stg2 = [sbuf(es2a, "stg2_%d" % i, [128, KC * TILE], F32) for i in range(2)]
                wout_bf = sbuf(es2a, "woutA_bf", [128, KC, 1024], BF16)
                sq = sbuf(es2a, "sq2", [128, KC, TILE], BF16)
                rt = sbuf(es2a, "rt2", [128, TILE], F32)
                rstd = sbuf(es2a, "rstd2", [128, TILE], F32)
                load_wout(es2a, wa_out, stg2, wout_bf)
                kvstg = sbuf(es2a, "kvstg", [128, KC * 128], F32)

                def prefetch_kvw():
                    for part in range(3):
                        src = kvw[:].rearrange("p (kc c) -> p kc c", kc=KC)[:, :, part * 128:(part + 1) * 128]
                        S.op("sp", lambda e, src=src: e.dma_start(out=kvstg[:, :].rearrange("p (kc c) -> p kc c", kc=KC), in_=src),
                             writes=["kvstg"], dma="kvstg")

                        def castkv(e, part=part):
                            ins = None
                            sv = kvstg[:, :].rearrange("p (kc c) -> p kc c", kc=KC)
                            for kc in range(KC):
                                ins = e.activation(out=kvw_bf[:, kc, part * 128:(part + 1) * 128], in_=sv[:, kc, :], func=AF.Identity,
                                                   scale=gains_sb[:, KV_G + kc:KV_G + kc + 1])
                            return ins
                        S.op("act", castkv, reads=["kvstg", "gains"], writes=["kvw"])
                bctr = [0]
                wb = [0]
                pendingA = []
                for ti, (u0, u1) in enumerate(_tiles(NU, TILE)):
                    n = u1 - u0
                    s = ti % 2
                    sv = stg2[s][:, 0:KC * n].rearrange("p (kc t) -> p kc t", kc=KC)
                    S.op("sp", lambda e, sv=sv, u0=u0, u1=u1: e.dma_start(out=sv, in_=xT_v[:, :, 512 + u0:512 + u1]),
                         writes=["wstg%d" % s], dma="wstg%d" % s)
                    for oc in range(KC):
                        b = wb[0] % 4
                        wb[0] += 1

                        def mm(e, b=b, oc=oc, u0=u0, u1=u1, n=n):
                            ins = None
                            for kc in range(KC):
                                ins = e.matmul(psum[:, b, 0:n], lhsT=wout_bf[:, kc, oc * 128:(oc + 1) * 128],
                                               rhs=attnT[:, kc, u0:u1], start=(kc == 0), stop=(kc == KC - 1))
                            return ins
                        S.op("pe", mm, reads=["wout%d" % (oc // 2), "attnT"], writes=[bk(b)])
                        S.op("dve", lambda e, b=b, oc=oc, u0=u0, u1=u1, n=n, sv=sv: e.tensor_tensor(
                            out=h1T[:, oc, u0:u1], in0=psum[:, b, 0:n], in1=sv[:, oc, :], op=ALU.add),
                            reads=[bk(b), "wstg%d" % s], writes=["h1T_%d" % ti])
                    def finish(ti=ti, u0=u0, u1=u1, n=n):
                        stats_b(n, sq, rt, rstd, bctr)
                        S.op("dve", lambda e: e.tensor_tensor(
                            out=h1n[:, :, u0:u1], in0=h1T[:, :, u0:u1], in1=rstd[:, 0:n].unsqueeze(1).to_broadcast([128, KC, n]), op=ALU.mult),
                            reads=["h1T_%d" % ti, "rstd"], writes=["h1n"])
                    if ti == 3:
                        prefetch_kvw()
                    if pendingA:
                        pendingA.pop()()
                    stats_a(lambda u0=u0, u1=u1: h1T[:, :, u0:u1], "h1T_%d" % ti, n, sq)
                    pendingA.append(finish)
                pendingA.pop()()
                dbg_dump("h1T", lambda c0, c1: h1T[:].rearrange("p a b -> p (a b)")[:, c0:c1], KC * NU, ["h1n"])
                dbg_dump("h1n", lambda c0, c1: h1n[:].rearrange("p a b -> p (a b)")[:, c0:c1], KC * NU, ["h1n"])
                S.flush()

            with ExitStack() as es2b:
                stgB = [sbuf(es2b, "stgB%d" % i, [128, KC * 256], F32) for i in range(2)]
                wqg_bf = [sbuf(es2b, "wqg%d" % i, [128, KC, 256], BF16) for i in range(2)]
                KshT = [sbuf(es2b, "KshT%d" % g, [128, NU], BF16) for g in range(2)]
                Vsh = sbuf(es2b, "Vsh", [128, NKB_B, 128], BF16)
                QTb = sbuf(es2b, "QTb", [128, NW], BF16)
                SGb = sbuf(es2b, "SGb", [128, NW], BF16)
                PTb = [[sbuf(es2b, "PTb%d_%d" % (h, s), [128, 256], BF16) for s in range(RING_B)] for h in range(2)]
                bstgB = sbuf(es2b, "bstgB", [128, 512], F32)
                EB = [sbuf(es2b, "EB%d" % i, [128, 2, 256], BF16) for i in range(2)]
                TtmpB = [sbuf(es2b, "TtmpB%d" % i, [128, 512], F32) for i in range(2)]
                RtB = [sbuf(es2b, "RtB%d" % i, [128, 128], F32) for i in range(2)]
                OtB = [sbuf(es2b, "OtB%d" % i, [128, 128], F32) for i in range(2)]
                sctr = [0]

                def next_stgB():
                    s = sctr[0] % 2
                    sctr[0] += 1
                    return s

                def prefetch_B(hp):
                    s = next_stgB()
                    slot = hp % 2
                    S.op("sp", lambda e, s=s, hp=hp: e.dma_start(out=stgB[s][:, :], in_=wb_qg[hp]),
                         writes=["stgB%d" % s], dma="stgB%d" % s)

                    def cast(e, s=s, slot=slot):
                        ins = None
                        sv = stgB[s][:, :].rearrange("p (kc c) -> p kc c", kc=KC)
                        for kc in range(KC):
                            ins = e.activation(out=wqg_bf[slot][:, kc, :], in_=sv[:, kc, :], func=AF.Identity,
                                               scale=gains_sb[:, B_G + kc:B_G + kc + 1])
                        return ins
                    S.op("act", cast, reads=["stgB%d" % s, "gains"], writes=["wqg%d" % slot])
                    S.op("sp", lambda e, hp=hp: e.dma_start(out=bstgB[:, :], in_=biasB[hp]), writes=["bstgB"], dma="bstgB")
                    S.op("act", lambda e, slot=slot: e.activation(out=EB[slot][:].rearrange("p a b -> p (a b)"), in_=bstgB[:, :], func=AF.Exp),
                         reads=["bstgB"], writes=["EB%d" % slot])

                    def corners(e, slot=slot):
                        ins = None
                        for h in range(2):
                            e.memset(EB[slot][64:128, h, 0:64], 0.0)
                            ins = e.memset(EB[slot][0:64, h, 192:256], 0.0)
                        return ins
                    S.op("pool", corners, reads=[], writes=["EB%d" % slot])

                pbank = [0]

                def next_pb():
                    b = pbank[0]
                    pbank[0] = (b + 1) % 8
                    return b

                for g in range(2):
                    for (c0, c1) in _tiles(NU, 512):
                        b = next_pb()
                        n = c1 - c0

                        def mm(e, b=b, g=g, c0=c0, c1=c1, n=n):
                            ins = None
                            for kc in range(KC):
                                ins = e.matmul(psum[:, b, 0:n], lhsT=kvw_bf[:, kc, g * 128:(g + 1) * 128], rhs=h1n[:, kc, c0:c1],
                                               start=(kc == 0), stop=(kc == KC - 1))
                            return ins
                        S.op("pe", mm, reads=["kvw", "h1n"], writes=[bk(b)])
                        S.op("dve", lambda e, b=b, g=g, c0=c0, c1=c1, n=n: e.tensor_copy(out=KshT[g][:, c0:c1], in_=psum[:, b, 0:n]),
                             reads=[bk(b)], writes=["KshT%d" % g])
                for tb0 in range(0, NKB_B, 4):
                    tbs = [tb for tb in range(tb0, tb0 + 4) if tb < NKB_B]
                    b = next_pb()

                    def mmv(e, b=b, tbs=tbs):
                        ins = None
                        for i, tb in enumerate(tbs):
                            for kc in range(KC):
                                ins = e.matmul(psum[:, b, i * 128:(i + 1) * 128], lhsT=h1n[:, kc, tb * 128:(tb + 1) * 128],
                                               rhs=kvw_bf[:, kc, 256:384], start=(kc == 0), stop=(kc == KC - 1))
                        return ins
                    S.op("pe", mmv, reads=["kvw", "h1n"], writes=[bk(b)])
                    nt = len(tbs)
                    S.op("act", lambda e, b=b, tb0=tb0, nt=nt: e.activation(
                        out=Vsh[:, tb0:tb0 + nt, :], in_=psum[:, b, 0:nt * 128].rearrange("p (a c) -> p a c", a=nt), func=AF.Copy),
                        reads=[bk(b)], writes=["Vsh"])

                def projection_B(hp):
                    slot = hp % 2
                    wkey = "wqg%d" % slot
                    ti = 0
                    for (c0, c1) in _tiles(NW, 512):
                        b = next_pb()
                        n = c1 - c0

                        def mm(e, b=b, c0=c0, c1=c1, n=n):
                            ins = None
                            for kc in range(KC):
                                ins = e.matmul(psum[:, b, 0:n], lhsT=wqg_bf[slot][:, kc, 128:256], rhs=h1n[:, kc, 128 + c0:128 + c1],
                                               start=(kc == 0), stop=(kc == KC - 1))
                            return ins
                        S.op("pe", mm, reads=[wkey, "h1n"], writes=[bk(b)])
                        tt = ti % 2
                        ti += 1
                        S.op("act", lambda e, b=b, n=n, tt=tt: e.activation(out=TtmpB[tt][:, 0:n], in_=psum[:, b, 0:n], func=AF.Exp, scale=-1.0),
                             reads=[bk(b)], writes=["TtmpB%d" % tt])
                        S.op("act", lambda e, n=n, tt=tt: e.activation(out=TtmpB[tt][:, 0:n], in_=TtmpB[tt][:, 0:n], func=AF.Ln, bias=one_t[:, 0:1]),
                             reads=["TtmpB%d" % tt, "one"], writes=["TtmpB%d" % tt])
                        S.op("act", lambda e, n=n, tt=tt: e.activation(out=TtmpB[tt][:, 0:n], in_=TtmpB[tt][:, 0:n], func=AF.Exp, scale=-1.0),
                             reads=["TtmpB%d" % tt], writes=["TtmpB%d" % tt])
                        S.op("dve", lambda e, b=b, c0=c0, c1=c1, n=n, tt=tt: e.tensor_tensor(
                            out=SGb[:, c0:c1], in0=psum[:, b, 0:n], in1=TtmpB[tt][:, 0:n], op=ALU.mult),
                            reads=[bk(b), "TtmpB%d" % tt], writes=["SGb"])
                    for (c0, c1) in _tiles(NW, 512):
                        b = next_pb()
                        n = c1 - c0

                        def mm(e, b=b, c0=c0, c1=c1, n=n):
                            ins = None
                            for kc in range(KC):
                                ins = e.matmul(psum[:, b, 0:n], lhsT=wqg_bf[slot][:, kc, 0:128], rhs=h1n[:, kc, 128 + c0:128 + c1],
                                               start=(kc == 0), stop=(kc == KC - 1))
                            return ins
                        S.op("pe", mm, reads=[wkey, "h1n"], writes=[bk(b)])
                        S.op("dve", lambda e, b=b, c0=c0, c1=c1, n=n: e.tensor_copy(out=QTb[:, c0:c1], in_=psum[:, b, 0:n]),
                             reads=[bk(b)], writes=["QTb"])

                ndb = [0]

                def attention_B(hp):
                    slot = hp % 2
                    g = hp // 4
                    ekey = "EB%d" % slot

                    def pv(jw):
                        i = ndb[0] % 2
                        ndb[0] += 1
                        nb, db = 4 + i, 6 + i

                        def mm(e, jw=jw, nb=nb, db=db):
                            ins = None
                            kbs = [jw, jw + 1]
                            for idx, kb2 in enumerate(kbs):
                                col = (jw + 1 - kb2) * 128
                                st, sp_ = (idx == 0), (idx == len(kbs) - 1)
                                r = kb2 % RING_B
                                for h in range(2):
                                    e.matmul(psum[h * 64:(h + 1) * 64, nb, 0:128], lhsT=Vsh[:, kb2, g * 64:(g + 1) * 64],
                                             rhs=PTb[h][r][:, col:col + 128], start=st, stop=sp_)
                                for h in range(2):
                                    ins = e.matmul(psum[h * 64:(h + 1) * 64, db, 0:128], lhsT=validB_sb[:, kb2, :],
                                                   rhs=PTb[h][r][:, col:col + 128], start=st, stop=sp_)
                            return ins
                        rd = ["Vsh", "validB"] + ["PTb%d_%d" % (h, kb2 % RING_B) for h in range(2) for kb2 in (jw, jw + 1)]
                        S.op("pe", mm, reads=rd, writes=[bk(nb), bk(db)])
                        S.op("act", lambda e, i=i, db=db: e.activation(out=RtB[i][:, :], in_=psum[:, db, 0:128], func=AF.Ln, bias=esink[:, hp:hp + 1]),
                             reads=[bk(db), "esink"], writes=["RtB%d" % i])
                        S.op("act", lambda e, i=i: e.activation(out=RtB[i][:, :], in_=RtB[i][:, :], func=AF.Exp, scale=-1.0),
                             reads=["RtB%d" % i], writes=["RtB%d" % i])
                        S.op("dve", lambda e, i=i, nb=nb: e.tensor_tensor(out=OtB[i][:, :], in0=psum[:, nb, 0:128], in1=RtB[i][:, :], op=ALU.mult),
                             reads=[bk(nb), "RtB%d" % i], writes=["OtB%d" % i])
                        S.op("pool", lambda e, i=i, jw=jw: e.tensor_tensor(out=attnT[:, hp, jw * 128:(jw + 1) * 128], in0=OtB[i][:, :],
                                                                           in1=SGb[:, jw * 128:(jw + 1) * 128], op=ALU.mult),
                             reads=["OtB%d" % i, "SGb"], writes=["attnT"])

                    for kb in range(NKB_B):
                        jmin, jmax = max(kb - 1, 0), min(kb, NQB_B - 1)
                        c0, c1 = (jmin + 1 - kb) * 128, (jmax + 1 - kb + 1) * 128
                        w0 = jmin * 128
                        r = kb % RING_B
                        for h in range(2):
                            S.op("pe", lambda e, h=h, kb=kb, c0=c0, c1=c1, w0=w0: e.matmul(
                                psum[:, 2 * h, c0:c1], lhsT=KshT[g][h * 64:(h + 1) * 64, kb * 128:(kb + 1) * 128],
                                rhs=QTb[h * 64:(h + 1) * 64, w0:w0 + (c1 - c0)], start=True, stop=True),
                                reads=["QTb", "KshT%d" % g], writes=[bk(2 * h)])
                            S.op("act", lambda e, h=h, r=r, c0=c0, c1=c1: e.activation(
                                out=PTb[h][r][:, c0:c1], in_=psum[:, 2 * h, c0:c1], func=AF.Exp, scale=0.125),
                                reads=[bk(2 * h)], writes=["PTb%d_%d" % (h, r)])
                            S.op("dve", lambda e, h=h, r=r, c0=c0, c1=c1: e.tensor_tensor(
                                out=PTb[h][r][:, c0:c1], in0=PTb[h][r][:, c0:c1], in1=EB[slot][:, h, c0:c1], op=ALU.mult),
                                reads=["PTb%d_%d" % (h, r), ekey], writes=["PTb%d_%d" % (h, r)])
                        jw = kb - 2
                        if jw >= 0:
                            pv(jw)
                    for jw in range(NKB_B - 2, NQB_B):
                        pv(jw)

                prefetch_B(0)
                for hp in range(8):
                    if hp + 1 < 8:
                        prefetch_B(hp + 1)
                    projection_B(hp)
                    attention_B(hp)
                dbg_dump("attnB", lambda c0, c1: attnT[:, c0 // NW, c0 % NW:c0 % NW + (c1 - c0)], KC * NW, ["attnT"])
                S.flush()

            with ExitStack() as es2c:
                TB = 256
                wstg = [sbuf(es2c, "wstgC%d" % i, [128, 2048], F32) for i in range(2)]
                wout_bf = sbuf(es2c, "woutB_bf", [128, KC, 1024], BF16)
                h2 = [sbuf(es2c, "h2_%d" % i, [128, KC, TB], F32) for i in range(3)]
                sq = sbuf(es2c, "sq3", [128, KC, TB], BF16)
                rt = sbuf(es2c, "rt3", [128, TB], F32)
                rstd = sbuf(es2c, "rstd3", [128, TB], F32)
                load_wout(es2c, wb_out, wstg, wout_bf)
                bctr = [0]
                wb = [0]
                pendingB = []
                for ti, (w0, w1) in enumerate(_tiles(NW, TB)):
                    n = w1 - w0
                    s = ti % 3
                    for oc in range(KC):
                        b = wb[0] % 4
                        wb[0] += 1

                        def mm(e, b=b, oc=oc, w0=w0, w1=w1, n=n):
                            ins = None
                            for kc in range(KC):
                                ins = e.matmul(psum[:, b, 0:n], lhsT=wout_bf[:, kc, oc * 128:(oc + 1) * 128],
                                               rhs=attnT[:, kc, w0:w1], start=(kc == 0), stop=(kc == KC - 1))
                            return ins
                        S.op("pe", mm, reads=["wout%d" % (oc // 2), "attnT"], writes=[bk(b)])
                        S.op("dve", lambda e, b=b, oc=oc, w0=w0, w1=w1, n=n, s=s: e.tensor_tensor(
                            out=h2[s][:, oc, 0:n], in0=psum[:, b, 0:n], in1=h1T[:, oc, 128 + w0:128 + w1], op=ALU.add),
                            reads=[bk(b)], writes=["h2_%d" % s])
                    def finishB(s=s, w0=w0, w1=w1, n=n):
                        stats_b(n, sq, rt, rstd, bctr)

                        def fin(e):
                            ins = None
                            for oc in range(KC):
                                ins = e.scalar_tensor_tensor(out=h2[s][:, oc, 0:n], in0=h2[s][:, oc, 0:n],
                                                             scalar=gains_sb[:, F_G + oc:F_G + oc + 1], in1=rstd[:, 0:n],
                                                             op0=ALU.mult, op1=ALU.mult)
                            return ins
                        S.op("dve", fin, reads=["h2_%d" % s, "rstd", "gains"], writes=["h2_%d" % s])
                        tok = S.op("sp", lambda e: e.dma_start(out=outT_v[:, :, w0:w1], in_=h2[s][:, :, 0:n]),
                                   reads=["h2_%d" % s], dma="out%d" % s)
                        out_tokens.append(tok)
                    if pendingB:
                        pendingB.pop()()
                    stats_a(lambda s=s, n=n: h2[s][:, :, 0:n], "h2_%d" % s, n, sq)
                    pendingB.append(finishB)
                pendingB.pop()()
                final = {}
                for key, val in out_tokens:
                    final[key] = max(final.get(key, 0), val)
                S.wait_tokens("sp", list(final.items()))
                S.flush()
    return nc


def _t5_bucket_np(rel):
    nb = 16
    max_exact = 8
    ret = np.where(rel > 0, nb, 0)
    n = np.abs(rel)
    nf = np.maximum(n, 1).astype(np.float32)
    large = max_exact + (np.log(nf / np.float32(max_exact)) / np.float32(math.log(128 / max_exact))
                         * np.float32(nb - max_exact)).astype(np.int32)
    large = np.minimum(large, nb - 1)
    return ret + np.where(n < max_exact, n, large)


def _prep_shared(a_norm, a_w_in, a_rel_bias, a_w_out, kv_norm, kv_w, t5_bias, b_norm, b_w_in, b_sinks, b_w_out, final_norm):
    f = np.float32
    w_in = np.asarray(a_w_in[0], f)
    w4 = w_in.reshape(KC, 128, 4, 8, 128)
    wa_qkg = np.ascontiguousarray(np.transpose(w4[:, :, [0, 1, 3]], (3, 1, 0, 2, 4))).reshape(8, 128, KC * 384)
    wv = w_in[:, 2048:3072].reshape(KC, 128, 4, 256)
    wa_v = np.ascontiguousarray(np.transpose(wv, (2, 1, 0, 3))).reshape(4, 128, KC * 256)
    wa_out = np.ascontiguousarray(np.transpose(np.asarray(a_w_out[0], f).reshape(KC, 128, 4, 256), (1, 2, 0, 3))).reshape(128, KC * 1024)
    wb_out = np.ascontiguousarray(np.transpose(np.asarray(b_w_out[0], f).reshape(KC, 128, 4, 256), (1, 2, 0, 3))).reshape(128, KC * 1024)

    def gcol(v):
        return np.asarray(v, f).reshape(KC, 128).T
    gains = np.ascontiguousarray(np.concatenate([gcol(a_norm[0]), gcol(kv_norm), gcol(b_norm[0]), gcol(final_norm)], axis=1))
    k = np.arange(128)[:, None]
    q = np.arange(640)[None, :]
    idxA = np.clip(q - k, -256, 256) + 256
    rb = np.asarray(a_rel_bias[0], f)
    bA = rb[idxA]
    biasA = np.ascontiguousarray(np.transpose(bA.reshape(128, 640, 8, 2), (2, 0, 3, 1))).reshape(8, 128, 1280)
    kvw_ = np.asarray(kv_w, f).reshape(KC, 128, 256)
    kcat = np.concatenate([kvw_[:, :, 0:64], kvw_[:, :, 0:64], kvw_[:, :, 64:128], kvw_[:, :, 64:128], kvw_[:, :, 128:256]], axis=2)
    kvw = np.ascontiguousarray(np.transpose(kcat, (1, 0, 2))).reshape(128, KC * 384)
    wb = np.asarray(b_w_in[0], f).reshape(KC, 128, 2, 8, 128)
    wb_qg = np.ascontiguousarray(np.transpose(wb, (3, 1, 0, 2, 4))).reshape(8, 128, KC * 256)
    qb = np.arange(256)[None, :]
    bucket = _t5_bucket_np((k - qb).astype(np.int32))
    tb = np.asarray(t5_bias, f)[bucket]
    biasB = np.ascontiguousarray(np.transpose(tb.reshape(128, 256, 8, 2), (2, 0, 3, 1))).reshape(8, 128, 512)
    sk = np.asarray(b_sinks[0], f).reshape(8, 2)
    sinkB = np.ascontiguousarray(np.repeat(sk.T, 64, axis=0))
    return dict(wa_qkg=wa_qkg, wa_v=wa_v, wa_out=wa_out, gains=gains, biasA=biasA, kvw=kvw, wb_qg=wb_qg,
                wb_out=wb_out, biasB=biasB, sinkB=sinkB)


def _prep_core(x, c):
    b, half = c // 2, c % 2
    T0 = half * NOWN
    lo = T0 - HALO
    xe = np.zeros((NTA, D), np.float32)
    src0 = max(lo, 0)
    xe[src0 - lo:, :] = x[b, src0:T0 + NOWN, :]
    xTc = np.ascontiguousarray(xe.T)
    tpos = lo + np.arange(NTA)
    vA = (tpos >= 0).astype(np.float32).reshape(NKB_A, 128).T
    validA = np.ascontiguousarray(np.repeat(vA[:, :, None], 64, axis=2)).reshape(128, NKB_A * 64).astype(ml_dtypes.bfloat16)
    upos = T0 - 128 + np.arange(NU)
    vB = (upos >= 0).astype(np.float32).reshape(NKB_B, 128).T
    validB = np.ascontiguousarray(np.repeat(vB[:, :, None], 64, axis=2)).reshape(128, NKB_B * 64).astype(ml_dtypes.bfloat16)
    return dict(xT=xTc, validA=validA, validB=validB)


_NC_CACHE = {}


def kernel(x, a_norm, a_w_in, a_rel_bias, a_w_out, kv_norm, kv_w, t5_bias, b_norm, b_w_in, b_sinks, b_w_out, final_norm,
           _debug=()):
    x = np.asarray(x, np.float32)
    shared = _prep_shared(np.asarray(a_norm), np.asarray(a_w_in), np.asarray(a_rel_bias), np.asarray(a_w_out),
                          np.asarray(kv_norm), np.asarray(kv_w), np.asarray(t5_bias), np.asarray(b_norm),
                          np.asarray(b_w_in), np.asarray(b_sinks), np.asarray(b_w_out), np.asarray(final_norm))
    in_maps = []
    for c in range(N_CORES):
        m = dict(shared)
        m.update(_prep_core(x, c))
        in_maps.append(m)
    key = tuple(_debug)
    if key not in _NC_CACHE:
        _NC_CACHE[key] = build_nc(debug=key)
    nc = _NC_CACHE[key]
    res = run_bass_kernel_spmd(nc, in_maps, core_ids=list(range(N_CORES)))
    out = np.empty((4, SEQ, D), np.float32)
    for c in range(N_CORES):
        b, half = c // 2, c % 2
        out[b, half * NOWN:(half + 1) * NOWN, :] = np.asarray(res.results[c]["outT"]).T
    if _debug:
        return out, res.results
    return out
```

```python
import math
from contextlib import ExitStack

import numpy as np
import ml_dtypes

import concourse.bass as bass
import concourse.mybir as mybir
from concourse.bass_utils import run_bass_kernel_spmd

F32 = mybir.dt.float32
BF16 = mybir.dt.bfloat16
AF = mybir.ActivationFunctionType
ALU = mybir.AluOpType

N_CORES = 8
D = 1024
KC = 8
SEQ = 4096
NOWN = 2048
HALO = 640
NTA = NOWN + HALO
NU = NOWN + 128
NW = NOWN
NKB_A = NTA // 128
NQB_A = NU // 128
NKB_B = NU // 128
NQB_B = NW // 128
RING_A = 6
RING_B = 4
TILE = 384
EPS = 1e-6

ENGS = ("pe", "act", "dve", "pool", "sp")


class Sched:
    def __init__(self, nc, es):
        self.nc = nc
        self.es = es
        self.sem = {e: es.enter_context(nc.semaphore("s_" + e)) for e in ENGS}
        self.cnt = {e: 0 for e in ENGS}
        self.dsem = {}
        self.dcnt = {}
        self.lastw = {}
        self.readers = {}
        self.pending = {e: [] for e in ENGS}
        self.seen = {e: {} for e in ENGS}
        self.know = {}
        self.order = {}
        self.nwaits = 0

    def op(self, eng, fn, reads=(), writes=(), dma=None, ndma=1):
        deps = {}

        def add(tok):
            if tok is None:
                return
            key, val = tok
            if deps.get(key, 0) < val:
                deps[key] = val

        for r in reads:
            add(self.lastw.get(r))
            if r.startswith("bk") and eng in ("act", "dve"):
                for k, v in self.readers.get(r, {}).items():
                    if k[0] in ("act", "dve") and k[0] != eng:
                        add((k, v))
        for w in writes:
            add(self.lastw.get(w))
            for k, v in self.readers.get(w, {}).items():
                add((k, v))
        if dma is not None:
            if dma not in self.dsem:
                self.dsem[dma] = self.es.enter_context(self.nc.semaphore("d_" + str(dma)))
                self.dcnt[dma] = 0
            self.dcnt[dma] += 16 * ndma
            tok = (("dma", dma), self.dcnt[dma])
        else:
            self.cnt[eng] += 1
            tok = ((eng,), self.cnt[eng])
        waits = []
        for key, val in sorted(deps.items(), key=lambda kv: -self.order.get(kv, 0)):
            if key == ("pe",) and eng == "pe":
                continue
            if self.seen[eng].get(key, 0) >= val:
                continue
            waits.append((key, val))
            for k2, v2 in self.know.get((key, val), {}).items():
                if self.seen[eng].get(k2, 0) < v2:
                    self.seen[eng][k2] = v2
            self.seen[eng][key] = val
        self.nwaits += len(waits)
        self.order[tok] = len(self.order) + 1
        if dma is None:
            kn = dict(self.seen[eng])
            kn[tok[0]] = tok[1]
            self.know[tok] = kn
        else:
            self.know[tok] = dict(self.seen[eng])
        self.pending[eng].append((fn, waits, tok))
        for r in reads:
            d = self.readers.setdefault(r, {})
            if d.get(tok[0], 0) < tok[1]:
                d[tok[0]] = tok[1]
        for w in writes:
            self.lastw[w] = tok
            self.readers[w] = {}
        return tok

    def _semof(self, key):
        if key[0] == "dma":
            return self.dsem[key[1]]
        return self.sem[key[0]]

    def wait_tokens(self, eng, toks):
        self.pending[eng].append((None, list(toks), None))

    def flush(self):
        nc = self.nc
        pend = self.pending
        self.pending = {e: [] for e in ENGS}
        with nc.Block() as block:
            def mk(engname):
                lst = pend[engname]

                def body(e):
                    class _First:
                        def __init__(self, eng):
                            self._eng = eng
                            self.first = None

                        def __getattr__(self, name):
                            real = getattr(self._eng, name)

                            def call(*a, **k):
                                r = real(*a, **k)
                                if self.first is None:
                                    self.first = r
                                return r
                            return call

                    for fn, waits, tok in lst:
                        fuse = fn is not None and len(waits) >= 1
                        for key, val in (waits[:-1] if fuse else waits):
                            e.wait_ge(self._semof(key), val)
                        if fn is None:
                            continue
                        if fuse:
                            px = _First(e)
                            ins = fn(px)
                            px.first._wait_ge(self._semof(waits[-1][0]), waits[-1][1])
                        else:
                            ins = fn(e)
                        key, val = tok
                        if key[0] == "dma":
                            if not isinstance(ins, (list, tuple)):
                                ins = [ins]
                            for i in ins:
                                i.then_inc(self.dsem[key[1]], 16)
                        else:
                            ins.then_inc(self.sem[engname], 1)
                return body

            if pend["pe"]:
                block.tensor(mk("pe"))
            if pend["act"]:
                block.scalar(mk("act"))
            if pend["dve"]:
                block.vector(mk("dve"))
            if pend["pool"]:
                block.gpsimd(mk("pool"))
            if pend["sp"]:
                block.sync(mk("sp"))


def _tiles(n, step):
    return [(a, min(a + step, n)) for a in range(0, n, step)]


def build_nc(debug=()):
    nc = bass.Bass("TRN2", target_bir_lowering=False)

    def din(name, shape, dt=F32):
        return nc.dram_tensor(name, shape, dt, kind="ExternalInput").ap()

    xT = din("xT", [D, NTA])
    wa_qkg = din("wa_qkg", [8, 128, KC * 384])
    wa_v = din("wa_v", [4, 128, KC * 256])
    wa_out = din("wa_out", [128, KC * 1024])
    gains = din("gains", [128, 32])
    biasA = din("biasA", [8, 128, 1280])
    validA = din("validA", [128, NKB_A * 64], BF16)
    kvw = din("kvw", [128, KC * 384])
    wb_qg = din("wb_qg", [8, 128, KC * 256])
    wb_out = din("wb_out", [128, KC * 1024])
    biasB = din("biasB", [8, 128, 512])
    sinkB = din("sinkB", [128, 8])
    validB = din("validB", [128, NKB_B * 64], BF16)
    outT = nc.dram_tensor("outT", [D, NW], F32, kind="ExternalOutput").ap()
    dbg_out = {}
    dbg_shapes = {"xTn": [128, KC * NTA], "attnA": [128, KC * NU], "h1T": [128, KC * NU],
                  "h1n": [128, KC * NU], "attnB": [128, KC * NW], "QT0": [128, NU], "KT0": [128, NTA],
                  "SG0": [128, NU], "V0": [128, NKB_A * 256]}
    for name in debug:
        dbg_out[name] = nc.dram_tensor("dbg_" + name, dbg_shapes[name], F32, kind="ExternalOutput").ap()

    xT_v = xT.rearrange("(kc p) t -> p kc t", p=128)
    outT_v = outT.rearrange("(kc p) t -> p kc t", p=128)

    with ExitStack() as es0:
        S = Sched(nc, es0)
        out_tokens = []

        def sbuf(es, name, shape, dt):
            return es.enter_context(nc.sbuf_tensor(name, shape, dt))

        psum = es0.enter_context(nc.psum_tensor("psum", [128, 8, 512], F32))

        def bk(b):
            return "bk%d" % b

        ones_bf = sbuf(es0, "ones_bf", [128, 128], BF16)
        eps_t = sbuf(es0, "eps_t", [128, 1], F32)
        one_t = sbuf(es0, "one_t", [128, 1], F32)
        eps60 = sbuf(es0, "eps60", [128, 1], F32)
        gains_sb = sbuf(es0, "gains_sb", [128, 32], F32)
        sink_sb = sbuf(es0, "sink_sb", [128, 8], F32)
        esink = sbuf(es0, "esink", [128, 8], F32)
        validB_sb = sbuf(es0, "validB_sb", [128, NKB_B, 64], BF16)
        attnT = sbuf(es0, "attnT", [128, KC, NU], BF16)
        dbgf = sbuf(es0, "dbgf", [128, 512], F32) if debug else None

        def dbg_dump(name, src_fn, ncols, key_reads):
            if name not in dbg_out:
                return
            for (c0, c1) in _tiles(ncols, 512):
                S.op("dve", lambda e, c0=c0, c1=c1: e.tensor_copy(out=dbgf[:, 0:c1 - c0], in_=src_fn(c0, c1)),
                     reads=key_reads, writes=["dbgf"])
                tok = S.op("sp", lambda e, c0=c0, c1=c1: e.dma_start(out=dbg_out[name][:, c0:c1], in_=dbgf[:, 0:c1 - c0]),
                           reads=["dbgf"], dma="dbg")
                out_tokens.append(tok)

        S.op("pool", lambda e: e.memset(ones_bf[:], 1.0), writes=["ones"])
        S.op("pool", lambda e: e.memset(eps_t[:], EPS), writes=["eps"])
        S.op("pool", lambda e: e.memset(one_t[:], 1.0), writes=["one"])
        S.op("pool", lambda e: e.memset(eps60[:], 2.0 ** -60), writes=["eps60"])
        S.op("sp", lambda e: e.dma_start(out=gains_sb[:], in_=gains[:]), writes=["gains"], dma="c0")
        S.op("sp", lambda e: e.dma_start(out=sink_sb[:], in_=sinkB[:]), writes=["sink"], dma="c1")
        S.op("sp", lambda e: e.dma_start(out=validB_sb[:].rearrange("p a b -> p (a b)"), in_=validB[:]),
             writes=["validB"], dma="c2")
        S.op("act", lambda e: e.activation(out=esink[:], in_=sink_sb[:], func=AF.Exp), reads=["sink"], writes=["esink"])

        A_G, KV_G, B_G, F_G = 0, 8, 16, 24

        with ExitStack() as es1:
            xT_bf = sbuf(es1, "xT_bf", [128, KC, NTA], BF16)
            validA_sb = sbuf(es1, "validA_sb", [128, NKB_A, 64], BF16)
            stg = [sbuf(es1, "stg%d" % i, [128, KC * TILE], F32) for i in range(2)]
            sq = sbuf(es1, "sq", [128, KC, TILE], BF16)
            rt = sbuf(es1, "rt", [128, TILE], F32)
            rstd = sbuf(es1, "rstd", [128, TILE], F32)
            bstg = sbuf(es1, "bstg", [128, 1280], F32)
            E_sb = [sbuf(es1, "E%d" % i, [128, 2, 640], BF16) for i in range(2)]
            wqkg_bf = [sbuf(es1, "wqkg%d" % i, [128, KC, 384], BF16) for i in range(2)]
            wv_bf = sbuf(es1, "wv_bf", [128, KC, 256], BF16)
            QT = sbuf(es1, "QT", [128, NU], BF16)
            KT = sbuf(es1, "KT", [128, NTA], BF16)
            SG = sbuf(es1, "SG", [128, NU], BF16)
            V_sb = sbuf(es1, "V_sb", [128, NKB_A, 256], BF16)
            PT = [[sbuf(es1, "PT%d_%d" % (h, s), [128, 640], BF16) for s in range(RING_A)] for h in range(2)]
            Ttmp = [sbuf(es1, "Ttmp%d" % i, [128, 512], F32) for i in range(2)]
            Rt = [sbuf(es1, "Rt%d" % i, [128, 128], F32) for i in range(2)]
            Ot = [sbuf(es1, "Ot%d" % i, [128, 128], F32) for i in range(2)]

            S.op("sp", lambda e: e.dma_start(out=validA_sb[:].rearrange("p a b -> p (a b)"), in_=validA[:]),
                 writes=["validA"], dma="c3")

            stg_ctr = [0]

            def next_stg():
                s = stg_ctr[0] % 2
                stg_ctr[0] += 1
                return s

            XT_TILES = _tiles(NTA, TILE)

            def xk(a, b_):
                return ["xT_%d" % i for i, (t0, t1) in enumerate(XT_TILES) if t0 < b_ and t1 > a]
            ALLX = xk(0, NTA)

            stat_bank = [4]

            def preamble(after_tile=None):
              for ti_, (t0, t1) in enumerate(XT_TILES):
                n = t1 - t0
                s = next_stg()
                sv = stg[s][:, 0:KC * n].rearrange("p (kc t) -> p kc t", kc=KC)
                S.op("sp", lambda e, sv=sv, t0=t0, t1=t1: e.dma_start(out=sv, in_=xT_v[:, :, t0:t1]),
                     writes=["stg%d" % s], dma="stg%d" % s)
                S.op("act", lambda e, sv=sv, n=n: e.activation(out=sq[:, :, 0:n], in_=sv, func=AF.Square),
                     reads=["stg%d" % s], writes=["sq"])
                b = stat_bank[0]
                stat_bank[0] = 4 + (stat_bank[0] - 3) % 4

                def stat_mm(e, b=b, n=n):
                    ins = None
                    for kc in range(KC):
                        ins = e.matmul(psum[:, b, 0:n], lhsT=ones_bf[:, :], rhs=sq[:, kc, 0:n],
                                       start=(kc == 0), stop=(kc == KC - 1))
                    return ins
                S.op("pe", stat_mm, reads=["sq", "ones"], writes=[bk(b)])
                S.op("act", lambda e, b=b, n=n: e.activation(out=rt[:, 0:n], in_=psum[:, b, 0:n], func=AF.Ln,
                                                             scale=1.0 / D, bias=eps_t[:, 0:1]),
                     reads=[bk(b), "eps"], writes=["rt"])
                S.op("act", lambda e, n=n: e.activation(out=rstd[:, 0:n], in_=rt[:, 0:n], func=AF.Exp, scale=-0.5),
                     reads=["rt"], writes=["rstd"])
                S.op("dve", lambda e, sv=sv, t0=t0, t1=t1, n=n: e.tensor_tensor(
                    out=xT_bf[:, :, t0:t1], in0=sv, in1=rstd[:, 0:n].unsqueeze(1).to_broadcast([128, KC, n]), op=ALU.mult),
                    reads=["stg%d" % s, "rstd"], writes=["xT_%d" % ti_])
                if after_tile is not None:
                    after_tile(t1)

            def prefetch_A(hp):
                s = next_stg()
                slot = hp % 2
                S.op("sp", lambda e, s=s, hp=hp: e.dma_start(out=stg[s][:, 0:KC * 384], in_=wa_qkg[hp]),
                     writes=["stg%d" % s], dma="stg%d" % s)

                def cast(e, s=s, slot=slot):
                    ins = None
                    sv = stg[s][:, 0:KC * 384].rearrange("p (kc c) -> p kc c", kc=KC)
                    for kc in range(KC):
                        ins = e.activation(out=wqkg_bf[slot][:, kc, :], in_=sv[:, kc, :], func=AF.Identity,
                                           scale=gains_sb[:, A_G + kc:A_G + kc + 1])
                    return ins
                S.op("act", cast, reads=["stg%d" % s, "gains"], writes=["wqkg%d" % slot])
                if hp % 2 == 0:
                    s2 = next_stg()
                    S.op("sp", lambda e, s2=s2, hp=hp: e.dma_start(out=stg[s2][:, 0:KC * 256], in_=wa_v[hp // 2]),
                         writes=["stg%d" % s2], dma="stg%d" % s2)

                    def castv(e, s2=s2):
                        ins = None
                        sv = stg[s2][:, 0:KC * 256].rearrange("p (kc c) -> p kc c", kc=KC)
                        for kc in range(KC):
                            ins = e.activation(out=wv_bf[:, kc, :], in_=sv[:, kc, :], func=AF.Identity,
                                               scale=gains_sb[:, A_G + kc:A_G + kc + 1])
                        return ins
                    S.op("act", castv, reads=["stg%d" % s2, "gains"], writes=["wv"])
                S.op("sp", lambda e, hp=hp: e.dma_start(out=bstg[:, :], in_=biasA[hp]), writes=["bstg"], dma="bstg")
                S.op("act", lambda e, slot=slot: e.activation(out=E_sb[slot][:].rearrange("p a b -> p (a b)"), in_=bstg[:, :], func=AF.Exp),
                     reads=["bstg"], writes=["E%d" % slot])

                def corners(e, slot=slot):
                    ins = None
                    for h in range(2):
                        e.memset(E_sb[slot][64:128, h, 0:64], 0.0)
                        ins = e.memset(E_sb[slot][0:64, h, 576:640], 0.0)
                    return ins
                S.op("pool", corners, reads=[], writes=["E%d" % slot])

            proj_bank = [0]
            proj_nbanks = [4]

            def next_pbank():
                b = proj_bank[0] % proj_nbanks[0]
                proj_bank[0] = (b + 1) % proj_nbanks[0]
                return b

            def proj_fm(wslot_ap_fn, rhs_fn, ncols_tiles, evac):
                for (c0, c1) in ncols_tiles:
                    b = next_pbank()
                    n = c1 - c0

                    def mm(e, b=b, c0=c0, c1=c1, n=n):
                        ins = None
                        for kc in range(KC):
                            ins = e.matmul(psum[:, b, 0:n], lhsT=wslot_ap_fn(kc), rhs=rhs_fn(kc, c0, c1),
                                           start=(kc == 0), stop=(kc == KC - 1))
                        return ins
                    yield b, c0, c1, n, mm

            def projection_items(hp):
                slot = hp % 2
                wkey = "wqkg%d" % slot
                items = []
                tcount = [0]

                def fm_tile(kind, c0, c1):
                    n = c1 - c0
                    wcol = {"g": 256, "q": 0, "k": 128}[kind]
                    xoff = 0 if kind == "k" else 512

                    def emit():
                        b = next_pbank()

                        def mm(e):
                            ins = None
                            for kc in range(KC):
                                ins = e.matmul(psum[:, b, 0:n], lhsT=wqkg_bf[slot][:, kc, wcol:wcol + 128],
                                               rhs=xT_bf[:, kc, xoff + c0:xoff + c1], start=(kc == 0), stop=(kc == KC - 1))
                            return ins
                        S.op("pe", mm, reads=[wkey] + xk(xoff + c0, xoff + c1), writes=[bk(b)])
                        if kind == "g":
                            tt = tcount[0] % 2
                            tcount[0] += 1
                            S.op("act", lambda e: e.activation(out=Ttmp[tt][:, 0:n], in_=psum[:, b, 0:n], func=AF.Exp, scale=-1.0),
                                 reads=[bk(b)], writes=["Ttmp%d" % tt])
                            S.op("act", lambda e: e.activation(out=Ttmp[tt][:, 0:n], in_=Ttmp[tt][:, 0:n], func=AF.Ln, bias=one_t[:, 0:1]),
                                 reads=["Ttmp%d" % tt, "one"], writes=["Ttmp%d" % tt])
                            S.op("act", lambda e: e.activation(out=Ttmp[tt][:, 0:n], in_=Ttmp[tt][:, 0:n], func=AF.Exp, scale=-1.0),
                                 reads=["Ttmp%d" % tt], writes=["Ttmp%d" % tt])
                            S.op("dve", lambda e: e.tensor_tensor(out=SG[:, c0:c1], in0=psum[:, b, 0:n], in1=Ttmp[tt][:, 0:n], op=ALU.mult),
                                 reads=[bk(b), "Ttmp%d" % tt], writes=["SG"])
                        elif kind == "q":
                            S.op("dve", lambda e: e.tensor_copy(out=QT[:, c0:c1], in_=psum[:, b, 0:n]), reads=[bk(b)], writes=["QT"])
                        else:
                            S.op("dve", lambda e: e.tensor_copy(out=KT[:, c0:c1], in_=psum[:, b, 0:n]), reads=[bk(b)], writes=["KT"])
                    return (xoff + c1, emit)

                def v_tile(tb0):
                    tbs = [tb for tb in (tb0, tb0 + 1) if tb < NKB_A]
                    nt = len(tbs)

                    def emit():
                        b = next_pbank()

                        def mmv(e):
                            ins = None
                            for i, tb in enumerate(tbs):
                                for kc in range(KC):
                                    ins = e.matmul(psum[:, b, i * 256:(i + 1) * 256], lhsT=xT_bf[:, kc, tb * 128:(tb + 1) * 128],
                                                   rhs=wv_bf[:, kc, :], start=(kc == 0), stop=(kc == KC - 1))
                            return ins
                        S.op("pe", mmv, reads=["wv"] + xk(tbs[0] * 128, (tbs[-1] + 1) * 128), writes=[bk(b)])
                        S.op("act", lambda e: e.activation(
                            out=V_sb[:, tb0:tb0 + nt, :], in_=psum[:, b, 0:nt * 256].rearrange("p (a c) -> p a c", a=nt), func=AF.Copy),
                            reads=[bk(b)], writes=["V"])
                    return ((tbs[-1] + 1) * 128, emit)

                for (c0, c1) in _tiles(NU, 512):
                    items.append(fm_tile("g", c0, c1))
                for (c0, c1) in _tiles(NU, 512):
                    items.append(fm_tile("q", c0, c1))
                for (c0, c1) in _tiles(NTA, 512):
                    items.append(fm_tile("k", c0, c1))
                if hp % 2 == 0:
                    for tb0 in range(0, NKB_A, 2):
                        items.append(v_tile(tb0))
                return items

            def projection_A(hp):
                for need, emit in projection_items(hp):
                    emit()

            nd_ctr = [0]

            def attention_A(hp):
                slot = hp % 2
                hpl = hp % 2
                ekey = "E%d" % slot

                def pv(ju):
                    i = nd_ctr[0] % 2
                    nd_ctr[0] += 1
                    nb, db = 4 + i, 6 + i

                    def mm(e, ju=ju, nb=nb, db=db):
                        ins = None
                        kbs = list(range(ju, ju + 5))
                        for idx, kb2 in enumerate(kbs):
                            col = (ju + 4 - kb2) * 128
                            st, sp_ = (idx == 0), (idx == len(kbs) - 1)
                            r = kb2 % RING_A
                            for h in range(2):
                                e.matmul(psum[h * 64:(h + 1) * 64, nb, 0:128],
                                         lhsT=V_sb[:, kb2, hpl * 128 + h * 64:hpl * 128 + (h + 1) * 64],
                                         rhs=PT[h][r][:, col:col + 128], start=st, stop=sp_)
                            for h in range(2):
                                ins = e.matmul(psum[h * 64:(h + 1) * 64, db, 0:128], lhsT=validA_sb[:, kb2, :],
                                               rhs=PT[h][r][:, col:col + 128], start=st, stop=sp_)
                        return ins
                    rd = ["V", "validA"] + ["PT%d_%d" % (h, kb2 % RING_A) for h in range(2) for kb2 in range(ju, ju + 5)]
                    S.op("pe", mm, reads=rd, writes=[bk(nb), bk(db)])
                    S.op("act", lambda e, i=i, db=db: e.activation(out=Rt[i][:, :], in_=psum[:, db, 0:128], func=AF.Ln, bias=eps60[:, 0:1]),
                         reads=[bk(db), "eps60"], writes=["Rt%d" % i])
                    S.op("act", lambda e, i=i: e.activation(out=Rt[i][:, :], in_=Rt[i][:, :], func=AF.Exp, scale=-1.0),
                         reads=["Rt%d" % i], writes=["Rt%d" % i])
                    S.op("dve", lambda e, i=i, nb=nb: e.tensor_tensor(out=Ot[i][:, :], in0=psum[:, nb, 0:128], in1=Rt[i][:, :], op=ALU.mult),
                         reads=[bk(nb), "Rt%d" % i], writes=["Ot%d" % i])
                    S.op("pool", lambda e, i=i, ju=ju: e.tensor_tensor(out=attnT[:, hp, ju * 128:(ju + 1) * 128], in0=Ot[i][:, :],
                                                                       in1=SG[:, ju * 128:(ju + 1) * 128], op=ALU.mult),
                         reads=["Ot%d" % i, "SG"], writes=["attnT"])

                for kb in range(NKB_A):
                    jmin, jmax = max(kb, 4), min(kb + 4, 20)
                    c0, c1 = (jmin - kb) * 128, (jmax - kb + 1) * 128
                    u0 = (jmin - 4) * 128
                    r = kb % RING_A
                    for h in range(2):
                        def mm(e, h=h, kb=kb, c0=c0, c1=c1, u0=u0):
                            ins = None
                            a = c0
                            while a < c1:
                                bnk = 2 * h + (a // 512)
                                bend = min(c1, (a // 512 + 1) * 512)
                                ins = e.matmul(psum[:, bnk, a % 512:a % 512 + (bend - a)],
                                               lhsT=KT[h * 64:(h + 1) * 64, kb * 128:(kb + 1) * 128],
                                               rhs=QT[h * 64:(h + 1) * 64, u0 + (a - c0):u0 + (bend - c0)], start=True, stop=True)
                                a = bend
                            return ins
                        S.op("pe", mm, reads=["QT", "KT"], writes=[bk(2 * h), bk(2 * h + 1)])
                        psS = psum[:, 2 * h:2 * h + 2, :].rearrange("p a b -> p (a b)")
                        S.op("act", lambda e, h=h, r=r, c0=c0, c1=c1, psS=psS: e.activation(
                            out=PT[h][r][:, c0:c1], in_=psS[:, c0:c1], func=AF.Exp, scale=0.125),
                            reads=[bk(2 * h), bk(2 * h + 1)], writes=["PT%d_%d" % (h, r)])
                        S.op("dve", lambda e, h=h, r=r, c0=c0, c1=c1, slot=slot: e.tensor_tensor(
                            out=PT[h][r][:, c0:c1], in0=PT[h][r][:, c0:c1], in1=E_sb[slot][:, h, c0:c1], op=ALU.mult),
                            reads=["PT%d_%d" % (h, r), ekey], writes=["PT%d_%d" % (h, r)])
                    ju = kb - 5
                    if ju >= 0:
                        pv(ju)
                for ju in range(NKB_A - 5, NQB_A):
                    pv(ju)

            prefetch_A(0)
            items0 = projection_items(0)

            prev_t1 = [0]

            def after_tile(t1):
                lim, prev_t1[0] = prev_t1[0], t1
                rest = []
                for need, emit in items0:
                    if need <= lim:
                        emit()
                    else:
                        rest.append((need, emit))
                items0[:] = rest
            preamble(after_tile)
            proj_nbanks[0] = 8
            after_tile(NTA)
            assert not items0
            prefetch_A(1)
            dbg_dump("xTn", lambda c0, c1: xT_bf[:].rearrange("p a b -> p (a b)")[:, c0:c1], KC * NTA, ALLX)
            for hp in range(8):
                if hp >= 1 and hp + 1 < 8:
                    prefetch_A(hp + 1)
                if hp >= 1:
                    projection_A(hp)
                if hp == 0:
                    dbg_dump("QT0", lambda c0, c1: QT[:, c0:c1], NU, ["QT"])
                    dbg_dump("KT0", lambda c0, c1: KT[:, c0:c1], NTA, ["KT"])
                    dbg_dump("SG0", lambda c0, c1: SG[:, c0:c1], NU, ["SG"])
                    dbg_dump("V0", lambda c0, c1: V_sb[:].rearrange("p a b -> p (a b)")[:, c0:c1], NKB_A * 256, ["V"])
                attention_A(hp)
            dbg_dump("attnA", lambda c0, c1: attnT[:].rearrange("p a b -> p (a b)")[:, c0:c1], KC * NU, ["attnT"])
            S.flush()

        with ExitStack() as es2:
            h1T = sbuf(es2, "h1T", [128, KC, NU], F32)
            h1n = sbuf(es2, "h1n", [128, KC, NU], BF16)
            kvw_bf = sbuf(es2, "kvw_bf", [128, KC, 384], BF16)

            def load_wout(es, wsrc, stgs, wout_bf):
                for i in range(4):
                    s = i % 2
                    S.op("sp", lambda e, s=s, i=i: e.dma_start(out=stgs[s][:, 0:2048], in_=wsrc[:, i * 2048:(i + 1) * 2048]),
                         writes=["wstg%d" % s], dma="wstg%d" % s)
                    S.op("act", lambda e, s=s, i=i: e.activation(
                        out=wout_bf[:, :, i * 256:(i + 1) * 256], in_=stgs[s][:, 0:2048].rearrange("p (kc c) -> p kc c", kc=KC), func=AF.Copy),
                        reads=["wstg%d" % s], writes=["wout%d" % i])

            def stats_a(src_ap_fn, src_key, n, sq):
                S.op("act", lambda e, n=n: e.activation(out=sq[:, :, 0:n], in_=src_ap_fn(), func=AF.Square),
                     reads=[src_key], writes=["sq"])

            def stats_b(n, sq, rt, rstd, bank_ctr):
                b = 4 + bank_ctr[0] % 4
                bank_ctr[0] += 1

                def stat_mm(e, b=b, n=n):
                    ins = None
                    for kc in range(KC):
                        ins = e.matmul(psum[:, b, 0:n], lhsT=ones_bf[:, :], rhs=sq[:, kc, 0:n],
                                       start=(kc == 0), stop=(kc == KC - 1))
                    return ins
                S.op("pe", stat_mm, reads=["sq", "ones"], writes=[bk(b)])
                S.op("act", lambda e, b=b, n=n: e.activation(out=rt[:, 0:n], in_=psum[:, b, 0:n], func=AF.Ln,
                                                             scale=1.0 / D, bias=eps_t[:, 0:1]),
                     reads=[bk(b), "eps"], writes=["rt"])
                S.op("act", lambda e, n=n: e.activation(out=rstd[:, 0:n], in_=rt[:, 0:n], func=AF.Exp, scale=-0.5),
                     reads=["rt"], writes=["rstd"])

            with ExitStack() as es2a:
                stg2 = [sbuf(es2a, "stg2_%d" % i, [128, KC * TILE], F32) for i in range(2)]
                wout_bf = sbuf(es2a, "woutA_bf", [128, KC, 1024], BF16)
                sq = sbuf(es2a, "sq2", [128, KC, TILE], BF16)
                rt = sbuf(es2a, "rt2", [128, TILE], F32)
                rstd = sbuf(es2a, "rstd2", [128, TILE], F32)
                load_wout(es2a, wa_out, stg2, wout_bf)
                kvstg = sbuf(es2a, "kvstg", [128, KC * 128], F32)

                def prefetch_kvw():
                    for part in range(3):
                        src = kvw[:].rearrange("p (kc c) -> p kc c", kc=KC)[:, :, part * 128:(part + 1) * 128]
                        S.op("sp", lambda e, src=src: e.dma_start(out=kvstg[:, :].rearrange("p (kc c) -> p kc c", kc=KC), in_=src),
                             writes=["kvstg"], dma="kvstg")

                        def castkv(e, part=part):
                            ins = None
                            sv = kvstg[:, :].rearrange("p (kc c) -> p kc c", kc=KC)
                            for kc in range(KC):
                                ins = e.activation(out=kvw_bf[:, kc, part * 128:(part + 1) * 128], in_=sv[:, kc, :], func=AF.Identity,
                                                   scale=gains_sb[:, KV_G + kc:KV_G + kc + 1])
                            return ins
                        S.op("act", castkv, reads=["kvstg", "gains"], writes=["kvw"])
                bctr = [0]
                wb = [0]
                pendingA = []
                for ti, (u0, u1) in enumerate(_tiles(NU, TILE)):
                    n = u1 - u0
                    s = ti % 2
                    sv = stg2[s][:, 0:KC * n].rearrange("p (kc t) -> p kc t", kc=KC)
                    S.op("sp", lambda e, sv=sv, u0=u0, u1=u1: e.dma_start(out=sv, in_=xT_v[:, :, 512 + u0:512 + u1]),
                         writes=["wstg%d" % s], dma="wstg%d" % s)
                    for oc in range(KC):
                        b = wb[0] % 4
                        wb[0] += 1

                        def mm(e, b=b, oc=oc, u0=u0, u1=u1, n=n):
                            ins = None
                            for kc in range(KC):
                                ins = e.matmul(psum[:, b, 0:n], lhsT=wout_bf[:, kc, oc * 128:(oc + 1) * 128],
                                               rhs=attnT[:, kc, u0:u1], start=(kc == 0), stop=(kc == KC - 1))
                            return ins
                        S.op("pe", mm, reads=["wout%d" % (oc // 2), "attnT"], writes=[bk(b)])
                        S.op("dve", lambda e, b=b, oc=oc, u0=u0, u1=u1, n=n, sv=sv: e.tensor_tensor(
                            out=h1T[:, oc, u0:u1], in0=psum[:, b, 0:n], in1=sv[:, oc, :], op=ALU.add),
                            reads=[bk(b), "wstg%d" % s], writes=["h1T_%d" % ti])
                    def finish(ti=ti, u0=u0, u1=u1, n=n):
                        stats_b(n, sq, rt, rstd, bctr)
                        S.op("dve", lambda e: e.tensor_tensor(
                            out=h1n[:, :, u0:u1], in0=h1T[:, :, u0:u1], in1=rstd[:, 0:n].unsqueeze(1).to_broadcast([128, KC, n]), op=ALU.mult),
                            reads=["h1T_%d" % ti, "rstd"], writes=["h1n"])
                    if ti == 3:
                        prefetch_kvw()
                    if pendingA:
                        pendingA.pop()()
                    stats_a(lambda u0=u0, u1=u1: h1T[:, :, u0:u1], "h1T_%d" % ti, n, sq)
                    pendingA.append(finish)
                pendingA.pop()()
                dbg_dump("h1T", lambda c0, c1: h1T[:].rearrange("p a b -> p (a b)")[:, c0:c1], KC * NU, ["h1n"])
                dbg_dump("h1n", lambda c0, c1: h1n[:].rearrange("p a b -> p (a b)")[:, c0:c1], KC * NU, ["h1n"])
                S.flush()

            with ExitStack() as es2b:
                stgB = [sbuf(es2b, "stgB%d" % i, [128, KC * 256], F32) for i in range(2)]
                wqg_bf = [sbuf(es2b, "wqg%d" % i, [128, KC, 256], BF16) for i in range(2)]
                KshT = [sbuf(es2b, "KshT%d" % g, [128, NU], BF16) for g in range(2)]
                Vsh = sbuf(es2b, "Vsh", [128, NKB_B, 128], BF16)
                QTb = sbuf(es2b, "QTb", [128, NW], BF16)
                SGb = sbuf(es2b, "SGb", [128, NW], BF16)
                PTb = [[sbuf(es2b, "PTb%d_%d" % (h, s), [128, 256], BF16) for s in range(RING_B)] for h in range(2)]
                bstgB = sbuf(es2b, "bstgB", [128, 512], F32)
                EB = [sbuf(es2b, "EB%d" % i, [128, 2, 256], BF16) for i in range(2)]
                TtmpB = [sbuf(es2b, "TtmpB%d" % i, [128, 512], F32) for i in range(2)]
                RtB = [sbuf(es2b, "RtB%d" % i, [128, 128], F32) for i in range(2)]
                OtB = [sbuf(es2b, "OtB%d" % i, [128, 128], F32) for i in range(2)]
                sctr = [0]

                def next_stgB():
                    s = sctr[0] % 2
                    sctr[0] += 1
                    return s

                def prefetch_B(hp):
                    s = next_stgB()
                    slot = hp % 2
                    S.op("sp", lambda e, s=s, hp=hp: e.dma_start(out=stgB[s][:, :], in_=wb_qg[hp]),
                         writes=["stgB%d" % s], dma="stgB%d" % s)

                    def cast(e, s=s, slot=slot):
                        ins = None
                        sv = stgB[s][:, :].rearrange("p (kc c) -> p kc c", kc=KC)
                        for kc in range(KC):
                            ins = e.activation(out=wqg_bf[slot][:, kc, :], in_=sv[:, kc, :], func=AF.Identity,
                                               scale=gains_sb[:, B_G + kc:B_G + kc + 1])
                        return ins
                    S.op("act", cast, reads=["stgB%d" % s, "gains"], writes=["wqg%d" % slot])
                    S.op("sp", lambda e, hp=hp: e.dma_start(out=bstgB[:, :], in_=biasB[hp]), writes=["bstgB"], dma="bstgB")
                    S.op("act", lambda e, slot=slot: e.activation(out=EB[slot][:].rearrange("p a b -> p (a b)"), in_=bstgB[:, :], func=AF.Exp),
                         reads=["bstgB"], writes=["EB%d" % slot])

                    def corners(e, slot=slot):
                        ins = None
                        for h in range(2):
                            e.memset(EB[slot][64:128, h, 0:64], 0.0)
                            ins = e.memset(EB[slot][0:64, h, 192:256], 0.0)
                        return ins
                    S.op("pool", corners, reads=[], writes=["EB%d" % slot])

                pbank = [0]

                def next_pb():
                    b = pbank[0]
                    pbank[0] = (b + 1) % 8
                    return b

                for g in range(2):
                    for (c0, c1) in _tiles(NU, 512):
                        b = next_pb()
                        n = c1 - c0

                        def mm(e, b=b, g=g, c0=c0, c1=c1, n=n):
                            ins = None
                            for kc in range(KC):
                                ins = e.matmul(psum[:, b, 0:n], lhsT=kvw_bf[:, kc, g * 128:(g + 1) * 128], rhs=h1n[:, kc, c0:c1],
                                               start=(kc == 0), stop=(kc == KC - 1))
                            return ins
                        S.op("pe", mm, reads=["kvw", "h1n"], writes=[bk(b)])
                        S.op("dve", lambda e, b=b, g=g, c0=c0, c1=c1, n=n: e.tensor_copy(out=KshT[g][:, c0:c1], in_=psum[:, b, 0:n]),
                             reads=[bk(b)], writes=["KshT%d" % g])
                for tb0 in range(0, NKB_B, 4):
                    tbs = [tb for tb in range(tb0, tb0 + 4) if tb < NKB_B]
                    b = next_pb()

                    def mmv(e, b=b, tbs=tbs):
                        ins = None
                        for i, tb in enumerate(tbs):
                            for kc in range(KC):
                                ins = e.matmul(psum[:, b, i * 128:(i + 1) * 128], lhsT=h1n[:, kc, tb * 128:(tb + 1) * 128],
                                               rhs=kvw_bf[:, kc, 256:384], start=(kc == 0), stop=(kc == KC - 1))
                        return ins
                    S.op("pe", mmv, reads=["kvw", "h1n"], writes=[bk(b)])
                    nt = len(tbs)
                    S.op("act", lambda e, b=b, tb0=tb0, nt=nt: e.activation(
                        out=Vsh[:, tb0:tb0 + nt, :], in_=psum[:, b, 0:nt * 128].rearrange("p (a c) -> p a c", a=nt), func=AF.Copy),
                        reads=[bk(b)], writes=["Vsh"])

                def projection_B(hp):
                    slot = hp % 2
                    wkey = "wqg%d" % slot
                    ti = 0
                    for (c0, c1) in _tiles(NW, 512):
                        b = next_pb()
                        n = c1 - c0

                        def mm(e, b=b, c0=c0, c1=c1, n=n):
                            ins = None
                            for kc in range(KC):
                                ins = e.matmul(psum[:, b, 0:n], lhsT=wqg_bf[slot][:, kc, 128:256], rhs=h1n[:, kc, 128 + c0:128 + c1],
                                               start=(kc == 0), stop=(kc == KC - 1))
                            return ins
                        S.op("pe", mm, reads=[wkey, "h1n"], writes=[bk(b)])
                        tt = ti % 2
                        ti += 1
                        S.op("act", lambda e, b=b, n=n, tt=tt: e.activation(out=TtmpB[tt][:, 0:n], in_=psum[:, b, 0:n], func=AF.Exp, scale=-1.0),
                             reads=[bk(b)], writes=["TtmpB%d" % tt])
                        S.op("act", lambda e, n=n, tt=tt: e.activation(out=TtmpB[tt][:, 0:n], in_=TtmpB[tt][:, 0:n], func=AF.Ln, bias=one_t[:, 0:1]),
                             reads=["TtmpB%d" % tt, "one"], writes=["TtmpB%d" % tt])
                        S.op("act", lambda e, n=n, tt=tt: e.activation(out=TtmpB[tt][:, 0:n], in_=TtmpB[tt][:, 0:n], func=AF.Exp, scale=-1.0),
                             reads=["TtmpB%d" % tt], writes=["TtmpB%d" % tt])
                        S.op("dve", lambda e, b=b, c0=c0, c1=c1, n=n, tt=tt: e.tensor_tensor(
                            out=SGb[:, c0:c1], in0=psum[:, b, 0:n], in1=TtmpB[tt][:, 0:n], op=ALU.mult),
                            reads=[bk(b), "TtmpB%d" % tt], writes=["SGb"])
                    for (c0, c1) in _tiles(NW, 512):
                        b = next_pb()
                        n = c1 - c0

                        def mm(e, b=b, c0=c0, c1=c1, n=n):
                            ins = None
                            for kc in range(KC):
                                ins = e.matmul(psum[:, b, 0:n], lhsT=wqg_bf[slot][:, kc, 0:128], rhs=h1n[:, kc, 128 + c0:128 + c1],
                                               start=(kc == 0), stop=(kc == KC - 1))
                            return ins
                        S.op("pe", mm, reads=[wkey, "h1n"], writes=[bk(b)])
                        S.op("dve", lambda e, b=b, c0=c0, c1=c1, n=n: e.tensor_copy(out=QTb[:, c0:c1], in_=psum[:, b, 0:n]),
                             reads=[bk(b)], writes=["QTb"])

                ndb = [0]

                def attention_B(hp):
                    slot = hp % 2
                    g = hp // 4
                    ekey = "EB%d" % slot

                    def pv(jw):
                        i = ndb[0] % 2
                        ndb[0] += 1
                        nb, db = 4 + i, 6 + i

                        def mm(e, jw=jw, nb=nb, db=db):
                            ins = None
                            kbs = [jw, jw + 1]
                            for idx, kb2 in enumerate(kbs):
                                col = (jw + 1 - kb2) * 128
                                st, sp_ = (idx == 0), (idx == len(kbs) - 1)
                                r = kb2 % RING_B
                                for h in range(2):
                                    e.matmul(psum[h * 64:(h + 1) * 64, nb, 0:128], lhsT=Vsh[:, kb2, g * 64:(g + 1) * 64],
                                             rhs=PTb[h][r][:, col:col + 128], start=st, stop=sp_)
                                for h in range(2):
                                    ins = e.matmul(psum[h * 64:(h + 1) * 64, db, 0:128], lhsT=validB_sb[:, kb2, :],
                                                   rhs=PTb[h][r][:, col:col + 128], start=st, stop=sp_)
                            return ins
                        rd = ["Vsh", "validB"] + ["PTb%d_%d" % (h, kb2 % RING_B) for h in range(2) for kb2 in (jw, jw + 1)]
                        S.op("pe", mm, reads=rd, writes=[bk(nb), bk(db)])
                        S.op("act", lambda e, i=i, db=db: e.activation(out=RtB[i][:, :], in_=psum[:, db, 0:128], func=AF.Ln, bias=esink[:, hp:hp + 1]),
                             reads=[bk(db), "esink"], writes=["RtB%d" % i])
                        S.op("act", lambda e, i=i: e.activation(out=RtB[i][:, :], in_=RtB[i][:, :], func=AF.Exp, scale=-1.0),
                             reads=["RtB%d" % i], writes=["RtB%d" % i])
                        S.op("dve", lambda e, i=i, nb=nb: e.tensor_tensor(out=OtB[i][:, :], in0=psum[:, nb, 0:128], in1=RtB[i][:, :], op=ALU.mult),
                             reads=[bk(nb), "RtB%d" % i], writes=["OtB%d" % i])
                        S.op("pool", lambda e, i=i, jw=jw: e.tensor_tensor(out=attnT[:, hp, jw * 128:(jw + 1) * 128], in0=OtB[i][:, :],
                                                                           in1=SGb[:, jw * 128:(jw + 1) * 128], op=ALU.mult),
                             reads=["OtB%d" % i, "SGb"], writes=["attnT"])

                    for kb in range(NKB_B):
                        jmin, jmax = max(kb - 1, 0), min(kb, NQB_B - 1)
                        c0, c1 = (jmin + 1 - kb) * 128, (jmax + 1 - kb + 1) * 128
                        w0 = jmin * 128
                        r = kb % RING_B
                        for h in range(2):
                            sbk = 2 * h + kb % 2
                            S.op("pe", lambda e, h=h, kb=kb, c0=c0, c1=c1, w0=w0, sbk=sbk: e.matmul(
                                psum[:, sbk, c0:c1], lhsT=KshT[g][h * 64:(h + 1) * 64, kb * 128:(kb + 1) * 128],
                                rhs=QTb[h * 64:(h + 1) * 64, w0:w0 + (c1 - c0)], start=True, stop=True),
                                reads=["QTb", "KshT%d" % g], writes=[bk(sbk)])
                            S.op("act", lambda e, h=h, r=r, c0=c0, c1=c1, sbk=sbk: e.activation(
                                out=PTb[h][r][:, c0:c1], in_=psum[:, sbk, c0:c1], func=AF.Exp, scale=0.125),
                                reads=[bk(sbk)], writes=["PTb%d_%d" % (h, r)])
                            S.op("dve", lambda e, h=h, r=r, c0=c0, c1=c1: e.tensor_tensor(
                                out=PTb[h][r][:, c0:c1], in0=PTb[h][r][:, c0:c1], in1=EB[slot][:, h, c0:c1], op=ALU.mult),
                                reads=["PTb%d_%d" % (h, r), ekey], writes=["PTb%d_%d" % (h, r)])
                        jw = kb - 2
                        if jw >= 0:
                            pv(jw)
                    for jw in range(NKB_B - 2, NQB_B):
                        pv(jw)

                prefetch_B(0)
                for hp in range(8):
                    if hp + 1 < 8:
                        prefetch_B(hp + 1)
                    projection_B(hp)
                    attention_B(hp)
                dbg_dump("attnB", lambda c0, c1: attnT[:, c0 // NW, c0 % NW:c0 % NW + (c1 - c0)], KC * NW, ["attnT"])
                S.flush()

            with ExitStack() as es2c:
                TB = 256
                wstg = [sbuf(es2c, "wstgC%d" % i, [128, 2048], F32) for i in range(2)]
                wout_bf = sbuf(es2c, "woutB_bf", [128, KC, 1024], BF16)
                h2 = [sbuf(es2c, "h2_%d" % i, [128, KC, TB], F32) for i in range(3)]
                sq = sbuf(es2c, "sq3", [128, KC, TB], BF16)
                rt = sbuf(es2c, "rt3", [128, TB], F32)
                rstd = sbuf(es2c, "rstd3", [128, TB], F32)
                load_wout(es2c, wb_out, wstg, wout_bf)
                bctr = [0]
                wb = [0]
                pendingB = []
                for ti, (w0, w1) in enumerate(_tiles(NW, TB)):
                    n = w1 - w0
                    s = ti % 3
                    for oc in range(KC):
                        b = wb[0] % 4
                        wb[0] += 1

                        def mm(e, b=b, oc=oc, w0=w0, w1=w1, n=n):
                            ins = None
                            for kc in range(KC):
                                ins = e.matmul(psum[:, b, 0:n], lhsT=wout_bf[:, kc, oc * 128:(oc + 1) * 128],
                                               rhs=attnT[:, kc, w0:w1], start=(kc == 0), stop=(kc == KC - 1))
                            return ins
                        S.op("pe", mm, reads=["wout%d" % (oc // 2), "attnT"], writes=[bk(b)])
                        S.op("dve", lambda e, b=b, oc=oc, w0=w0, w1=w1, n=n, s=s: e.tensor_tensor(
                            out=h2[s][:, oc, 0:n], in0=psum[:, b, 0:n], in1=h1T[:, oc, 128 + w0:128 + w1], op=ALU.add),
                            reads=[bk(b)], writes=["h2_%d" % s])
                    def finishB(s=s, w0=w0, w1=w1, n=n):
                        stats_b(n, sq, rt, rstd, bctr)

                        def fin(e):
                            ins = None
                            for oc in range(KC):
                                ins = e.scalar_tensor_tensor(out=h2[s][:, oc, 0:n], in0=h2[s][:, oc, 0:n],
                                                             scalar=gains_sb[:, F_G + oc:F_G + oc + 1], in1=rstd[:, 0:n],
                                                             op0=ALU.mult, op1=ALU.mult)
                            return ins
                        S.op("dve", fin, reads=["h2_%d" % s, "rstd", "gains"], writes=["h2_%d" % s])
                        tok = S.op("sp", lambda e: e.dma_start(out=outT_v[:, :, w0:w1], in_=h2[s][:, :, 0:n]),
                                   reads=["h2_%d" % s], dma="out%d" % s)
                        out_tokens.append(tok)
                    if pendingB:
                        pendingB.pop()()
                    stats_a(lambda s=s, n=n: h2[s][:, :, 0:n], "h2_%d" % s, n, sq)
                    pendingB.append(finishB)
                pendingB.pop()()
                final = {}
                for key, val in out_tokens:
                    final[key] = max(final.get(key, 0), val)
                S.wait_tokens("sp", list(final.items()))
                S.flush()
    return nc


def _t5_bucket_np(rel):
    nb = 16
    max_exact = 8
    ret = np.where(rel > 0, nb, 0)
    n = np.abs(rel)
    nf = np.maximum(n, 1).astype(np.float32)
    large = max_exact + (np.log(nf / np.float32(max_exact)) / np.float32(math.log(128 / max_exact))
                         * np.float32(nb - max_exact)).astype(np.int32)
    large = np.minimum(large, nb - 1)
    return ret + np.where(n < max_exact, n, large)


def _prep_shared(a_norm, a_w_in, a_rel_bias, a_w_out, kv_norm, kv_w, t5_bias, b_norm, b_w_in, b_sinks, b_w_out, final_norm):
    f = np.float32
    w_in = np.asarray(a_w_in[0], f)
    w4 = w_in.reshape(KC, 128, 4, 8, 128)
    wa_qkg = np.ascontiguousarray(np.transpose(w4[:, :, [0, 1, 3]], (3, 1, 0, 2, 4))).reshape(8, 128, KC * 384)
    wv = w_in[:, 2048:3072].reshape(KC, 128, 4, 256)
    wa_v = np.ascontiguousarray(np.transpose(wv, (2, 1, 0, 3))).reshape(4, 128, KC * 256)
    wa_out = np.ascontiguousarray(np.transpose(np.asarray(a_w_out[0], f).reshape(KC, 128, 4, 256), (1, 2, 0, 3))).reshape(128, KC * 1024)
    wb_out = np.ascontiguousarray(np.transpose(np.asarray(b_w_out[0], f).reshape(KC, 128, 4, 256), (1, 2, 0, 3))).reshape(128, KC * 1024)

    def gcol(v):
        return np.asarray(v, f).reshape(KC, 128).T
    gains = np.ascontiguousarray(np.concatenate([gcol(a_norm[0]), gcol(kv_norm), gcol(b_norm[0]), gcol(final_norm)], axis=1))
    k = np.arange(128)[:, None]
    q = np.arange(640)[None, :]
    idxA = np.clip(q - k, -256, 256) + 256
    rb = np.asarray(a_rel_bias[0], f)
    bA = rb[idxA]
    biasA = np.ascontiguousarray(np.transpose(bA.reshape(128, 640, 8, 2), (2, 0, 3, 1))).reshape(8, 128, 1280)
    kvw_ = np.asarray(kv_w, f).reshape(KC, 128, 256)
    kcat = np.concatenate([kvw_[:, :, 0:64], kvw_[:, :, 0:64], kvw_[:, :, 64:128], kvw_[:, :, 64:128], kvw_[:, :, 128:256]], axis=2)
    kvw = np.ascontiguousarray(np.transpose(kcat, (1, 0, 2))).reshape(128, KC * 384)
    wb = np.asarray(b_w_in[0], f).reshape(KC, 128, 2, 8, 128)
    wb_qg = np.ascontiguousarray(np.transpose(wb, (3, 1, 0, 2, 4))).reshape(8, 128, KC * 256)
    qb = np.arange(256)[None, :]
    bucket = _t5_bucket_np((k - qb).astype(np.int32))
    tb = np.asarray(t5_bias, f)[bucket]
    biasB = np.ascontiguousarray(np.transpose(tb.reshape(128, 256, 8, 2), (2, 0, 3, 1))).reshape(8, 128, 512)
    sk = np.asarray(b_sinks[0], f).reshape(8, 2)
    sinkB = np.ascontiguousarray(np.repeat(sk.T, 64, axis=0))
    return dict(wa_qkg=wa_qkg, wa_v=wa_v, wa_out=wa_out, gains=gains, biasA=biasA, kvw=kvw, wb_qg=wb_qg,
                wb_out=wb_out, biasB=biasB, sinkB=sinkB)


def _prep_core(x, c):
    b, half = c // 2, c % 2
    T0 = half * NOWN
    lo = T0 - HALO
    xe = np.zeros((NTA, D), np.float32)
    src0 = max(lo, 0)
    xe[src0 - lo:, :] = x[b, src0:T0 + NOWN, :]
    xTc = np.ascontiguousarray(xe.T)
    tpos = lo + np.arange(NTA)
    vA = (tpos >= 0).astype(np.float32).reshape(NKB_A, 128).T
    validA = np.ascontiguousarray(np.repeat(vA[:, :, None], 64, axis=2)).reshape(128, NKB_A * 64).astype(ml_dtypes.bfloat16)
    upos = T0 - 128 + np.arange(NU)
    vB = (upos >= 0).astype(np.float32).reshape(NKB_B, 128).T
    validB = np.ascontiguousarray(np.repeat(vB[:, :, None], 64, axis=2)).reshape(128, NKB_B * 64).astype(ml_dtypes.bfloat16)
    return dict(xT=xTc, validA=validA, validB=validB)


_NC_CACHE = {}


def kernel(x, a_norm, a_w_in, a_rel_bias, a_w_out, kv_norm, kv_w, t5_bias, b_norm, b_w_in, b_sinks, b_w_out, final_norm,
           _debug=()):
    x = np.asarray(x, np.float32)
    shared = _prep_shared(np.asarray(a_norm), np.asarray(a_w_in), np.asarray(a_rel_bias), np.asarray(a_w_out),
                          np.asarray(kv_norm), np.asarray(kv_w), np.asarray(t5_bias), np.asarray(b_norm),
                          np.asarray(b_w_in), np.asarray(b_sinks), np.asarray(b_w_out), np.asarray(final_norm))
    in_maps = []
    for c in range(N_CORES):
        m = dict(shared)
        m.update(_prep_core(x, c))
        in_maps.append(m)
    key = tuple(_debug)
    if key not in _NC_CACHE:
        _NC_CACHE[key] = build_nc(debug=key)
    nc = _NC_CACHE[key]
    res = run_bass_kernel_spmd(nc, in_maps, core_ids=list(range(N_CORES)))
    out = np.empty((4, SEQ, D), np.float32)
    for c in range(N_CORES):
        b, half = c // 2, c % 2
        out[b, half * NOWN:(half + 1) * NOWN, :] = np.asarray(res.results[c]["outT"]).T
    if _debug:
        return out, res.results
    return out
```

```python
import math
from contextlib import ExitStack

import numpy as np
import ml_dtypes

import concourse.bass as bass
import concourse.mybir as mybir
from concourse.bass_utils import run_bass_kernel_spmd

F32 = mybir.dt.float32
BF16 = mybir.dt.bfloat16
AF = mybir.ActivationFunctionType
ALU = mybir.AluOpType

N_CORES = 8
D = 1024
KC = 8
SEQ = 4096
NOWN = 2048
HALO = 640
NTA = NOWN + HALO
NU = NOWN + 128
NW = NOWN
NKB_A = NTA // 128
NQB_A = NU // 128
NKB_B = NU // 128
NQB_B = NW // 128
RING_A = 6
RING_B = 4
TILE = 384
EPS = 1e-6

ENGS = ("pe", "act", "dve", "pool", "sp")


class Sched:
    def __init__(self, nc, es):
        self.nc = nc
        self.es = es
        self.sem = {e: es.enter_context(nc.semaphore("s_" + e)) for e in ENGS}
        self.cnt = {e: 0 for e in ENGS}
        self.dsem = {}
        self.dcnt = {}
        self.lastw = {}
        self.readers = {}
        self.pending = {e: [] for e in ENGS}
        self.seen = {e: {} for e in ENGS}
        self.know = {}
        self.order = {}
        self.nwaits = 0

    def op(self, eng, fn, reads=(), writes=(), dma=None, ndma=1):
        deps = {}

        def add(tok):
            if tok is None:
                return
            key, val = tok
            if deps.get(key, 0) < val:
                deps[key] = val

        for r in reads:
            add(self.lastw.get(r))
            if r.startswith("bk") and eng in ("act", "dve"):
                for k, v in self.readers.get(r, {}).items():
                    if k[0] in ("act", "dve") and k[0] != eng:
                        add((k, v))
        for w in writes:
            add(self.lastw.get(w))
            for k, v in self.readers.get(w, {}).items():
                add((k, v))
        if dma is not None:
            if dma not in self.dsem:
                self.dsem[dma] = self.es.enter_context(self.nc.semaphore("d_" + str(dma)))
                self.dcnt[dma] = 0
            self.dcnt[dma] += 16 * ndma
            tok = (("dma", dma), self.dcnt[dma])
        else:
            self.cnt[eng] += 1
            tok = ((eng,), self.cnt[eng])
        waits = []
        for key, val in sorted(deps.items(), key=lambda kv: -self.order.get(kv, 0)):
            if key == ("pe",) and eng == "pe":
                continue
            if self.seen[eng].get(key, 0) >= val:
                continue
            waits.append((key, val))
            for k2, v2 in self.know.get((key, val), {}).items():
                if self.seen[eng].get(k2, 0) < v2:
                    self.seen[eng][k2] = v2
            self.seen[eng][key] = val
        self.nwaits += len(waits)
        self.order[tok] = len(self.order) + 1
        if dma is None:
            kn = dict(self.seen[eng])
            kn[tok[0]] = tok[1]
            self.know[tok] = kn
        else:
            self.know[tok] = dict(self.seen[eng])
        self.pending[eng].append((fn, waits, tok))
        for r in reads:
            d = self.readers.setdefault(r, {})
            if d.get(tok[0], 0) < tok[1]:
                d[tok[0]] = tok[1]
        for w in writes:
            self.lastw[w] = tok
            self.readers[w] = {}
        return tok

    def _semof(self, key):
        if key[0] == "dma":
            return self.dsem[key[1]]
        return self.sem[key[0]]

    def wait_tokens(self, eng, toks):
        self.pending[eng].append((None, list(toks), None))

    def flush(self):
        nc = self.nc
        pend = self.pending
        self.pending = {e: [] for e in ENGS}
        with nc.Block() as block:
            def mk(engname):
                lst = pend[engname]

                def body(e):
                    class _First:
                        def __init__(self, eng):
                            self._eng = eng
                            self.first = None

                        def __getattr__(self, name):
                            real = getattr(self._eng, name)

                            def call(*a, **k):
                                r = real(*a, **k)
                                if self.first is None:
                                    self.first = r
                                return r
                            return call

                    for fn, waits, tok in lst:
                        fuse = fn is not None and len(waits) >= 1
                        for key, val in (waits[:-1] if fuse else waits):
                            e.wait_ge(self._semof(key), val)
                        if fn is None:
                            continue
                        if fuse:
                            px = _First(e)
                            ins = fn(px)
                            px.first._wait_ge(self._semof(waits[-1][0]), waits[-1][1])
                        else:
                            ins = fn(e)
                        key, val = tok
                        if key[0] == "dma":
                            if not isinstance(ins, (list, tuple)):
                                ins = [ins]
                            for i in ins:
                                i.then_inc(self.dsem[key[1]], 16)
                        else:
                            ins.then_inc(self.sem[engname], 1)
                return body

            if pend["pe"]:
                block.tensor(mk("pe"))
            if pend["act"]:
                block.scalar(mk("act"))
            if pend["dve"]:
                block.vector(mk("dve"))
            if pend["pool"]:
                block.gpsimd(mk("pool"))
            if pend["sp"]:
                block.sync(mk("sp"))


def _tiles(n, step):
    return [(a, min(a + step, n)) for a in range(0, n, step)]


def build_nc(debug=()):
    nc = bass.Bass("TRN2", target_bir_lowering=False)

    def din(name, shape, dt=F32):
        return nc.dram_tensor(name, shape, dt, kind="ExternalInput").ap()

    xT = din("xT", [D, NTA])
    wa_qkg = din("wa_qkg", [8, 128, KC * 384])
    wa_v = din("wa_v", [4, 128, KC * 256])
    wa_out = din("wa_out", [128, KC * 1024])
    gains = din("gains", [128, 32])
    biasA = din("biasA", [8, 128, 1280])
    validA = din("validA", [128, NKB_A * 64], BF16)
    kvw = din("kvw", [128, KC * 384])
    wb_qg = din("wb_qg", [8, 128, KC * 256])
    wb_out = din("wb_out", [128, KC * 1024])
    biasB = din("biasB", [8, 128, 512])
    sinkB = din("sinkB", [128, 8])
    validB = din("validB", [128, NKB_B * 64], BF16)
    outT = nc.dram_tensor("outT", [D, NW], F32, kind="ExternalOutput").ap()
    dbg_out = {}
    dbg_shapes = {"xTn": [128, KC * NTA], "attnA": [128, KC * NU], "h1T": [128, KC * NU],
                  "h1n": [128, KC * NU], "attnB": [128, KC * NW], "QT0": [128, NU], "KT0": [128, NTA],
                  "SG0": [128, NU], "V0": [128, NKB_A * 256]}
    for name in debug:
        dbg_out[name] = nc.dram_tensor("dbg_" + name, dbg_shapes[name], F32, kind="ExternalOutput").ap()

    xT_v = xT.rearrange("(kc p) t -> p kc t", p=128)
    outT_v = outT.rearrange("(kc p) t -> p kc t", p=128)

    with ExitStack() as es0:
        S = Sched(nc, es0)
        out_tokens = []

        def sbuf(es, name, shape, dt):
            return es.enter_context(nc.sbuf_tensor(name, shape, dt))

        psum = es0.enter_context(nc.psum_tensor("psum", [128, 8, 512], F32))

        def bk(b):
            return "bk%d" % b

        ones_bf = sbuf(es0, "ones_bf", [128, 128], BF16)
        eps_t = sbuf(es0, "eps_t", [128, 1], F32)
        one_t = sbuf(es0, "one_t", [128, 1], F32)
        eps60 = sbuf(es0, "eps60", [128, 1], F32)
        gains_sb = sbuf(es0, "gains_sb", [128, 32], F32)
        sink_sb = sbuf(es0, "sink_sb", [128, 8], F32)
        esink = sbuf(es0, "esink", [128, 8], F32)
        validB_sb = sbuf(es0, "validB_sb", [128, NKB_B, 64], BF16)
        attnT = sbuf(es0, "attnT", [128, KC, NU], BF16)
        dbgf = sbuf(es0, "dbgf", [128, 512], F32) if debug else None

        def dbg_dump(name, src_fn, ncols, key_reads):
            if name not in dbg_out:
                return
            for (c0, c1) in _tiles(ncols, 512):
                S.op("dve", lambda e, c0=c0, c1=c1: e.tensor_copy(out=dbgf[:, 0:c1 - c0], in_=src_fn(c0, c1)),
                     reads=key_reads, writes=["dbgf"])
                tok = S.op("sp", lambda e, c0=c0, c1=c1: e.dma_start(out=dbg_out[name][:, c0:c1], in_=dbgf[:, 0:c1 - c0]),
                           reads=["dbgf"], dma="dbg")
                out_tokens.append(tok)

        S.op("pool", lambda e: e.memset(ones_bf[:], 1.0), writes=["ones"])
        S.op("pool", lambda e: e.memset(eps_t[:], EPS), writes=["eps"])
        S.op("pool", lambda e: e.memset(one_t[:], 1.0), writes=["one"])
        S.op("pool", lambda e: e.memset(eps60[:], 2.0 ** -60), writes=["eps60"])
        S.op("sp", lambda e: e.dma_start(out=gains_sb[:], in_=gains[:]), writes=["gains"], dma="c0")
        S.op("sp", lambda e: e.dma_start(out=sink_sb[:], in_=sinkB[:]), writes=["sink"], dma="c1")
        S.op("sp", lambda e: e.dma_start(out=validB_sb[:].rearrange("p a b -> p (a b)"), in_=validB[:]),
             writes=["validB"], dma="c2")
        S.op("act", lambda e: e.activation(out=esink[:], in_=sink_sb[:], func=AF.Exp), reads=["sink"], writes=["esink"])

        A_G, KV_G, B_G, F_G = 0, 8, 16, 24

        with ExitStack() as es1:
            xT_bf = sbuf(es1, "xT_bf", [128, KC, NTA], BF16)
            validA_sb = sbuf(es1, "validA_sb", [128, NKB_A, 64], BF16)
            stg = [sbuf(es1, "stg%d" % i, [128, KC * TILE], F32) for i in range(2)]
            sq = sbuf(es1, "sq", [128, KC, TILE], BF16)
            rt = sbuf(es1, "rt", [128, TILE], F32)
            rstd = sbuf(es1, "rstd", [128, TILE], F32)
            bstg = sbuf(es1, "bstg", [128, 1280], F32)
            E_sb = [sbuf(es1, "E%d" % i, [128, 2, 640], BF16) for i in range(2)]
            wqkg_bf = [sbuf(es1, "wqkg%d" % i, [128, KC, 384], BF16) for i in range(2)]
            wv_bf = sbuf(es1, "wv_bf", [128, KC, 256], BF16)
            QT = sbuf(es1, "QT", [128, NU], BF16)
            KT = sbuf(es1, "KT", [128, NTA], BF16)
            SG = sbuf(es1, "SG", [128, NU], BF16)
            V_sb = sbuf(es1, "V_sb", [128, NKB_A, 256], BF16)
            PT = [[sbuf(es1, "PT%d_%d" % (h, s), [128, 640], BF16) for s in range(RING_A)] for h in range(2)]
            Ttmp = [sbuf(es1, "Ttmp%d" % i, [128, 512], F32) for i in range(2)]
            Rt = [sbuf(es1, "Rt%d" % i, [128, 128], F32) for i in range(2)]
            Ot = [sbuf(es1, "Ot%d" % i, [128, 128], F32) for i in range(2)]

            S.op("sp", lambda e: e.dma_start(out=validA_sb[:].rearrange("p a b -> p (a b)"), in_=validA[:]),
                 writes=["validA"], dma="c3")

            stg_ctr = [0]

            def next_stg():
                s = stg_ctr[0] % 2
                stg_ctr[0] += 1
                return s

            XT_TILES = _tiles(NTA, TILE)

            def xk(a, b_):
                return ["xT_%d" % i for i, (t0, t1) in enumerate(XT_TILES) if t0 < b_ and t1 > a]
            ALLX = xk(0, NTA)

            stat_bank = [6]

            def preamble(after_tile=None):
              for ti_, (t0, t1) in enumerate(XT_TILES):
                n = t1 - t0
                s = next_stg()
                sv = stg[s][:, 0:KC * n].rearrange("p (kc t) -> p kc t", kc=KC)
                S.op("sp", lambda e, sv=sv, t0=t0, t1=t1: e.dma_start(out=sv, in_=xT_v[:, :, t0:t1]),
                     writes=["stg%d" % s], dma="stg%d" % s)
                S.op("act", lambda e, sv=sv, n=n: e.activation(out=sq[:, :, 0:n], in_=sv, func=AF.Square),
                     reads=["stg%d" % s], writes=["sq"])
                b = stat_bank[0]
                stat_bank[0] = 13 - stat_bank[0]

                def stat_mm(e, b=b, n=n):
                    ins = None
                    for kc in range(KC):
                        ins = e.matmul(psum[:, b, 0:n], lhsT=ones_bf[:, :], rhs=sq[:, kc, 0:n],
                                       start=(kc == 0), stop=(kc == KC - 1))
                    return ins
                S.op("pe", stat_mm, reads=["sq", "ones"], writes=[bk(b)])
                S.op("act", lambda e, b=b, n=n: e.activation(out=rt[:, 0:n], in_=psum[:, b, 0:n], func=AF.Ln,
                                                             scale=1.0 / D, bias=eps_t[:, 0:1]),
                     reads=[bk(b), "eps"], writes=["rt"])
                S.op("act", lambda e, n=n: e.activation(out=rstd[:, 0:n], in_=rt[:, 0:n], func=AF.Exp, scale=-0.5),
                     reads=["rt"], writes=["rstd"])
                S.op("dve", lambda e, sv=sv, t0=t0, t1=t1, n=n: e.tensor_tensor(
                    out=xT_bf[:, :, t0:t1], in0=sv, in1=rstd[:, 0:n].unsqueeze(1).to_broadcast([128, KC, n]), op=ALU.mult),
                    reads=["stg%d" % s, "rstd"], writes=["xT_%d" % ti_])
                if after_tile is not None:
                    after_tile(t1)

            def prefetch_A(hp):
                s = next_stg()
                slot = hp % 2
                S.op("sp", lambda e, s=s, hp=hp: e.dma_start(out=stg[s][:, 0:KC * 384], in_=wa_qkg[hp]),
                     writes=["stg%d" % s], dma="stg%d" % s)

                def cast(e, s=s, slot=slot):
                    ins = None
                    sv = stg[s][:, 0:KC * 384].rearrange("p (kc c) -> p kc c", kc=KC)
                    for kc in range(KC):
                        ins = e.activation(out=wqkg_bf[slot][:, kc, :], in_=sv[:, kc, :], func=AF.Identity,
                                           scale=gains_sb[:, A_G + kc:A_G + kc + 1])
                    return ins
                S.op("act", cast, reads=["stg%d" % s, "gains"], writes=["wqkg%d" % slot])
                if hp % 2 == 0:
                    s2 = next_stg()
                    S.op("sp", lambda e, s2=s2, hp=hp: e.dma_start(out=stg[s2][:, 0:KC * 256], in_=wa_v[hp // 2]),
                         writes=["stg%d" % s2], dma="stg%d" % s2)

                    def castv(e, s2=s2):
                        ins = None
                        sv = stg[s2][:, 0:KC * 256].rearrange("p (kc c) -> p kc c", kc=KC)
                        for kc in range(KC):
                            ins = e.activation(out=wv_bf[:, kc, :], in_=sv[:, kc, :], func=AF.Identity,
                                               scale=gains_sb[:, A_G + kc:A_G + kc + 1])
                        return ins
                    S.op("act", castv, reads=["stg%d" % s2, "gains"], writes=["wv"])
                S.op("sp", lambda e, hp=hp: e.dma_start(out=bstg[:, :], in_=biasA[hp]), writes=["bstg"], dma="bstg")
                S.op("act", lambda e, slot=slot: e.activation(out=E_sb[slot][:].rearrange("p a b -> p (a b)"), in_=bstg[:, :], func=AF.Exp),
                     reads=["bstg"], writes=["E%d" % slot])

                def corners(e, slot=slot):
                    ins = None
                    for h in range(2):
                        e.memset(E_sb[slot][64:128, h, 0:64], 0.0)
                        ins = e.memset(E_sb[slot][0:64, h, 576:640], 0.0)
                    return ins
                S.op("pool", corners, reads=[], writes=["E%d" % slot])

            proj_bank = [0]
            proj_nbanks = [6]

            def next_pbank():
                b = proj_bank[0] % proj_nbanks[0]
                proj_bank[0] = (b + 1) % proj_nbanks[0]
                return b

            def proj_fm(wslot_ap_fn, rhs_fn, ncols_tiles, evac):
                for (c0, c1) in ncols_tiles:
                    b = next_pbank()
                    n = c1 - c0

                    def mm(e, b=b, c0=c0, c1=c1, n=n):
                        ins = None
                        for kc in range(KC):
                            ins = e.matmul(psum[:, b, 0:n], lhsT=wslot_ap_fn(kc), rhs=rhs_fn(kc, c0, c1),
                                           start=(kc == 0), stop=(kc == KC - 1))
                        return ins
                    yield b, c0, c1, n, mm

            def projection_items(hp):
                slot = hp % 2
                wkey = "wqkg%d" % slot
                items = []
                tcount = [0]

                def fm_tile(kind, c0, c1):
                    n = c1 - c0
                    wcol = {"g": 256, "q": 0, "k": 128}[kind]
                    xoff = 0 if kind == "k" else 512

                    def emit():
                        b = next_pbank()

                        def mm(e):
                            ins = None
                            for kc in range(KC):
                                ins = e.matmul(psum[:, b, 0:n], lhsT=wqkg_bf[slot][:, kc, wcol:wcol + 128],
                                               rhs=xT_bf[:, kc, xoff + c0:xoff + c1], start=(kc == 0), stop=(kc == KC - 1))
                            return ins
                        S.op("pe", mm, reads=[wkey] + xk(xoff + c0, xoff + c1), writes=[bk(b)])
                        if kind == "g":
                            tt = tcount[0] % 2
                            tcount[0] += 1
                            S.op("act", lambda e: e.activation(out=Ttmp[tt][:, 0:n], in_=psum[:, b, 0:n], func=AF.Exp, scale=-1.0),
                                 reads=[bk(b)], writes=["Ttmp%d" % tt])
                            S.op("act", lambda e: e.activation(out=Ttmp[tt][:, 0:n], in_=Ttmp[tt][:, 0:n], func=AF.Ln, bias=one_t[:, 0:1]),
                                 reads=["Ttmp%d" % tt, "one"], writes=["Ttmp%d" % tt])
                            S.op("act", lambda e: e.activation(out=Ttmp[tt][:, 0:n], in_=Ttmp[tt][:, 0:n], func=AF.Exp, scale=-1.0),
                                 reads=["Ttmp%d" % tt], writes=["Ttmp%d" % tt])
                            S.op("dve", lambda e: e.tensor_tensor(out=SG[:, c0:c1], in0=psum[:, b, 0:n], in1=Ttmp[tt][:, 0:n], op=ALU.mult),
                                 reads=[bk(b), "Ttmp%d" % tt], writes=["SG"])
                        elif kind == "q":
                            S.op("dve", lambda e: e.tensor_copy(out=QT[:, c0:c1], in_=psum[:, b, 0:n]), reads=[bk(b)], writes=["QT"])
                        else:
                            S.op("dve", lambda e: e.tensor_copy(out=KT[:, c0:c1], in_=psum[:, b, 0:n]), reads=[bk(b)], writes=["KT"])
                    return (xoff + c1, emit)

                def v_tile(tb0):
                    tbs = [tb for tb in (tb0, tb0 + 1) if tb < NKB_A]
                    nt = len(tbs)

                    def emit():
                        b = next_pbank()

                        def mmv(e):
                            ins = None
                            for i, tb in enumerate(tbs):
                                for kc in range(KC):
                                    ins = e.matmul(psum[:, b, i * 256:(i + 1) * 256], lhsT=xT_bf[:, kc, tb * 128:(tb + 1) * 128],
                                                   rhs=wv_bf[:, kc, :], start=(kc == 0), stop=(kc == KC - 1))
                            return ins
                        S.op("pe", mmv, reads=["wv"] + xk(tbs[0] * 128, (tbs[-1] + 1) * 128), writes=[bk(b)])
                        S.op("act", lambda e: e.activation(
                            out=V_sb[:, tb0:tb0 + nt, :], in_=psum[:, b, 0:nt * 256].rearrange("p (a c) -> p a c", a=nt), func=AF.Copy),
                            reads=[bk(b)], writes=["V"])
                    return ((tbs[-1] + 1) * 128, emit)

                for (c0, c1) in _tiles(NU, 512):
                    items.append(fm_tile("g", c0, c1))
                for (c0, c1) in _tiles(NU, 512):
                    items.append(fm_tile("q", c0, c1))
                for (c0, c1) in _tiles(NTA, 512):
                    items.append(fm_tile("k", c0, c1))
                if hp % 2 == 0:
                    for tb0 in range(0, NKB_A, 2):
                        items.append(v_tile(tb0))
                return items

            def projection_A(hp):
                for need, emit in projection_items(hp):
                    emit()

            nd_ctr = [0]

            def attention_A(hp):
                slot = hp % 2
                hpl = hp % 2
                ekey = "E%d" % slot

                def pv(ju):
                    i = nd_ctr[0] % 2
                    nd_ctr[0] += 1
                    nb, db = 4 + i, 6 + i

                    def mm(e, ju=ju, nb=nb, db=db):
                        ins = None
                        kbs = list(range(ju, ju + 5))
                        for idx, kb2 in enumerate(kbs):
                            col = (ju + 4 - kb2) * 128
                            st, sp_ = (idx == 0), (idx == len(kbs) - 1)
                            r = kb2 % RING_A
                            for h in range(2):
                                e.matmul(psum[h * 64:(h + 1) * 64, nb, 0:128],
                                         lhsT=V_sb[:, kb2, hpl * 128 + h * 64:hpl * 128 + (h + 1) * 64],
                                         rhs=PT[h][r][:, col:col + 128], start=st, stop=sp_)
                            for h in range(2):
                                ins = e.matmul(psum[h * 64:(h + 1) * 64, db, 0:128], lhsT=validA_sb[:, kb2, :],
                                               rhs=PT[h][r][:, col:col + 128], start=st, stop=sp_)
                        return ins
                    rd = ["V", "validA"] + ["PT%d_%d" % (h, kb2 % RING_A) for h in range(2) for kb2 in range(ju, ju + 5)]
                    S.op("pe", mm, reads=rd, writes=[bk(nb), bk(db)])
                    S.op("act", lambda e, i=i, db=db: e.activation(out=Rt[i][:, :], in_=psum[:, db, 0:128], func=AF.Ln, bias=eps60[:, 0:1]),
                         reads=[bk(db), "eps60"], writes=["Rt%d" % i])
                    S.op("act", lambda e, i=i: e.activation(out=Rt[i][:, :], in_=Rt[i][:, :], func=AF.Exp, scale=-1.0),
                         reads=["Rt%d" % i], writes=["Rt%d" % i])
                    S.op("dve", lambda e, i=i, nb=nb: e.tensor_tensor(out=Ot[i][:, :], in0=psum[:, nb, 0:128], in1=Rt[i][:, :], op=ALU.mult),
                         reads=[bk(nb), "Rt%d" % i], writes=["Ot%d" % i])
                    S.op("pool", lambda e, i=i, ju=ju: e.tensor_tensor(out=attnT[:, hp, ju * 128:(ju + 1) * 128], in0=Ot[i][:, :],
                                                                       in1=SG[:, ju * 128:(ju + 1) * 128], op=ALU.mult),
                         reads=["Ot%d" % i, "SG"], writes=["attnT"])

                for kb in range(NKB_A):
                    jmin, jmax = max(kb, 4), min(kb + 4, 20)
                    c0, c1 = (jmin - kb) * 128, (jmax - kb + 1) * 128
                    u0 = (jmin - 4) * 128
                    r = kb % RING_A
                    for h in range(2):
                        def mm(e, h=h, kb=kb, c0=c0, c1=c1, u0=u0):
                            ins = None
                            a = c0
                            while a < c1:
                                bnk = 2 * h + (a // 512)
                                bend = min(c1, (a // 512 + 1) * 512)
                                ins = e.matmul(psum[:, bnk, a % 512:a % 512 + (bend - a)],
                                               lhsT=KT[h * 64:(h + 1) * 64, kb * 128:(kb + 1) * 128],
                                               rhs=QT[h * 64:(h + 1) * 64, u0 + (a - c0):u0 + (bend - c0)], start=True, stop=True)
                                a = bend
                            return ins
                        S.op("pe", mm, reads=["QT", "KT"], writes=[bk(2 * h), bk(2 * h + 1)])
                        psS = psum[:, 2 * h:2 * h + 2, :].rearrange("p a b -> p (a b)")
                        S.op("act", lambda e, h=h, r=r, c0=c0, c1=c1, psS=psS: e.activation(
                            out=PT[h][r][:, c0:c1], in_=psS[:, c0:c1], func=AF.Exp, scale=0.125),
                            reads=[bk(2 * h), bk(2 * h + 1)], writes=["PT%d_%d" % (h, r)])
                        S.op("dve", lambda e, h=h, r=r, c0=c0, c1=c1, slot=slot: e.tensor_tensor(
                            out=PT[h][r][:, c0:c1], in0=PT[h][r][:, c0:c1], in1=E_sb[slot][:, h, c0:c1], op=ALU.mult),
                            reads=["PT%d_%d" % (h, r), ekey], writes=["PT%d_%d" % (h, r)])
                    ju = kb - 5
                    if ju >= 0:
                        pv(ju)
                for ju in range(NKB_A - 5, NQB_A):
                    pv(ju)

            prefetch_A(0)
            items0 = projection_items(0)

            prev_t1 = [0]

            def after_tile(t1):
                lim, prev_t1[0] = prev_t1[0], t1
                rest = []
                for need, emit in items0:
                    if need <= lim:
                        emit()
                    else:
                        rest.append((need, emit))
                items0[:] = rest
            preamble(after_tile)
            proj_nbanks[0] = 8
            after_tile(NTA)
            assert not items0
            prefetch_A(1)
            dbg_dump("xTn", lambda c0, c1: xT_bf[:].rearrange("p a b -> p (a b)")[:, c0:c1], KC * NTA, ALLX)
            for hp in range(8):
                if hp >= 1 and hp + 1 < 8:
                    prefetch_A(hp + 1)
                if hp >= 1:
                    projection_A(hp)
                if hp == 0:
                    dbg_dump("QT0", lambda c0, c1: QT[:, c0:c1], NU, ["QT"])
                    dbg_dump("KT0", lambda c0, c1: KT[:, c0:c1], NTA, ["KT"])
                    dbg_dump("SG0", lambda c0, c1: SG[:, c0:c1], NU, ["SG"])
                    dbg_dump("V0", lambda c0, c1: V_sb[:].rearrange("p a b -> p (a b)")[:, c0:c1], NKB_A * 256, ["V"])
                attention_A(hp)
            dbg_dump("attnA", lambda c0, c1: attnT[:].rearrange("p a b -> p (a b)")[:, c0:c1], KC * NU, ["attnT"])
            S.flush()

        with ExitStack() as es2:
            h1T = sbuf(es2, "h1T", [128, KC, NU], F32)
            h1n = sbuf(es2, "h1n", [128, KC, NU], BF16)
            kvw_bf = sbuf(es2, "kvw_bf", [128, KC, 384], BF16)

            def load_wout(es, wsrc, stgs, wout_bf):
                for i in range(4):
                    s = i % 2
                    S.op("sp", lambda e, s=s, i=i: e.dma_start(out=stgs[s][:, 0:2048], in_=wsrc[:, i * 2048:(i + 1) * 2048]),
                         writes=["wstg%d" % s], dma="wstg%d" % s)
                    S.op("act", lambda e, s=s, i=i: e.activation(
                        out=wout_bf[:, :, i * 256:(i + 1) * 256], in_=stgs[s][:, 0:2048].rearrange("p (kc c) -> p kc c", kc=KC), func=AF.Copy),
                        reads=["wstg%d" % s], writes=["wout%d" % i])

            def stats_a(src_ap_fn, src_key, n, sq):
                S.op("act", lambda e, n=n: e.activation(out=sq[:, :, 0:n], in_=src_ap_fn(), func=AF.Square),
                     reads=[src_key], writes=["sq"])

            def stats_b(n, sq, rt, rstd, bank_ctr):
                b = 4 + bank_ctr[0] % 4
                bank_ctr[0] += 1

                def stat_mm(e, b=b, n=n):
                    ins = None
                    for kc in range(KC):
                        ins = e.matmul(psum[:, b, 0:n], lhsT=ones_bf[:, :], rhs=sq[:, kc, 0:n],
                                       start=(kc == 0), stop=(kc == KC - 1))
                    return ins
                S.op("pe", stat_mm, reads=["sq", "ones"], writes=[bk(b)])
                S.op("act", lambda e, b=b, n=n: e.activation(out=rt[:, 0:n], in_=psum[:, b, 0:n], func=AF.Ln,
                                                             scale=1.0 / D, bias=eps_t[:, 0:1]),
                     reads=[bk(b), "eps"], writes=["rt"])
                S.op("act", lambda e, n=n: e.activation(out=rstd[:, 0:n], in_=rt[:, 0:n], func=AF.Exp, scale=-0.5),
                     reads=["rt"], writes=["rstd"])

            with ExitStack() as es2a:
                stg2 = [sbuf(es2a, "stg2_%d" % i, [128, KC * TILE], F32) for i in range(2)]
                wout_bf = sbuf(es2a, "woutA_bf", [128, KC, 1024], BF16)
                sq = sbuf(es2a, "sq2", [128, KC, TILE], BF16)
                rt = sbuf(es2a, "rt2", [128, TILE], F32)
                rstd = sbuf(es2a, "rstd2", [128, TILE], F32)
                load_wout(es2a, wa_out, stg2, wout_bf)
                kvstg = sbuf(es2a, "kvstg", [128, KC * 128], F32)

                def prefetch_kvw():
                    for part in range(3):
                        src = kvw[:].rearrange("p (kc c) -> p kc c", kc=KC)[:, :, part * 128:(part + 1) * 128]
                        S.op("sp", lambda e, src=src: e.dma_start(out=kvstg[:, :].rearrange("p (kc c) -> p kc c", kc=KC), in_=src),
                             writes=["kvstg"], dma="kvstg")

                        def castkv(e, part=part):
                            ins = None
                            sv = kvstg[:, :].rearrange("p (kc c) -> p kc c", kc=KC)
                            for kc in range(KC):
                                ins = e.activation(out=kvw_bf[:, kc, part * 128:(part + 1) * 128], in_=sv[:, kc, :], func=AF.Identity,
                                                   scale=gains_sb[:, KV_G + kc:KV_G + kc + 1])
                            return ins
                        S.op("act", castkv, reads=["kvstg", "gains"], writes=["kvw"])
                bctr = [0]
                wb = [0]
                pendingA = []
                for ti, (u0, u1) in enumerate(_tiles(NU, TILE)):
                    n = u1 - u0
                    s = ti % 2
                    sv = stg2[s][:, 0:KC * n].rearrange("p (kc t) -> p kc t", kc=KC)
                    S.op("sp", lambda e, sv=sv, u0=u0, u1=u1: e.dma_start(out=sv, in_=xT_v[:, :, 512 + u0:512 + u1]),
                         writes=["wstg%d" % s], dma="wstg%d" % s)
                    for oc in range(KC):
                        b = wb[0] % 4
                        wb[0] += 1

                        def mm(e, b=b, oc=oc, u0=u0, u1=u1, n=n):
                            ins = None
                            for kc in range(KC):
                                ins = e.matmul(psum[:, b, 0:n], lhsT=wout_bf[:, kc, oc * 128:(oc + 1) * 128],
                                               rhs=attnT[:, kc, u0:u1], start=(kc == 0), stop=(kc == KC - 1))
                            return ins
                        S.op("pe", mm, reads=["wout%d" % (oc // 2), "attnT"], writes=[bk(b)])
                        S.op("dve", lambda e, b=b, oc=oc, u0=u0, u1=u1, n=n, sv=sv: e.tensor_tensor(
                            out=h1T[:, oc, u0:u1], in0=psum[:, b, 0:n], in1=sv[:, oc, :], op=ALU.add),
                            reads=[bk(b), "wstg%d" % s], writes=["h1T_%d" % ti])
                    def finish(ti=ti, u0=u0, u1=u1, n=n):
                        stats_b(n, sq, rt, rstd, bctr)
                        S.op("dve", lambda e: e.tensor_tensor(
                            out=h1n[:, :, u0:u1], in0=h1T[:, :, u0:u1], in1=rstd[:, 0:n].unsqueeze(1).to_broadcast([128, KC, n]), op=ALU.mult),
                            reads=["h1T_%d" % ti, "rstd"], writes=["h1n"])
                    if ti == 3:
                        prefetch_kvw()
                    if pendingA:
                        pendingA.pop()()
                    stats_a(lambda u0=u0, u1=u1: h1T[:, :, u0:u1], "h1T_%d" % ti, n, sq)
                    pendingA.append(finish)
                pendingA.pop()()
                dbg_dump("h1T", lambda c0, c1: h1T[:].rearrange("p a b -> p (a b)")[:, c0:c1], KC * NU, ["h1n"])
                dbg_dump("h1n", lambda c0, c1: h1n[:].rearrange("p a b -> p (a b)")[:, c0:c1], KC * NU, ["h1n"])
                S.flush()

            with ExitStack() as es2b:
                stgB = [sbuf(es2b, "stgB%d" % i, [128, KC * 256], F32) for i in range(2)]
                wqg_bf = [sbuf(es2b, "wqg%d" % i, [128, KC, 256], BF16) for i in range(2)]
                KshT = [sbuf(es2b, "KshT%d" % g, [128, NU], BF16) for g in range(2)]
                Vsh = sbuf(es2b, "Vsh", [128, NKB_B, 128], BF16)
                QTb = sbuf(es2b, "QTb", [128, NW], BF16)
                SGb = sbuf(es2b, "SGb", [128, NW], BF16)
                PTb = [[sbuf(es2b, "PTb%d_%d" % (h, s), [128, 256], BF16) for s in range(RING_B)] for h in range(2)]
                bstgB = sbuf(es2b, "bstgB", [128, 512], F32)
                EB = [sbuf(es2b, "EB%d" % i, [128, 2, 256], BF16) for i in range(2)]
                TtmpB = [sbuf(es2b, "TtmpB%d" % i, [128, 512], F32) for i in range(2)]
                RtB = [sbuf(es2b, "RtB%d" % i, [128, 128], F32) for i in range(2)]
                OtB = [sbuf(es2b, "OtB%d" % i, [128, 128], F32) for i in range(2)]
                sctr = [0]

                def next_stgB():
                    s = sctr[0] % 2
                    sctr[0] += 1
                    return s

                def prefetch_B(hp):
                    s = next_stgB()
                    slot = hp % 2
                    S.op("sp", lambda e, s=s, hp=hp: e.dma_start(out=stgB[s][:, :], in_=wb_qg[hp]),
                         writes=["stgB%d" % s], dma="stgB%d" % s)

                    def cast(e, s=s, slot=slot):
                        ins = None
                        sv = stgB[s][:, :].rearrange("p (kc c) -> p kc c", kc=KC)
                        for kc in range(KC):
                            ins = e.activation(out=wqg_bf[slot][:, kc, :], in_=sv[:, kc, :], func=AF.Identity,
                                               scale=gains_sb[:, B_G + kc:B_G + kc + 1])
                        return ins
                    S.op("act", cast, reads=["stgB%d" % s, "gains"], writes=["wqg%d" % slot])
                    S.op("sp", lambda e, hp=hp: e.dma_start(out=bstgB[:, :], in_=biasB[hp]), writes=["bstgB"], dma="bstgB")
                    S.op("act", lambda e, slot=slot: e.activation(out=EB[slot][:].rearrange("p a b -> p (a b)"), in_=bstgB[:, :], func=AF.Exp),
                         reads=["bstgB"], writes=["EB%d" % slot])

                    def corners(e, slot=slot):
                        ins = None
                        for h in range(2):
                            e.memset(EB[slot][64:128, h, 0:64], 0.0)
                            ins = e.memset(EB[slot][0:64, h, 192:256], 0.0)
                        return ins
                    S.op("pool", corners, reads=[], writes=["EB%d" % slot])

                pbank = [0]

                def next_pb():
                    b = pbank[0]
                    pbank[0] = (b + 1) % 8
                    return b

                for g in range(2):
                    for (c0, c1) in _tiles(NU, 512):
                        b = next_pb()
                        n = c1 - c0

                        def mm(e, b=b, g=g, c0=c0, c1=c1, n=n):
                            ins = None
                            for kc in range(KC):
                                ins = e.matmul(psum[:, b, 0:n], lhsT=kvw_bf[:, kc, g * 128:(g + 1) * 128], rhs=h1n[:, kc, c0:c1],
                                               start=(kc == 0), stop=(kc == KC - 1))
                            return ins
                        S.op("pe", mm, reads=["kvw", "h1n"], writes=[bk(b)])
                        S.op("dve", lambda e, b=b, g=g, c0=c0, c1=c1, n=n: e.tensor_copy(out=KshT[g][:, c0:c1], in_=psum[:, b, 0:n]),
                             reads=[bk(b)], writes=["KshT%d" % g])
                for tb0 in range(0, NKB_B, 4):
                    tbs = [tb for tb in range(tb0, tb0 + 4) if tb < NKB_B]
                    b = next_pb()

                    def mmv(e, b=b, tbs=tbs):
                        ins = None
                        for i, tb in enumerate(tbs):
                            for kc in range(KC):
                                ins = e.matmul(psum[:, b, i * 128:(i + 1) * 128], lhsT=h1n[:, kc, tb * 128:(tb + 1) * 128],
                                               rhs=kvw_bf[:, kc, 256:384], start=(kc == 0), stop=(kc == KC - 1))
                        return ins
                    S.op("pe", mmv, reads=["kvw", "h1n"], writes=[bk(b)])
                    nt = len(tbs)
                    S.op("act", lambda e, b=b, tb0=tb0, nt=nt: e.activation(
                        out=Vsh[:, tb0:tb0 + nt, :], in_=psum[:, b, 0:nt * 128].rearrange("p (a c) -> p a c", a=nt), func=AF.Copy),
                        reads=[bk(b)], writes=["Vsh"])

                def projection_B(hp):
                    slot = hp % 2
                    wkey = "wqg%d" % slot
                    ti = 0
                    for (c0, c1) in _tiles(NW, 512):
                        b = next_pb()
                        n = c1 - c0

                        def mm(e, b=b, c0=c0, c1=c1, n=n):
                            ins = None
                            for kc in range(KC):
                                ins = e.matmul(psum[:, b, 0:n], lhsT=wqg_bf[slot][:, kc, 128:256], rhs=h1n[:, kc, 128 + c0:128 + c1],
                                               start=(kc == 0), stop=(kc == KC - 1))
                            return ins
                        S.op("pe", mm, reads=[wkey, "h1n"], writes=[bk(b)])
                        tt = ti % 2
                        ti += 1
                        S.op("act", lambda e, b=b, n=n, tt=tt: e.activation(out=TtmpB[tt][:, 0:n], in_=psum[:, b, 0:n], func=AF.Exp, scale=-1.0),
                             reads=[bk(b)], writes=["TtmpB%d" % tt])
                        S.op("act", lambda e, n=n, tt=tt: e.activation(out=TtmpB[tt][:, 0:n], in_=TtmpB[tt][:, 0:n], func=AF.Ln, bias=one_t[:, 0:1]),
                             reads=["TtmpB%d" % tt, "one"], writes=["TtmpB%d" % tt])
                        S.op("act", lambda e, n=n, tt=tt: e.activation(out=TtmpB[tt][:, 0:n], in_=TtmpB[tt][:, 0:n], func=AF.Exp, scale=-1.0),
                             reads=["TtmpB%d" % tt], writes=["TtmpB%d" % tt])
                        S.op("dve", lambda e, b=b, c0=c0, c1=c1, n=n, tt=tt: e.tensor_tensor(
                            out=SGb[:, c0:c1], in0=psum[:, b, 0:n], in1=TtmpB[tt][:, 0:n], op=ALU.mult),
                            reads=[bk(b), "TtmpB%d" % tt], writes=["SGb"])
                    for (c0, c1) in _tiles(NW, 512):
                        b = next_pb()
                        n = c1 - c0

                        def mm(e, b=b, c0=c0, c1=c1, n=n):
                            ins = None
                            for kc in range(KC):
                                ins = e.matmul(psum[:, b, 0:n], lhsT=wqg_bf[slot][:, kc, 0:128], rhs=h1n[:, kc, 128 + c0:128 + c1],
                                               start=(kc == 0), stop=(kc == KC - 1))
                            return ins
                        S.op("pe", mm, reads=[wkey, "h1n"], writes=[bk(b)])
                        S.op("dve", lambda e, b=b, c0=c0, c1=c1, n=n: e.tensor_copy(out=QTb[:, c0:c1], in_=psum[:, b, 0:n]),
                             reads=[bk(b)], writes=["QTb"])

                ndb = [0]

                def attention_B(hp):
                    slot = hp % 2
                    g = hp // 4
                    ekey = "EB%d" % slot

                    def pv(jw):
                        i = ndb[0] % 2
                        ndb[0] += 1
                        nb, db = 4 + i, 6 + i

                        def mm(e, jw=jw, nb=nb, db=db):
                            ins = None
                            kbs = [jw, jw + 1]
                            for idx, kb2 in enumerate(kbs):
                                col = (jw + 1 - kb2) * 128
                                st, sp_ = (idx == 0), (idx == len(kbs) - 1)
                                r = kb2 % RING_B
                                for h in range(2):
                                    e.matmul(psum[h * 64:(h + 1) * 64, nb, 0:128], lhsT=Vsh[:, kb2, g * 64:(g + 1) * 64],
                                             rhs=PTb[h][r][:, col:col + 128], start=st, stop=sp_)
                                for h in range(2):
                                    ins = e.matmul(psum[h * 64:(h + 1) * 64, db, 0:128], lhsT=validB_sb[:, kb2, :],
                                                   rhs=PTb[h][r][:, col:col + 128], start=st, stop=sp_)
                            return ins
                        rd = ["Vsh", "validB"] + ["PTb%d_%d" % (h, kb2 % RING_B) for h in range(2) for kb2 in (jw, jw + 1)]
                        S.op("pe", mm, reads=rd, writes=[bk(nb), bk(db)])
                        S.op("act", lambda e, i=i, db=db: e.activation(out=RtB[i][:, :], in_=psum[:, db, 0:128], func=AF.Ln, bias=esink[:, hp:hp + 1]),
                             reads=[bk(db), "esink"], writes=["RtB%d" % i])
                        S.op("act", lambda e, i=i: e.activation(out=RtB[i][:, :], in_=RtB[i][:, :], func=AF.Exp, scale=-1.0),
                             reads=["RtB%d" % i], writes=["RtB%d" % i])
                        S.op("dve", lambda e, i=i, nb=nb: e.tensor_tensor(out=OtB[i][:, :], in0=psum[:, nb, 0:128], in1=RtB[i][:, :], op=ALU.mult),
                             reads=[bk(nb), "RtB%d" % i], writes=["OtB%d" % i])
                        S.op("pool", lambda e, i=i, jw=jw: e.tensor_tensor(out=attnT[:, hp, jw * 128:(jw + 1) * 128], in0=OtB[i][:, :],
                                                                           in1=SGb[:, jw * 128:(jw + 1) * 128], op=ALU.mult),
                             reads=["OtB%d" % i, "SGb"], writes=["attnT"])

                    for kb in range(NKB_B):
                        jmin, jmax = max(kb - 1, 0), min(kb, NQB_B - 1)
                        c0, c1 = (jmin + 1 - kb) * 128, (jmax + 1 - kb + 1) * 128
                        w0 = jmin * 128
                        r = kb % RING_B
                        for h in range(2):
                            S.op("pe", lambda e, h=h, kb=kb, c0=c0, c1=c1, w0=w0: e.matmul(
                                psum[:, 2 * h, c0:c1], lhsT=KshT[g][h * 64:(h + 1) * 64, kb * 128:(kb + 1) * 128],
                                rhs=QTb[h * 64:(h + 1) * 64, w0:w0 + (c1 - c0)], start=True, stop=True),
                                reads=["QTb", "KshT%d" % g], writes=[bk(2 * h)])
                            S.op("act", lambda e, h=h, r=r, c0=c0, c1=c1: e.activation(
                                out=PTb[h][r][:, c0:c1], in_=psum[:, 2 * h, c0:c1], func=AF.Exp, scale=0.125),
                                reads=[bk(2 * h)], writes=["PTb%d_%d" % (h, r)])
                            S.op("dve", lambda e, h=h, r=r, c0=c0, c1=c1: e.tensor_tensor(
                                out=PTb[h][r][:, c0:c1], in0=PTb[h][r][:, c0:c1], in1=EB[slot][:, h, c0:c1], op=ALU.mult),
                                reads=["PTb%d_%d" % (h, r), ekey], writes=["PTb%d_%d" % (h, r)])
                        jw = kb - 2
                        if jw >= 0:
                            pv(jw)
                    for jw in range(NKB_B - 2, NQB_B):
                        pv(jw)

                prefetch_B(0)
                for hp in range(8):
                    if hp + 1 < 8:
                        prefetch_B(hp + 1)
                    projection_B(hp)
                    attention_B(hp)
                dbg_dump("attnB", lambda c0, c1: attnT[:, c0 // NW, c0 % NW:c0 % NW + (c1 - c0)], KC * NW, ["attnT"])
                S.flush()

            with ExitStack() as es2c:
                TB = 256
                wstg = [sbuf(es2c, "wstgC%d" % i, [128, 2048], F32) for i in range(2)]
                wout_bf = sbuf(es2c, "woutB_bf", [128, KC, 1024], BF16)
                h2 = [sbuf(es2c, "h2_%d" % i, [128, KC, TB], F32) for i in range(3)]
                sq = sbuf(es2c, "sq3", [128, KC, TB], BF16)
                rt = sbuf(es2c, "rt3", [128, TB], F32)
                rstd = sbuf(es2c, "rstd3", [128, TB], F32)
                load_wout(es2c, wb_out, wstg, wout_bf)
                bctr = [0]
                wb = [0]
                pendingB = []
                for ti, (w0, w1) in enumerate(_tiles(NW, TB)):
                    n = w1 - w0
                    s = ti % 3
                    for oc in range(KC):
                        b = wb[0] % 4
                        wb[0] += 1

                        def mm(e, b=b, oc=oc, w0=w0, w1=w1, n=n):
                            ins = None
                            for kc in range(KC):
                                ins = e.matmul(psum[:, b, 0:n], lhsT=wout_bf[:, kc, oc * 128:(oc + 1) * 128],
                                               rhs=attnT[:, kc, w0:w1], start=(kc == 0), stop=(kc == KC - 1))
                            return ins
                        S.op("pe", mm, reads=["wout%d" % (oc // 2), "attnT"], writes=[bk(b)])
                        S.op("dve", lambda e, b=b, oc=oc, w0=w0, w1=w1, n=n, s=s: e.tensor_tensor(
                            out=h2[s][:, oc, 0:n], in0=psum[:, b, 0:n], in1=h1T[:, oc, 128 + w0:128 + w1], op=ALU.add),
                            reads=[bk(b)], writes=["h2_%d" % s])
                    def finishB(s=s, w0=w0, w1=w1, n=n):
                        stats_b(n, sq, rt, rstd, bctr)

                        def fin(e):
                            ins = None
                            for oc in range(KC):
                                ins = e.scalar_tensor_tensor(out=h2[s][:, oc, 0:n], in0=h2[s][:, oc, 0:n],
                                                             scalar=gains_sb[:, F_G + oc:F_G + oc + 1], in1=rstd[:, 0:n],
                                                             op0=ALU.mult, op1=ALU.mult)
                            return ins
                        S.op("dve", fin, reads=["h2_%d" % s, "rstd", "gains"], writes=["h2_%d" % s])
                        tok = S.op("sp", lambda e: e.dma_start(out=outT_v[:, :, w0:w1], in_=h2[s][:, :, 0:n]),
                                   reads=["h2_%d" % s], dma="out%d" % s)
                        out_tokens.append(tok)
                    if pendingB:
                        pendingB.pop()()
                    stats_a(lambda s=s, n=n: h2[s][:, :, 0:n], "h2_%d" % s, n, sq)
                    pendingB.append(finishB)
                pendingB.pop()()
                final = {}
                for key, val in out_tokens:
                    final[key] = max(final.get(key, 0), val)
                S.wait_tokens("sp", list(final.items()))
                S.flush()
    return nc


def _t5_bucket_np(rel):
    nb = 16
    max_exact = 8
    ret = np.where(rel > 0, nb, 0)
    n = np.abs(rel)
    nf = np.maximum(n, 1).astype(np.float32)
    large = max_exact + (np.log(nf / np.float32(max_exact)) / np.float32(math.log(128 / max_exact))
                         * np.float32(nb - max_exact)).astype(np.int32)
    large = np.minimum(large, nb - 1)
    return ret + np.where(n < max_exact, n, large)


def _prep_shared(a_norm, a_w_in, a_rel_bias, a_w_out, kv_norm, kv_w, t5_bias, b_norm, b_w_in, b_sinks, b_w_out, final_norm):
    f = np.float32
    w_in = np.asarray(a_w_in[0], f)
    w4 = w_in.reshape(KC, 128, 4, 8, 128)
    wa_qkg = np.ascontiguousarray(np.transpose(w4[:, :, [0, 1, 3]], (3, 1, 0, 2, 4))).reshape(8, 128, KC * 384)
    wv = w_in[:, 2048:3072].reshape(KC, 128, 4, 256)
    wa_v = np.ascontiguousarray(np.transpose(wv, (2, 1, 0, 3))).reshape(4, 128, KC * 256)
    wa_out = np.ascontiguousarray(np.transpose(np.asarray(a_w_out[0], f).reshape(KC, 128, 4, 256), (1, 2, 0, 3))).reshape(128, KC * 1024)
    wb_out = np.ascontiguousarray(np.transpose(np.asarray(b_w_out[0], f).reshape(KC, 128, 4, 256), (1, 2, 0, 3))).reshape(128, KC * 1024)

    def gcol(v):
        return np.asarray(v, f).reshape(KC, 128).T
    gains = np.ascontiguousarray(np.concatenate([gcol(a_norm[0]), gcol(kv_norm), gcol(b_norm[0]), gcol(final_norm)], axis=1))
    k = np.arange(128)[:, None]
    q = np.arange(640)[None, :]
    idxA = np.clip(q - k, -256, 256) + 256
    rb = np.asarray(a_rel_bias[0], f)
    bA = rb[idxA]
    biasA = np.ascontiguousarray(np.transpose(bA.reshape(128, 640, 8, 2), (2, 0, 3, 1))).reshape(8, 128, 1280)
    kvw_ = np.asarray(kv_w, f).reshape(KC, 128, 256)
    kcat = np.concatenate([kvw_[:, :, 0:64], kvw_[:, :, 0:64], kvw_[:, :, 64:128], kvw_[:, :, 64:128], kvw_[:, :, 128:256]], axis=2)
    kvw = np.ascontiguousarray(np.transpose(kcat, (1, 0, 2))).reshape(128, KC * 384)
    wb = np.asarray(b_w_in[0], f).reshape(KC, 128, 2, 8, 128)
    wb_qg = np.ascontiguousarray(np.transpose(wb, (3, 1, 0, 2, 4))).reshape(8, 128, KC * 256)
    qb = np.arange(256)[None, :]
    bucket = _t5_bucket_np((k - qb).astype(np.int32))
    tb = np.asarray(t5_bias, f)[bucket]
    biasB = np.ascontiguousarray(np.transpose(tb.reshape(128, 256, 8, 2), (2, 0, 3, 1))).reshape(8, 128, 512)
    sk = np.asarray(b_sinks[0], f).reshape(8, 2)
    sinkB = np.ascontiguousarray(np.repeat(sk.T, 64, axis=0))
    return dict(wa_qkg=wa_qkg, wa_v=wa_v, wa_out=wa_out, gains=gains, biasA=biasA, kvw=kvw, wb_qg=wb_qg,
                wb_out=wb_out, biasB=biasB, sinkB=sinkB)


def _prep_core(x, c):
    b, half = c // 2, c % 2
    T0 = half * NOWN
    lo = T0 - HALO
    xe = np.zeros((NTA, D), np.float32)
    src0 = max(lo, 0)
    xe[src0 - lo:, :] = x[b, src0:T0 + NOWN, :]
    xTc = np.ascontiguousarray(xe.T)
    tpos = lo + np.arange(NTA)
    vA = (tpos >= 0).astype(np.float32).reshape(NKB_A, 128).T
    validA = np.ascontiguousarray(np.repeat(vA[:, :, None], 64, axis=2)).reshape(128, NKB_A * 64).astype(ml_dtypes.bfloat16)
    upos = T0 - 128 + np.arange(NU)
    vB = (upos >= 0).astype(np.float32).reshape(NKB_B, 128).T
    validB = np.ascontiguousarray(np.repeat(vB[:, :, None], 64, axis=2)).reshape(128, NKB_B * 64).astype(ml_dtypes.bfloat16)
    return dict(xT=xTc, validA=validA, validB=validB)


_NC_CACHE = {}


def kernel(x, a_norm, a_w_in, a_rel_bias, a_w_out, kv_norm, kv_w, t5_bias, b_norm, b_w_in, b_sinks, b_w_out, final_norm,
           _debug=()):
    x = np.asarray(x, np.float32)
    shared = _prep_shared(np.asarray(a_norm), np.asarray(a_w_in), np.asarray(a_rel_bias), np.asarray(a_w_out),
                          np.asarray(kv_norm), np.asarray(kv_w), np.asarray(t5_bias), np.asarray(b_norm),
                          np.asarray(b_w_in), np.asarray(b_sinks), np.asarray(b_w_out), np.asarray(final_norm))
    in_maps = []
    for c in range(N_CORES):
        m = dict(shared)
        m.update(_prep_core(x, c))
        in_maps.append(m)
    key = tuple(_debug)
    if key not in _NC_CACHE:
        _NC_CACHE[key] = build_nc(debug=key)
    nc = _NC_CACHE[key]
    res = run_bass_kernel_spmd(nc, in_maps, core_ids=list(range(N_CORES)))
    out = np.empty((4, SEQ, D), np.float32)
    for c in range(N_CORES):
        b, half = c // 2, c % 2
        out[b, half * NOWN:(half + 1) * NOWN, :] = np.asarray(res.results[c]["outT"]).T
    if _debug:
        return out, res.results
    return out
```

```python
import math
from contextlib import ExitStack

import numpy as np
import ml_dtypes

import concourse.bass as bass
import concourse.mybir as mybir
from concourse.bass_utils import run_bass_kernel_spmd

F32 = mybir.dt.float32
BF16 = mybir.dt.bfloat16
AF = mybir.ActivationFunctionType
ALU = mybir.AluOpType

N_CORES = 8
D = 1024
KC = 8
SEQ = 4096
NOWN = 2048
HALO = 640
NTA = NOWN + HALO
NU = NOWN + 128
NW = NOWN
NKB_A = NTA // 128
NQB_A = NU // 128
NKB_B = NU // 128
NQB_B = NW // 128
RING_A = 6
RING_B = 4
TILE = 384
EPS = 1e-6

ENGS = ("pe", "act", "dve", "pool", "sp")


class Sched:
    def __init__(self, nc, es):
        self.nc = nc
        self.es = es
        self.sem = {e: es.enter_context(nc.semaphore("s_" + e)) for e in ENGS}
        self.cnt = {e: 0 for e in ENGS}
        self.dsem = {}
        self.dcnt = {}
        self.lastw = {}
        self.readers = {}
        self.pending = {e: [] for e in ENGS}
        self.seen = {e: {} for e in ENGS}
        self.know = {}
        self.order = {}
        self.nwaits = 0

    def op(self, eng, fn, reads=(), writes=(), dma=None, ndma=1):
        deps = {}

        def add(tok):
            if tok is None:
                return
            key, val = tok
            if deps.get(key, 0) < val:
                deps[key] = val

        for r in reads:
            add(self.lastw.get(r))
            if r.startswith("bk") and eng in ("act", "dve"):
                for k, v in self.readers.get(r, {}).items():
                    if k[0] in ("act", "dve") and k[0] != eng:
                        add((k, v))
        for w in writes:
            add(self.lastw.get(w))
            for k, v in self.readers.get(w, {}).items():
                add((k, v))
        if dma is not None:
            if dma not in self.dsem:
                self.dsem[dma] = self.es.enter_context(self.nc.semaphore("d_" + str(dma)))
                self.dcnt[dma] = 0
            self.dcnt[dma] += 16 * ndma
            tok = (("dma", dma), self.dcnt[dma])
        else:
            self.cnt[eng] += 1
            tok = ((eng,), self.cnt[eng])
        waits = []
        for key, val in sorted(deps.items(), key=lambda kv: -self.order.get(kv, 0)):
            if key == ("pe",) and eng == "pe":
                continue
            if self.seen[eng].get(key, 0) >= val:
                continue
            waits.append((key, val))
            for k2, v2 in self.know.get((key, val), {}).items():
                if self.seen[eng].get(k2, 0) < v2:
                    self.seen[eng][k2] = v2
            self.seen[eng][key] = val
        self.nwaits += len(waits)
        self.order[tok] = len(self.order) + 1
        if dma is None:
            kn = dict(self.seen[eng])
            kn[tok[0]] = tok[1]
            self.know[tok] = kn
        else:
            self.know[tok] = dict(self.seen[eng])
        self.pending[eng].append((fn, waits, tok))
        for r in reads:
            d = self.readers.setdefault(r, {})
            if d.get(tok[0], 0) < tok[1]:
                d[tok[0]] = tok[1]
        for w in writes:
            self.lastw[w] = tok
            self.readers[w] = {}
        return tok

    def _semof(self, key):
        if key[0] == "dma":
            return self.dsem[key[1]]
        return self.sem[key[0]]

    def wait_tokens(self, eng, toks):
        self.pending[eng].append((None, list(toks), None))

    def flush(self):
        nc = self.nc
        pend = self.pending
        self.pending = {e: [] for e in ENGS}
        with nc.Block() as block:
            def mk(engname):
                lst = pend[engname]

                def body(e):
                    class _First:
                        def __init__(self, eng):
                            self._eng = eng
                            self.first = None

                        def __getattr__(self, name):
                            real = getattr(self._eng, name)

                            def call(*a, **k):
                                r = real(*a, **k)
                                if self.first is None:
                                    self.first = r
                                return r
                            return call

                    for fn, waits, tok in lst:
                        fuse = fn is not None and len(waits) >= 1
                        for key, val in (waits[:-1] if fuse else waits):
                            e.wait_ge(self._semof(key), val)
                        if fn is None:
                            continue
                        if fuse:
                            px = _First(e)
                            ins = fn(px)
                            px.first._wait_ge(self._semof(waits[-1][0]), waits[-1][1])
                        else:
                            ins = fn(e)
                        key, val = tok
                        if key[0] == "dma":
                            if not isinstance(ins, (list, tuple)):
                                ins = [ins]
                            for i in ins:
                                i.then_inc(self.dsem[key[1]], 16)
                        else:
                            ins.then_inc(self.sem[engname], 1)
                return body

            if pend["pe"]:
                block.tensor(mk("pe"))
            if pend["act"]:
                block.scalar(mk("act"))
            if pend["dve"]:
                block.vector(mk("dve"))
            if pend["pool"]:
                block.gpsimd(mk("pool"))
            if pend["sp"]:
                block.sync(mk("sp"))


def _tiles(n, step):
    return [(a, min(a + step, n)) for a in range(0, n, step)]


def build_nc(debug=()):
    nc = bass.Bass("TRN2", target_bir_lowering=False)

    def din(name, shape, dt=F32):
        return nc.dram_tensor(name, shape, dt, kind="ExternalInput").ap()

    xT = din("xT", [D, NTA])
    wa_qkg = din("wa_qkg", [8, 128, KC * 384])
    wa_v = din("wa_v", [4, 128, KC * 256])
    wa_out = din("wa_out", [128, KC * 1024])
    gains = din("gains", [128, 32])
    biasA = din("biasA", [8, 128, 1280])
    validA = din("validA", [128, NKB_A * 64], BF16)
    kvw = din("kvw", [128, KC * 384])
    wb_qg = din("wb_qg", [8, 128, KC * 256])
    wb_out = din("wb_out", [128, KC * 1024])
    biasB = din("biasB", [8, 128, 512])
    sinkB = din("sinkB", [128, 8])
    validB = din("validB", [128, NKB_B * 64], BF16)
    outT = nc.dram_tensor("outT", [D, NW], F32, kind="ExternalOutput").ap()
    dbg_out = {}
    dbg_shapes = {"xTn": [128, KC * NTA], "attnA": [128, KC * NU], "h1T": [128, KC * NU],
                  "h1n": [128, KC * NU], "attnB": [128, KC * NW], "QT0": [128, NU], "KT0": [128, NTA],
                  "SG0": [128, NU], "V0": [128, NKB_A * 256]}
    for name in debug:
        dbg_out[name] = nc.dram_tensor("dbg_" + name, dbg_shapes[name], F32, kind="ExternalOutput").ap()

    xT_v = xT.rearrange("(kc p) t -> p kc t", p=128)
    outT_v = outT.rearrange("(kc p) t -> p kc t", p=128)

    with ExitStack() as es0:
        S = Sched(nc, es0)
        out_tokens = []

        def sbuf(es, name, shape, dt):
            return es.enter_context(nc.sbuf_tensor(name, shape, dt))

        psum = es0.enter_context(nc.psum_tensor("psum", [128, 8, 512], F32))

        def bk(b):
            return "bk%d" % b

        ones_bf = sbuf(es0, "ones_bf", [128, 128], BF16)
        eps_t = sbuf(es0, "eps_t", [128, 1], F32)
        one_t = sbuf(es0, "one_t", [128, 1], F32)
        eps60 = sbuf(es0, "eps60", [128, 1], F32)
        gains_sb = sbuf(es0, "gains_sb", [128, 32], F32)
        sink_sb = sbuf(es0, "sink_sb", [128, 8], F32)
        esink = sbuf(es0, "esink", [128, 8], F32)
        validB_sb = sbuf(es0, "validB_sb", [128, NKB_B, 64], BF16)
        attnT = sbuf(es0, "attnT", [128, KC, NU], BF16)
        dbgf = sbuf(es0, "dbgf", [128, 512], F32) if debug else None

        def dbg_dump(name, src_fn, ncols, key_reads):
            if name not in dbg_out:
                return
            for (c0, c1) in _tiles(ncols, 512):
                S.op("dve", lambda e, c0=c0, c1=c1: e.tensor_copy(out=dbgf[:, 0:c1 - c0], in_=src_fn(c0, c1)),
                     reads=key_reads, writes=["dbgf"])
                tok = S.op("sp", lambda e, c0=c0, c1=c1: e.dma_start(out=dbg_out[name][:, c0:c1], in_=dbgf[:, 0:c1 - c0]),
                           reads=["dbgf"], dma="dbg")
                out_tokens.append(tok)

        S.op("pool", lambda e: e.memset(ones_bf[:], 1.0), writes=["ones"])
        S.op("pool", lambda e: e.memset(eps_t[:], EPS), writes=["eps"])
        S.op("pool", lambda e: e.memset(one_t[:], 1.0), writes=["one"])
        S.op("pool", lambda e: e.memset(eps60[:], 2.0 ** -60), writes=["eps60"])
        S.op("sp", lambda e: e.dma_start(out=gains_sb[:], in_=gains[:]), writes=["gains"], dma="c0")
        S.op("sp", lambda e: e.dma_start(out=sink_sb[:], in_=sinkB[:]), writes=["sink"], dma="c1")
        S.op("sp", lambda e: e.dma_start(out=validB_sb[:].rearrange("p a b -> p (a b)"), in_=validB[:]),
             writes=["validB"], dma="c2")
        S.op("act", lambda e: e.activation(out=esink[:], in_=sink_sb[:], func=AF.Exp), reads=["sink"], writes=["esink"])

        A_G, KV_G, B_G, F_G = 0, 8, 16, 24

        with ExitStack() as es1:
            xT_bf = sbuf(es1, "xT_bf", [128, KC, NTA], BF16)
            validA_sb = sbuf(es1, "validA_sb", [128, NKB_A, 64], BF16)
            stg = [sbuf(es1, "stg%d" % i, [128, KC * TILE], F32) for i in range(2)]
            sq = sbuf(es1, "sq", [128, KC, TILE], BF16)
            rt = sbuf(es1, "rt", [128, TILE], F32)
            rstd = sbuf(es1, "rstd", [128, TILE], F32)
            bstg = sbuf(es1, "bstg", [128, 1280], F32)
            E_sb = [sbuf(es1, "E%d" % i, [128, 2, 640], BF16) for i in range(2)]
            wqkg_bf = [sbuf(es1, "wqkg%d" % i, [128, KC, 384], BF16) for i in range(2)]
            wv_bf = sbuf(es1, "wv_bf", [128, KC, 256], BF16)
            QT = sbuf(es1, "QT", [128, NU], BF16)
            KT = sbuf(es1, "KT", [128, NTA], BF16)
            SG = sbuf(es1, "SG", [128, NU], BF16)
            V_sb = sbuf(es1, "V_sb", [128, NKB_A, 256], BF16)
            PT = [[sbuf(es1, "PT%d_%d" % (h, s), [128, 640], BF16) for s in range(RING_A)] for h in range(2)]
            Ttmp = [sbuf(es1, "Ttmp%d" % i, [128, 512], F32) for i in range(2)]
            Rt = [sbuf(es1, "Rt%d" % i, [128, 128], F32) for i in range(2)]
            Ot = [sbuf(es1, "Ot%d" % i, [128, 128], F32) for i in range(2)]

            S.op("sp", lambda e: e.dma_start(out=validA_sb[:].rearrange("p a b -> p (a b)"), in_=validA[:]),
                 writes=["validA"], dma="c3")

            stg_ctr = [0]

            def next_stg():
                s = stg_ctr[0] % 2
                stg_ctr[0] += 1
                return s

            XT_TILES = _tiles(NTA, TILE)

            def xk(a, b_):
                return ["xT_%d" % i for i, (t0, t1) in enumerate(XT_TILES) if t0 < b_ and t1 > a]
            ALLX = xk(0, NTA)

            stat_bank = [4]

            def preamble(after_tile=None):
              for ti_, (t0, t1) in enumerate(XT_TILES):
                n = t1 - t0
                s = next_stg()
                sv = stg[s][:, 0:KC * n].rearrange("p (kc t) -> p kc t", kc=KC)
                S.op("sp", lambda e, sv=sv, t0=t0, t1=t1: e.dma_start(out=sv, in_=xT_v[:, :, t0:t1]),
                     writes=["stg%d" % s], dma="stg%d" % s)
                S.op("act", lambda e, sv=sv, n=n: e.activation(out=sq[:, :, 0:n], in_=sv, func=AF.Square),
                     reads=["stg%d" % s], writes=["sq"])
                b = stat_bank[0]
                stat_bank[0] = 4 + (stat_bank[0] - 3) % 4

                def stat_mm(e, b=b, n=n):
                    ins = None
                    for kc in range(KC):
                        ins = e.matmul(psum[:, b, 0:n], lhsT=ones_bf[:, :], rhs=sq[:, kc, 0:n],
                                       start=(kc == 0), stop=(kc == KC - 1))
                    return ins
                S.op("pe", stat_mm, reads=["sq", "ones"], writes=[bk(b)])
                S.op("act", lambda e, b=b, n=n: e.activation(out=rt[:, 0:n], in_=psum[:, b, 0:n], func=AF.Ln,
                                                             scale=1.0 / D, bias=eps_t[:, 0:1]),
                     reads=[bk(b), "eps"], writes=["rt"])
                S.op("act", lambda e, n=n: e.activation(out=rstd[:, 0:n], in_=rt[:, 0:n], func=AF.Exp, scale=-0.5),
                     reads=["rt"], writes=["rstd"])
                S.op("dve", lambda e, sv=sv, t0=t0, t1=t1, n=n: e.tensor_tensor(
                    out=xT_bf[:, :, t0:t1], in0=sv, in1=rstd[:, 0:n].unsqueeze(1).to_broadcast([128, KC, n]), op=ALU.mult),
                    reads=["stg%d" % s, "rstd"], writes=["xT_%d" % ti_])
                if after_tile is not None:
                    after_tile(t1)

            def prefetch_A(hp):
                s = next_stg()
                slot = hp % 2
                S.op("sp", lambda e, s=s, hp=hp: e.dma_start(out=stg[s][:, 0:KC * 384], in_=wa_qkg[hp]),
                     writes=["stg%d" % s], dma="stg%d" % s)

                def cast(e, s=s, slot=slot):
                    ins = None
                    sv = stg[s][:, 0:KC * 384].rearrange("p (kc c) -> p kc c", kc=KC)
                    for kc in range(KC):
                        ins = e.activation(out=wqkg_bf[slot][:, kc, :], in_=sv[:, kc, :], func=AF.Identity,
                                           scale=gains_sb[:, A_G + kc:A_G + kc + 1])
                    return ins
                S.op("act", cast, reads=["stg%d" % s, "gains"], writes=["wqkg%d" % slot])
                if hp % 2 == 0:
                    s2 = next_stg()
                    S.op("sp", lambda e, s2=s2, hp=hp: e.dma_start(out=stg[s2][:, 0:KC * 256], in_=wa_v[hp // 2]),
                         writes=["stg%d" % s2], dma="stg%d" % s2)

                    def castv(e, s2=s2):
                        ins = None
                        sv = stg[s2][:, 0:KC * 256].rearrange("p (kc c) -> p kc c", kc=KC)
                        for kc in range(KC):
                            ins = e.activation(out=wv_bf[:, kc, :], in_=sv[:, kc, :], func=AF.Identity,
                                               scale=gains_sb[:, A_G + kc:A_G + kc + 1])
                        return ins
                    S.op("act", castv, reads=["stg%d" % s2, "gains"], writes=["wv"])
                S.op("sp", lambda e, hp=hp: e.dma_start(out=bstg[:, :], in_=biasA[hp]), writes=["bstg"], dma="bstg")
                S.op("act", lambda e, slot=slot: e.activation(out=E_sb[slot][:].rearrange("p a b -> p (a b)"), in_=bstg[:, :], func=AF.Exp),
                     reads=["bstg"], writes=["E%d" % slot])

                def corners(e, slot=slot):
                    ins = None
                    for h in range(2):
                        e.memset(E_sb[slot][64:128, h, 0:64], 0.0)
                        ins = e.memset(E_sb[slot][0:64, h, 576:640], 0.0)
                    return ins
                S.op("pool", corners, reads=[], writes=["E%d" % slot])

            proj_bank = [0]
            proj_nbanks = [4]

            def next_pbank():
                b = proj_bank[0] % proj_nbanks[0]
                proj_bank[0] = (b + 1) % proj_nbanks[0]
                return b

            def proj_fm(wslot_ap_fn, rhs_fn, ncols_tiles, evac):
                for (c0, c1) in ncols_tiles:
                    b = next_pbank()
                    n = c1 - c0

                    def mm(e, b=b, c0=c0, c1=c1, n=n):
                        ins = None
                        for kc in range(KC):
                            ins = e.matmul(psum[:, b, 0:n], lhsT=wslot_ap_fn(kc), rhs=rhs_fn(kc, c0, c1),
                                           start=(kc == 0), stop=(kc == KC - 1))
                        return ins
                    yield b, c0, c1, n, mm

            def projection_items(hp):
                slot = hp % 2
                wkey = "wqkg%d" % slot
                items = []
                tcount = [0]

                def fm_tile(kind, c0, c1):
                    n = c1 - c0
                    wcol = {"g": 256, "q": 0, "k": 128}[kind]
                    xoff = 0 if kind == "k" else 512

                    def emit():
                        b = next_pbank()

                        def mm(e):
                            ins = None
                            for kc in range(KC):
                                ins = e.matmul(psum[:, b, 0:n], lhsT=wqkg_bf[slot][:, kc, wcol:wcol + 128],
                                               rhs=xT_bf[:, kc, xoff + c0:xoff + c1], start=(kc == 0), stop=(kc == KC - 1))
                            return ins
                        S.op("pe", mm, reads=[wkey] + xk(xoff + c0, xoff + c1), writes=[bk(b)])
                        if kind == "g":
                            tt = tcount[0] % 2
                            tcount[0] += 1
                            S.op("act", lambda e: e.activation(out=Ttmp[tt][:, 0:n], in_=psum[:, b, 0:n], func=AF.Exp, scale=-1.0),
                                 reads=[bk(b)], writes=["Ttmp%d" % tt])
                            S.op("act", lambda e: e.activation(out=Ttmp[tt][:, 0:n], in_=Ttmp[tt][:, 0:n], func=AF.Ln, bias=one_t[:, 0:1]),
                                 reads=["Ttmp%d" % tt, "one"], writes=["Ttmp%d" % tt])
                            S.op("act", lambda e: e.activation(out=Ttmp[tt][:, 0:n], in_=Ttmp[tt][:, 0:n], func=AF.Exp, scale=-1.0),
                                 reads=["Ttmp%d" % tt], writes=["Ttmp%d" % tt])
                            S.op("dve", lambda e: e.tensor_tensor(out=SG[:, c0:c1], in0=psum[:, b, 0:n], in1=Ttmp[tt][:, 0:n], op=ALU.mult),
                                 reads=[bk(b), "Ttmp%d" % tt], writes=["SG"])
                        elif kind == "q":
                            S.op("dve", lambda e: e.tensor_copy(out=QT[:, c0:c1], in_=psum[:, b, 0:n]), reads=[bk(b)], writes=["QT"])
                        else:
                            S.op("dve", lambda e: e.tensor_copy(out=KT[:, c0:c1], in_=psum[:, b, 0:n]), reads=[bk(b)], writes=["KT"])
                    return (xoff + c1, emit)

                def v_tile(tb0):
                    tbs = [tb for tb in (tb0, tb0 + 1) if tb < NKB_A]
                    nt = len(tbs)

                    def emit():
                        b = next_pbank()

                        def mmv(e):
                            ins = None
                            for i, tb in enumerate(tbs):
                                for kc in range(KC):
                                    ins = e.matmul(psum[:, b, i * 256:(i + 1) * 256], lhsT=xT_bf[:, kc, tb * 128:(tb + 1) * 128],
                                                   rhs=wv_bf[:, kc, :], start=(kc == 0), stop=(kc == KC - 1))
                            return ins
                        S.op("pe", mmv, reads=["wv"] + xk(tbs[0] * 128, (tbs[-1] + 1) * 128), writes=[bk(b)])
                        S.op("act", lambda e: e.activation(
                            out=V_sb[:, tb0:tb0 + nt, :], in_=psum[:, b, 0:nt * 256].rearrange("p (a c) -> p a c", a=nt), func=AF.Copy),
                            reads=[bk(b)], writes=["V"])
                    return ((tbs[-1] + 1) * 128, emit)

                for (c0, c1) in _tiles(NU, 512):
                    items.append(fm_tile("g", c0, c1))
                for (c0, c1) in _tiles(NU, 512):
                    items.append(fm_tile("q", c0, c1))
                for (c0, c1) in _tiles(NTA, 512):
                    items.append(fm_tile("k", c0, c1))
                if hp % 2 == 0:
                    for tb0 in range(0, NKB_A, 2):
                        items.append(v_tile(tb0))
                return items

            def projection_A(hp):
                for need, emit in projection_items(hp):
                    emit()

            nd_ctr = [0]

            def attention_A(hp):
                slot = hp % 2
                hpl = hp % 2
                ekey = "E%d" % slot

                def pv(ju):
                    i = nd_ctr[0] % 2
                    nd_ctr[0] += 1
                    nb, db = 4 + i, 6 + i

                    def mm(e, ju=ju, nb=nb, db=db):
                        ins = None
                        kbs = list(range(ju, ju + 5))
                        for idx, kb2 in enumerate(kbs):
                            col = (ju + 4 - kb2) * 128
                            st, sp_ = (idx == 0), (idx == len(kbs) - 1)
                            r = kb2 % RING_A
                            for h in range(2):
                                e.matmul(psum[h * 64:(h + 1) * 64, nb, 0:128],
                                         lhsT=V_sb[:, kb2, hpl * 128 + h * 64:hpl * 128 + (h + 1) * 64],
                                         rhs=PT[h][r][:, col:col + 128], start=st, stop=sp_)
                            for h in range(2):
                                ins = e.matmul(psum[h * 64:(h + 1) * 64, db, 0:128], lhsT=validA_sb[:, kb2, :],
                                               rhs=PT[h][r][:, col:col + 128], start=st, stop=sp_)
                        return ins
                    rd = ["V", "validA"] + ["PT%d_%d" % (h, kb2 % RING_A) for h in range(2) for kb2 in range(ju, ju + 5)]
                    S.op("pe", mm, reads=rd, writes=[bk(nb), bk(db)])
                    S.op("act", lambda e, i=i, db=db: e.activation(out=Rt[i][:, :], in_=psum[:, db, 0:128], func=AF.Ln, bias=eps60[:, 0:1]),
                         reads=[bk(db), "eps60"], writes=["Rt%d" % i])
                    S.op("act", lambda e, i=i: e.activation(out=Rt[i][:, :], in_=Rt[i][:, :], func=AF.Exp, scale=-1.0),
                         reads=["Rt%d" % i], writes=["Rt%d" % i])
                    S.op("dve", lambda e, i=i, nb=nb: e.tensor_tensor(out=Ot[i][:, :], in0=psum[:, nb, 0:128], in1=Rt[i][:, :], op=ALU.mult),
                         reads=[bk(nb), "Rt%d" % i], writes=["Ot%d" % i])
                    S.op("pool", lambda e, i=i, ju=ju: e.tensor_tensor(out=attnT[:, hp, ju * 128:(ju + 1) * 128], in0=Ot[i][:, :],
                                                                       in1=SG[:, ju * 128:(ju + 1) * 128], op=ALU.mult),
                         reads=["Ot%d" % i, "SG"], writes=["attnT"])

                for kb in range(NKB_A):
                    jmin, jmax = max(kb, 4), min(kb + 4, 20)
                    c0, c1 = (jmin - kb) * 128, (jmax - kb + 1) * 128
                    u0 = (jmin - 4) * 128
                    r = kb % RING_A
                    for h in range(2):
                        def mm(e, h=h, kb=kb, c0=c0, c1=c1, u0=u0):
                            ins = None
                            a = c0
                            while a < c1:
                                bnk = 2 * h + (a // 512)
                                bend = min(c1, (a // 512 + 1) * 512)
                                ins = e.matmul(psum[:, bnk, a % 512:a % 512 + (bend - a)],
                                               lhsT=KT[h * 64:(h + 1) * 64, kb * 128:(kb + 1) * 128],
                                               rhs=QT[h * 64:(h + 1) * 64, u0 + (a - c0):u0 + (bend - c0)], start=True, stop=True)
                                a = bend
                            return ins
                        S.op("pe", mm, reads=["QT", "KT"], writes=[bk(2 * h), bk(2 * h + 1)])
                        psS = psum[:, 2 * h:2 * h + 2, :].rearrange("p a b -> p (a b)")
                        S.op("act", lambda e, h=h, r=r, c0=c0, c1=c1, psS=psS: e.activation(
                            out=PT[h][r][:, c0:c1], in_=psS[:, c0:c1], func=AF.Exp, scale=0.125),
                            reads=[bk(2 * h), bk(2 * h + 1)], writes=["PT%d_%d" % (h, r)])
                        S.op("dve", lambda e, h=h, r=r, c0=c0, c1=c1, slot=slot: e.tensor_tensor(
                            out=PT[h][r][:, c0:c1], in0=PT[h][r][:, c0:c1], in1=E_sb[slot][:, h, c0:c1], op=ALU.mult),
                            reads=["PT%d_%d" % (h, r), ekey], writes=["PT%d_%d" % (h, r)])
                    ju = kb - 5
                    if ju >= 0:
                        pv(ju)
                for ju in range(NKB_A - 5, NQB_A):
                    pv(ju)

            prefetch_A(0)
            items0 = projection_items(0)

            prev_t1 = [0]

            def after_tile(t1):
                lim, prev_t1[0] = prev_t1[0], t1
                rest = []
                for need, emit in items0:
                    if need <= lim:
                        emit()
                    else:
                        rest.append((need, emit))
                items0[:] = rest
            preamble(after_tile)
            proj_nbanks[0] = 8
            after_tile(NTA)
            assert not items0
            prefetch_A(1)
            dbg_dump("xTn", lambda c0, c1: xT_bf[:].rearrange("p a b -> p (a b)")[:, c0:c1], KC * NTA, ALLX)
            for hp in range(8):
                if hp >= 1 and hp + 1 < 8:
                    prefetch_A(hp + 1)
                if hp >= 1:
                    projection_A(hp)
                if hp == 0:
                    dbg_dump("QT0", lambda c0, c1: QT[:, c0:c1], NU, ["QT"])
                    dbg_dump("KT0", lambda c0, c1: KT[:, c0:c1], NTA, ["KT"])
                    dbg_dump("SG0", lambda c0, c1: SG[:, c0:c1], NU, ["SG"])
                    dbg_dump("V0", lambda c0, c1: V_sb[:].rearrange("p a b -> p (a b)")[:, c0:c1], NKB_A * 256, ["V"])
                attention_A(hp)
            dbg_dump("attnA", lambda c0, c1: attnT[:].rearrange("p a b -> p (a b)")[:, c0:c1], KC * NU, ["attnT"])
            S.flush()

        with ExitStack() as es2:
            h1T = sbuf(es2, "h1T", [128, KC, NU], F32)
            h1n = sbuf(es2, "h1n", [128, KC, NU], BF16)
            kvw_bf = sbuf(es2, "kvw_bf", [128, KC, 384], BF16)

            def load_wout(es, wsrc, stgs, wout_bf):
                for i in range(4):
                    s = i % 2
                    S.op("sp", lambda e, s=s, i=i: e.dma_start(out=stgs[s][:, 0:2048], in_=wsrc[:, i * 2048:(i + 1) * 2048]),
                         writes=["wstg%d" % s], dma="wstg%d" % s)
                    S.op("act", lambda e, s=s, i=i: e.activation(
                        out=wout_bf[:, :, i * 256:(i + 1) * 256], in_=stgs[s][:, 0:2048].rearrange("p (kc c) -> p kc c", kc=KC), func=AF.Copy),
                        reads=["wstg%d" % s], writes=["wout%d" % i])

            def stats_a(src_ap_fn, src_key, n, sq):
                S.op("act", lambda e, n=n: e.activation(out=sq[:, :, 0:n], in_=src_ap_fn(), func=AF.Square),
                     reads=[src_key], writes=["sq"])

            def stats_b(n, sq, rt, rstd, bank_ctr):
                b = 6 + bank_ctr[0] % 2
                bank_ctr[0] += 1

                def stat_mm(e, b=b, n=n):
                    ins = None
                    for kc in range(KC):
                        ins = e.matmul(psum[:, b, 0:n], lhsT=ones_bf[:, :], rhs=sq[:, kc, 0:n],
                                       start=(kc == 0), stop=(kc == KC - 1))
                    return ins
                S.op("pe", stat_mm, reads=["sq", "ones"], writes=[bk(b)])
                S.op("act", lambda e, b=b, n=n: e.activation(out=rt[:, 0:n], in_=psum[:, b, 0:n], func=AF.Ln,
                                                             scale=1.0 / D, bias=eps_t[:, 0:1]),
                     reads=[bk(b), "eps"], writes=["rt"])
                S.op("act", lambda e, n=n: e.activation(out=rstd[:, 0:n], in_=rt[:, 0:n], func=AF.Exp, scale=-0.5),
                     reads=["rt"], writes=["rstd"])

            with ExitStack() as es2a:
                stg2 = [sbuf(es2a, "stg2_%d" % i, [128, KC * TILE], F32) for i in range(2)]
                wout_bf = sbuf(es2a, "woutA_bf", [128, KC, 1024], BF16)
                sq = sbuf(es2a, "sq2", [128, KC, TILE], BF16)
                rt = sbuf(es2a, "rt2", [128, TILE], F32)
                rstd = sbuf(es2a, "rstd2", [128, TILE], F32)
                load_wout(es2a, wa_out, stg2, wout_bf)
                kvstg = sbuf(es2a, "kvstg", [128, KC * 128], F32)

                def prefetch_kvw():
                    for part in range(3):
                        src = kvw[:].rearrange("p (kc c) -> p kc c", kc=KC)[:, :, part * 128:(part + 1) * 128]
                        S.op("sp", lambda e, src=src: e.dma_start(out=kvstg[:, :].rearrange("p (kc c) -> p kc c", kc=KC), in_=src),
                             writes=["kvstg"], dma="kvstg")

                        def castkv(e, part=part):
                            ins = None
                            sv = kvstg[:, :].rearrange("p (kc c) -> p kc c", kc=KC)
                            for kc in range(KC):
                                ins = e.activation(out=kvw_bf[:, kc, part * 128:(part + 1) * 128], in_=sv[:, kc, :], func=AF.Identity,
                                                   scale=gains_sb[:, KV_G + kc:KV_G + kc + 1])
                            return ins
                        S.op("act", castkv, reads=["kvstg", "gains"], writes=["kvw"])
                bctr = [0]
                wb = [0]
                pendingA = []
                for ti, (u0, u1) in enumerate(_tiles(NU, TILE)):
                    n = u1 - u0
                    s = ti % 2
                    sv = stg2[s][:, 0:KC * n].rearrange("p (kc t) -> p kc t", kc=KC)
                    S.op("sp", lambda e, sv=sv, u0=u0, u1=u1: e.dma_start(out=sv, in_=xT_v[:, :, 512 + u0:512 + u1]),
                         writes=["wstg%d" % s], dma="wstg%d" % s)
                    for oc in range(KC):
                        b = wb[0] % 6
                        wb[0] += 1

                        def mm(e, b=b, oc=oc, u0=u0, u1=u1, n=n):
                            ins = None
                            for kc in range(KC):
                                ins = e.matmul(psum[:, b, 0:n], lhsT=wout_bf[:, kc, oc * 128:(oc + 1) * 128],
                                               rhs=attnT[:, kc, u0:u1], start=(kc == 0), stop=(kc == KC - 1))
                            return ins
                        S.op("pe", mm, reads=["wout%d" % (oc // 2), "attnT"], writes=[bk(b)])
                        S.op("dve", lambda e, b=b, oc=oc, u0=u0, u1=u1, n=n, sv=sv: e.tensor_tensor(
                            out=h1T[:, oc, u0:u1], in0=psum[:, b, 0:n], in1=sv[:, oc, :], op=ALU.add),
                            reads=[bk(b), "wstg%d" % s], writes=["h1T_%d" % ti])
                    def finish(ti=ti, u0=u0, u1=u1, n=n):
                        stats_b(n, sq, rt, rstd, bctr)
                        S.op("dve", lambda e: e.tensor_tensor(
                            out=h1n[:, :, u0:u1], in0=h1T[:, :, u0:u1], in1=rstd[:, 0:n].unsqueeze(1).to_broadcast([128, KC, n]), op=ALU.mult),
                            reads=["h1T_%d" % ti, "rstd"], writes=["h1n"])
                    if ti == 3:
                        prefetch_kvw()
                    if pendingA:
                        pendingA.pop()()
                    stats_a(lambda u0=u0, u1=u1: h1T[:, :, u0:u1], "h1T_%d" % ti, n, sq)
                    pendingA.append(finish)
                pendingA.pop()()
                dbg_dump("h1T", lambda c0, c1: h1T[:].rearrange("p a b -> p (a b)")[:, c0:c1], KC * NU, ["h1n"])
                dbg_dump("h1n", lambda c0, c1: h1n[:].rearrange("p a b -> p (a b)")[:, c0:c1], KC * NU, ["h1n"])
                S.flush()

            with ExitStack() as es2b:
                stgB = [sbuf(es2b, "stgB%d" % i, [128, KC * 256], F32) for i in range(2)]
                wqg_bf = [sbuf(es2b, "wqg%d" % i, [128, KC, 256], BF16) for i in range(2)]
                KshT = [sbuf(es2b, "KshT%d" % g, [128, NU], BF16) for g in range(2)]
                Vsh = sbuf(es2b, "Vsh", [128, NKB_B, 128], BF16)
                QTb = sbuf(es2b, "QTb", [128, NW], BF16)
                SGb = sbuf(es2b, "SGb", [128, NW], BF16)
                PTb = [[sbuf(es2b, "PTb%d_%d" % (h, s), [128, 256], BF16) for s in range(RING_B)] for h in range(2)]
                bstgB = sbuf(es2b, "bstgB", [128, 512], F32)
                EB = [sbuf(es2b, "EB%d" % i, [128, 2, 256], BF16) for i in range(2)]
                TtmpB = [sbuf(es2b, "TtmpB%d" % i, [128, 512], F32) for i in range(2)]
                RtB = [sbuf(es2b, "RtB%d" % i, [128, 128], F32) for i in range(2)]
                OtB = [sbuf(es2b, "OtB%d" % i, [128, 128], F32) for i in range(2)]
                sctr = [0]

                def next_stgB():
                    s = sctr[0] % 2
                    sctr[0] += 1
                    return s

                def prefetch_B(hp):
                    s = next_stgB()
                    slot = hp % 2
                    S.op("sp", lambda e, s=s, hp=hp: e.dma_start(out=stgB[s][:, :], in_=wb_qg[hp]),
                         writes=["stgB%d" % s], dma="stgB%d" % s)

                    def cast(e, s=s, slot=slot):
                        ins = None
                        sv = stgB[s][:, :].rearrange("p (kc c) -> p kc c", kc=KC)
                        for kc in range(KC):
                            ins = e.activation(out=wqg_bf[slot][:, kc, :], in_=sv[:, kc, :], func=AF.Identity,
                                               scale=gains_sb[:, B_G + kc:B_G + kc + 1])
                        return ins
                    S.op("act", cast, reads=["stgB%d" % s, "gains"], writes=["wqg%d" % slot])
                    S.op("sp", lambda e, hp=hp: e.dma_start(out=bstgB[:, :], in_=biasB[hp]), writes=["bstgB"], dma="bstgB")
                    S.op("act", lambda e, slot=slot: e.activation(out=EB[slot][:].rearrange("p a b -> p (a b)"), in_=bstgB[:, :], func=AF.Exp),
                         reads=["bstgB"], writes=["EB%d" % slot])

                    def corners(e, slot=slot):
                        ins = None
                        for h in range(2):
                            e.memset(EB[slot][64:128, h, 0:64], 0.0)
                            ins = e.memset(EB[slot][0:64, h, 192:256], 0.0)
                        return ins
                    S.op("pool", corners, reads=[], writes=["EB%d" % slot])

                pbank = [0]

                def next_pb():
                    b = pbank[0]
                    pbank[0] = (b + 1) % 8
                    return b

                for g in range(2):
                    for (c0, c1) in _tiles(NU, 512):
                        b = next_pb()
                        n = c1 - c0

                        def mm(e, b=b, g=g, c0=c0, c1=c1, n=n):
                            ins = None
                            for kc in range(KC):
                                ins = e.matmul(psum[:, b, 0:n], lhsT=kvw_bf[:, kc, g * 128:(g + 1) * 128], rhs=h1n[:, kc, c0:c1],
                                               start=(kc == 0), stop=(kc == KC - 1))
                            return ins
                        S.op("pe", mm, reads=["kvw", "h1n"], writes=[bk(b)])
                        S.op("dve", lambda e, b=b, g=g, c0=c0, c1=c1, n=n: e.tensor_copy(out=KshT[g][:, c0:c1], in_=psum[:, b, 0:n]),
                             reads=[bk(b)], writes=["KshT%d" % g])
                for tb0 in range(0, NKB_B, 4):
                    tbs = [tb for tb in range(tb0, tb0 + 4) if tb < NKB_B]
                    b = next_pb()

                    def mmv(e, b=b, tbs=tbs):
                        ins = None
                        for i, tb in enumerate(tbs):
                            for kc in range(KC):
                                ins = e.matmul(psum[:, b, i * 128:(i + 1) * 128], lhsT=h1n[:, kc, tb * 128:(tb + 1) * 128],
                                               rhs=kvw_bf[:, kc, 256:384], start=(kc == 0), stop=(kc == KC - 1))
                        return ins
                    S.op("pe", mmv, reads=["kvw", "h1n"], writes=[bk(b)])
                    nt = len(tbs)
                    S.op("act", lambda e, b=b, tb0=tb0, nt=nt: e.activation(
                        out=Vsh[:, tb0:tb0 + nt, :], in_=psum[:, b, 0:nt * 128].rearrange("p (a c) -> p a c", a=nt), func=AF.Copy),
                        reads=[bk(b)], writes=["Vsh"])

                def projection_B(hp):
                    slot = hp % 2
                    wkey = "wqg%d" % slot
                    ti = 0
                    for (c0, c1) in _tiles(NW, 512):
                        b = next_pb()
                        n = c1 - c0

                        def mm(e, b=b, c0=c0, c1=c1, n=n):
                            ins = None
                            for kc in range(KC):
                                ins = e.matmul(psum[:, b, 0:n], lhsT=wqg_bf[slot][:, kc, 128:256], rhs=h1n[:, kc, 128 + c0:128 + c1],
                                               start=(kc == 0), stop=(kc == KC - 1))
                            return ins
                        S.op("pe", mm, reads=[wkey, "h1n"], writes=[bk(b)])
                        tt = ti % 2
                        ti += 1
                        S.op("act", lambda e, b=b, n=n, tt=tt: e.activation(out=TtmpB[tt][:, 0:n], in_=psum[:, b, 0:n], func=AF.Exp, scale=-1.0),
                             reads=[bk(b)], writes=["TtmpB%d" % tt])
                        S.op("act", lambda e, n=n, tt=tt: e.activation(out=TtmpB[tt][:, 0:n], in_=TtmpB[tt][:, 0:n], func=AF.Ln, bias=one_t[:, 0:1]),
                             reads=["TtmpB%d" % tt, "one"], writes=["TtmpB%d" % tt])
                        S.op("act", lambda e, n=n, tt=tt: e.activation(out=TtmpB[tt][:, 0:n], in_=TtmpB[tt][:, 0:n], func=AF.Exp, scale=-1.0),
                             reads=["TtmpB%d" % tt], writes=["TtmpB%d" % tt])
                        S.op("dve", lambda e, b=b, c0=c0, c1=c1, n=n, tt=tt: e.tensor_tensor(
                            out=SGb[:, c0:c1], in0=psum[:, b, 0:n], in1=TtmpB[tt][:, 0:n], op=ALU.mult),
                            reads=[bk(b), "TtmpB%d" % tt], writes=["SGb"])
                    for (c0, c1) in _tiles(NW, 512):
                        b = next_pb()
                        n = c1 - c0

                        def mm(e, b=b, c0=c0, c1=c1, n=n):
                            ins = None
                            for kc in range(KC):
                                ins = e.matmul(psum[:, b, 0:n], lhsT=wqg_bf[slot][:, kc, 0:128], rhs=h1n[:, kc, 128 + c0:128 + c1],
                                               start=(kc == 0), stop=(kc == KC - 1))
                            return ins
                        S.op("pe", mm, reads=[wkey, "h1n"], writes=[bk(b)])
                        S.op("dve", lambda e, b=b, c0=c0, c1=c1, n=n: e.tensor_copy(out=QTb[:, c0:c1], in_=psum[:, b, 0:n]),
                             reads=[bk(b)], writes=["QTb"])

                ndb = [0]

                def attention_B(hp):
                    slot = hp % 2
                    g = hp // 4
                    ekey = "EB%d" % slot

                    def pv(jw):
                        i = ndb[0] % 2
                        ndb[0] += 1
                        nb, db = 4 + i, 6 + i

                        def mm(e, jw=jw, nb=nb, db=db):
                            ins = None
                            kbs = [jw, jw + 1]
                            for idx, kb2 in enumerate(kbs):
                                col = (jw + 1 - kb2) * 128
                                st, sp_ = (idx == 0), (idx == len(kbs) - 1)
                                r = kb2 % RING_B
                                for h in range(2):
                                    e.matmul(psum[h * 64:(h + 1) * 64, nb, 0:128], lhsT=Vsh[:, kb2, g * 64:(g + 1) * 64],
                                             rhs=PTb[h][r][:, col:col + 128], start=st, stop=sp_)
                                for h in range(2):
                                    ins = e.matmul(psum[h * 64:(h + 1) * 64, db, 0:128], lhsT=validB_sb[:, kb2, :],
                                                   rhs=PTb[h][r][:, col:col + 128], start=st, stop=sp_)
                            return ins
                        rd = ["Vsh", "validB"] + ["PTb%d_%d" % (h, kb2 % RING_B) for h in range(2) for kb2 in (jw, jw + 1)]
                        S.op("pe", mm, reads=rd, writes=[bk(nb), bk(db)])
                        S.op("act", lambda e, i=i, db=db: e.activation(out=RtB[i][:, :], in_=psum[:, db, 0:128], func=AF.Ln, bias=esink[:, hp:hp + 1]),
                             reads=[bk(db), "esink"], writes=["RtB%d" % i])
                        S.op("act", lambda e, i=i: e.activation(out=RtB[i][:, :], in_=RtB[i][:, :], func=AF.Exp, scale=-1.0),
                             reads=["RtB%d" % i], writes=["RtB%d" % i])
                        S.op("dve", lambda e, i=i, nb=nb: e.tensor_tensor(out=OtB[i][:, :], in0=psum[:, nb, 0:128], in1=RtB[i][:, :], op=ALU.mult),
                             reads=[bk(nb), "RtB%d" % i], writes=["OtB%d" % i])
                        S.op("pool", lambda e, i=i, jw=jw: e.tensor_tensor(out=attnT[:, hp, jw * 128:(jw + 1) * 128], in0=OtB[i][:, :],
                                                                           in1=SGb[:, jw * 128:(jw + 1) * 128], op=ALU.mult),
                             reads=["OtB%d" % i, "SGb"], writes=["attnT"])

                    for kb in range(NKB_B):
                        jmin, jmax = max(kb - 1, 0), min(kb, NQB_B - 1)
                        c0, c1 = (jmin + 1 - kb) * 128, (jmax + 1 - kb + 1) * 128
                        w0 = jmin * 128
                        r = kb % RING_B
                        for h in range(2):
                            S.op("pe", lambda e, h=h, kb=kb, c0=c0, c1=c1, w0=w0: e.matmul(
                                psum[:, 2 * h, c0:c1], lhsT=KshT[g][h * 64:(h + 1) * 64, kb * 128:(kb + 1) * 128],
                                rhs=QTb[h * 64:(h + 1) * 64, w0:w0 + (c1 - c0)], start=True, stop=True),
                                reads=["QTb", "KshT%d" % g], writes=[bk(2 * h)])
                            S.op("act", lambda e, h=h, r=r, c0=c0, c1=c1: e.activation(
                                out=PTb[h][r][:, c0:c1], in_=psum[:, 2 * h, c0:c1], func=AF.Exp, scale=0.125),
                                reads=[bk(2 * h)], writes=["PTb%d_%d" % (h, r)])
                            S.op("dve", lambda e, h=h, r=r, c0=c0, c1=c1: e.tensor_tensor(
                                out=PTb[h][r][:, c0:c1], in0=PTb[h][r][:, c0:c1], in1=EB[slot][:, h, c0:c1], op=ALU.mult),
                                reads=["PTb%d_%d" % (h, r), ekey], writes=["PTb%d_%d" % (h, r)])
                        jw = kb - 2
                        if jw >= 0:
                            pv(jw)
                    for jw in range(NKB_B - 2, NQB_B):
                        pv(jw)

                prefetch_B(0)
                for hp in range(8):
                    if hp + 1 < 8:
                        prefetch_B(hp + 1)
                    projection_B(hp)
                    attention_B(hp)
                dbg_dump("attnB", lambda c0, c1: attnT[:, c0 // NW, c0 % NW:c0 % NW + (c1 - c0)], KC * NW, ["attnT"])
                S.flush()

            with ExitStack() as es2c:
                TB = 256
                wstg = [sbuf(es2c, "wstgC%d" % i, [128, 2048], F32) for i in range(2)]
                wout_bf = sbuf(es2c, "woutB_bf", [128, KC, 1024], BF16)
                h2 = [sbuf(es2c, "h2_%d" % i, [128, KC, TB], F32) for i in range(3)]
                sq = sbuf(es2c, "sq3", [128, KC, TB], BF16)
                rt = sbuf(es2c, "rt3", [128, TB], F32)
                rstd = sbuf(es2c, "rstd3", [128, TB], F32)
                load_wout(es2c, wb_out, wstg, wout_bf)
                bctr = [0]
                wb = [0]
                pendingB = []
                for ti, (w0, w1) in enumerate(_tiles(NW, TB)):
                    n = w1 - w0
                    s = ti % 3
                    for oc in range(KC):
                        b = wb[0] % 6
                        wb[0] += 1

                        def mm(e, b=b, oc=oc, w0=w0, w1=w1, n=n):
                            ins = None
                            for kc in range(KC):
                                ins = e.matmul(psum[:, b, 0:n], lhsT=wout_bf[:, kc, oc * 128:(oc + 1) * 128],
                                               rhs=attnT[:, kc, w0:w1], start=(kc == 0), stop=(kc == KC - 1))
                            return ins
                        S.op("pe", mm, reads=["wout%d" % (oc // 2), "attnT"], writes=[bk(b)])
                        S.op("dve", lambda e, b=b, oc=oc, w0=w0, w1=w1, n=n, s=s: e.tensor_tensor(
                            out=h2[s][:, oc, 0:n], in0=psum[:, b, 0:n], in1=h1T[:, oc, 128 + w0:128 + w1], op=ALU.add),
                            reads=[bk(b)], writes=["h2_%d" % s])
                    def finishB(s=s, w0=w0, w1=w1, n=n):
                        stats_b(n, sq, rt, rstd, bctr)

                        def fin(e):
                            ins = None
                            for oc in range(KC):
                                ins = e.scalar_tensor_tensor(out=h2[s][:, oc, 0:n], in0=h2[s][:, oc, 0:n],
                                                             scalar=gains_sb[:, F_G + oc:F_G + oc + 1], in1=rstd[:, 0:n],
                                                             op0=ALU.mult, op1=ALU.mult)
                            return ins
                        S.op("dve", fin, reads=["h2_%d" % s, "rstd", "gains"], writes=["h2_%d" % s])
                        tok = S.op("sp", lambda e: e.dma_start(out=outT_v[:, :, w0:w1], in_=h2[s][:, :, 0:n]),
                                   reads=["h2_%d" % s], dma="out%d" % s)
                        out_tokens.append(tok)
                    if pendingB:
                        pendingB.pop()()
                    stats_a(lambda s=s, n=n: h2[s][:, :, 0:n], "h2_%d" % s, n, sq)
                    pendingB.append(finishB)
                pendingB.pop()()
                final = {}
                for key, val in out_tokens:
                    final[key] = max(final.get(key, 0), val)
                S.wait_tokens("sp", list(final.items()))
                S.flush()
    return nc


def _t5_bucket_np(rel):
    nb = 16
    max_exact = 8
    ret = np.where(rel > 0, nb, 0)
    n = np.abs(rel)
    nf = np.maximum(n, 1).astype(np.float32)
    large = max_exact + (np.log(nf / np.float32(max_exact)) / np.float32(math.log(128 / max_exact))
                         * np.float32(nb - max_exact)).astype(np.int32)
    large = np.minimum(large, nb - 1)
    return ret + np.where(n < max_exact, n, large)


def _prep_shared(a_norm, a_w_in, a_rel_bias, a_w_out, kv_norm, kv_w, t5_bias, b_norm, b_w_in, b_sinks, b_w_out, final_norm):
    f = np.float32
    w_in = np.asarray(a_w_in[0], f)
    w4 = w_in.reshape(KC, 128, 4, 8, 128)
    wa_qkg = np.ascontiguousarray(np.transpose(w4[:, :, [0, 1, 3]], (3, 1, 0, 2, 4))).reshape(8, 128, KC * 384)
    wv = w_in[:, 2048:3072].reshape(KC, 128, 4, 256)
    wa_v = np.ascontiguousarray(np.transpose(wv, (2, 1, 0, 3))).reshape(4, 128, KC * 256)
    wa_out = np.ascontiguousarray(np.transpose(np.asarray(a_w_out[0], f).reshape(KC, 128, 4, 256), (1, 2, 0, 3))).reshape(128, KC * 1024)
    wb_out = np.ascontiguousarray(np.transpose(np.asarray(b_w_out[0], f).reshape(KC, 128, 4, 256), (1, 2, 0, 3))).reshape(128, KC * 1024)

    def gcol(v):
        return np.asarray(v, f).reshape(KC, 128).T
    gains = np.ascontiguousarray(np.concatenate([gcol(a_norm[0]), gcol(kv_norm), gcol(b_norm[0]), gcol(final_norm)], axis=1))
    k = np.arange(128)[:, None]
    q = np.arange(640)[None, :]
    idxA = np.clip(q - k, -256, 256) + 256
    rb = np.asarray(a_rel_bias[0], f)
    bA = rb[idxA]
    biasA = np.ascontiguousarray(np.transpose(bA.reshape(128, 640, 8, 2), (2, 0, 3, 1))).reshape(8, 128, 1280)
    kvw_ = np.asarray(kv_w, f).reshape(KC, 128, 256)
    kcat = np.concatenate([kvw_[:, :, 0:64], kvw_[:, :, 0:64], kvw_[:, :, 64:128], kvw_[:, :, 64:128], kvw_[:, :, 128:256]], axis=2)
    kvw = np.ascontiguousarray(np.transpose(kcat, (1, 0, 2))).reshape(128, KC * 384)
    wb = np.asarray(b_w_in[0], f).reshape(KC, 128, 2, 8, 128)
    wb_qg = np.ascontiguousarray(np.transpose(wb, (3, 1, 0, 2, 4))).reshape(8, 128, KC * 256)
    qb = np.arange(256)[None, :]
    bucket = _t5_bucket_np((k - qb).astype(np.int32))
    tb = np.asarray(t5_bias, f)[bucket]
    biasB = np.ascontiguousarray(np.transpose(tb.reshape(128, 256, 8, 2), (2, 0, 3, 1))).reshape(8, 128, 512)
    sk = np.asarray(b_sinks[0], f).reshape(8, 2)
    sinkB = np.ascontiguousarray(np.repeat(sk.T, 64, axis=0))
    return dict(wa_qkg=wa_qkg, wa_v=wa_v, wa_out=wa_out, gains=gains, biasA=biasA, kvw=kvw, wb_qg=wb_qg,
                wb_out=wb_out, biasB=biasB, sinkB=sinkB)


def _prep_core(x, c):
    b, half = c // 2, c % 2
    T0 = half * NOWN
    lo = T0 - HALO
    xe = np.zeros((NTA, D), np.float32)
    src0 = max(lo, 0)
    xe[src0 - lo:, :] = x[b, src0:T0 + NOWN, :]
    xTc = np.ascontiguousarray(xe.T)
    tpos = lo + np.arange(NTA)
    vA = (tpos >= 0).astype(np.float32).reshape(NKB_A, 128).T
    validA = np.ascontiguousarray(np.repeat(vA[:, :, None], 64, axis=2)).reshape(128, NKB_A * 64).astype(ml_dtypes.bfloat16)
    upos = T0 - 128 + np.arange(NU)
    vB = (upos >= 0).astype(np.float32).reshape(NKB_B, 128).T
    validB = np.ascontiguousarray(np.repeat(vB[:, :, None], 64, axis=2)).reshape(128, NKB_B * 64).astype(ml_dtypes.bfloat16)
    return dict(xT=xTc, validA=validA, validB=validB)


_NC_CACHE = {}


def kernel(x, a_norm, a_w_in, a_rel_bias, a_w_out, kv_norm, kv_w, t5_bias, b_norm, b_w_in, b_sinks, b_w_out, final_norm,
           _debug=()):
    x = np.asarray(x, np.float32)
    shared = _prep_shared(np.asarray(a_norm), np.asarray(a_w_in), np.asarray(a_rel_bias), np.asarray(a_w_out),
                          np.asarray(kv_norm), np.asarray(kv_w), np.asarray(t5_bias), np.asarray(b_norm),
                          np.asarray(b_w_in), np.asarray(b_sinks), np.asarray(b_w_out), np.asarray(final_norm))
    in_maps = []
    for c in range(N_CORES):
        m = dict(shared)
        m.update(_prep_core(x, c))
        in_maps.append(m)
    key = tuple(_debug)
    if key not in _NC_CACHE:
        _NC_CACHE[key] = build_nc(debug=key)
    nc = _NC_CACHE[key]
    res = run_bass_kernel_spmd(nc, in_maps, core_ids=list(range(N_CORES)))
    out = np.empty((4, SEQ, D), np.float32)
    for c in range(N_CORES):
        b, half = c // 2, c % 2
        out[b, half * NOWN:(half + 1) * NOWN, :] = np.asarray(res.results[c]["outT"]).T
    if _debug:
        return out, res.results
    return out
```

```python
import math
from contextlib import ExitStack

import numpy as np
import ml_dtypes

import concourse.bass as bass
import concourse.mybir as mybir
from concourse.bass_utils import run_bass_kernel_spmd

F32 = mybir.dt.float32
BF16 = mybir.dt.bfloat16
AF = mybir.ActivationFunctionType
ALU = mybir.AluOpType

N_CORES = 8
D = 1024
KC = 8
SEQ = 4096
NOWN = 2048
HALO = 640
NTA = NOWN + HALO
NU = NOWN + 128
NW = NOWN
NKB_A = NTA // 128
NQB_A = NU // 128
NKB_B = NU // 128
NQB_B = NW // 128
RING_A = 6
RING_B = 4
TILE = 384
EPS = 1e-6

ENGS = ("pe", "act", "dve", "pool", "sp")


class Sched:
    def __init__(self, nc, es):
        self.nc = nc
        self.es = es
        self.sem = {e: es.enter_context(nc.semaphore("s_" + e)) for e in ENGS}
        self.cnt = {e: 0 for e in ENGS}
        self.dsem = {}
        self.dcnt = {}
        self.lastw = {}
        self.readers = {}
        self.pending = {e: [] for e in ENGS}
        self.seen = {e: {} for e in ENGS}
        self.know = {}
        self.order = {}
        self.nwaits = 0

    def op(self, eng, fn, reads=(), writes=(), dma=None, ndma=1):
        deps = {}

        def add(tok):
            if tok is None:
                return
            key, val = tok
            if deps.get(key, 0) < val:
                deps[key] = val

        for r in reads:
            add(self.lastw.get(r))
            if r.startswith("bk") and eng in ("act", "dve"):
                for k, v in self.readers.get(r, {}).items():
                    if k[0] in ("act", "dve") and k[0] != eng:
                        add((k, v))
        for w in writes:
            add(self.lastw.get(w))
            for k, v in self.readers.get(w, {}).items():
                add((k, v))
        if dma is not None:
            if dma not in self.dsem:
                self.dsem[dma] = self.es.enter_context(self.nc.semaphore("d_" + str(dma)))
                self.dcnt[dma] = 0
            self.dcnt[dma] += 16 * ndma
            tok = (("dma", dma), self.dcnt[dma])
        else:
            self.cnt[eng] += 1
            tok = ((eng,), self.cnt[eng])
        waits = []
        for key, val in sorted(deps.items(), key=lambda kv: -self.order.get(kv, 0)):
            if key == ("pe",) and eng == "pe":
                continue
            if self.seen[eng].get(key, 0) >= val:
                continue
            waits.append((key, val))
            for k2, v2 in self.know.get((key, val), {}).items():
                if self.seen[eng].get(k2, 0) < v2:
                    self.seen[eng][k2] = v2
            self.seen[eng][key] = val
        self.nwaits += len(waits)
        self.order[tok] = len(self.order) + 1
        if dma is None:
            kn = dict(self.seen[eng])
            kn[tok[0]] = tok[1]
            self.know[tok] = kn
        else:
            self.know[tok] = dict(self.seen[eng])
        self.pending[eng].append((fn, waits, tok))
        for r in reads:
            d = self.readers.setdefault(r, {})
            if d.get(tok[0], 0) < tok[1]:
                d[tok[0]] = tok[1]
        for w in writes:
            self.lastw[w] = tok
            self.readers[w] = {}
        return tok

    def _semof(self, key):
        if key[0] == "dma":
            return self.dsem[key[1]]
        return self.sem[key[0]]

    def wait_tokens(self, eng, toks):
        self.pending[eng].append((None, list(toks), None))

    def flush(self):
        nc = self.nc
        pend = self.pending
        self.pending = {e: [] for e in ENGS}
        with nc.Block() as block:
            def mk(engname):
                lst = pend[engname]

                def body(e):
                    class _First:
                        def __init__(self, eng):
                            self._eng = eng
                            self.first = None

                        def __getattr__(self, name):
                            real = getattr(self._eng, name)

                            def call(*a, **k):
                                r = real(*a, **k)
                                if self.first is None:
                                    self.first = r
                                return r
                            return call

                    for fn, waits, tok in lst:
                        fuse = fn is not None and len(waits) >= 1
                        for key, val in (waits[:-1] if fuse else waits):
                            e.wait_ge(self._semof(key), val)
                        if fn is None:
                            continue
                        if fuse:
                            px = _First(e)
                            ins = fn(px)
                            px.first._wait_ge(self._semof(waits[-1][0]), waits[-1][1])
                        else:
                            ins = fn(e)
                        key, val = tok
                        if key[0] == "dma":
                            if not isinstance(ins, (list, tuple)):
                                ins = [ins]
                            for i in ins:
                                i.then_inc(self.dsem[key[1]], 16)
                        else:
                            ins.then_inc(self.sem[engname], 1)
                return body

            if pend["pe"]:
                block.tensor(mk("pe"))
            if pend["act"]:
                block.scalar(mk("act"))
            if pend["dve"]:
                block.vector(mk("dve"))
            if pend["pool"]:
                block.gpsimd(mk("pool"))
            if pend["sp"]:
                block.sync(mk("sp"))


def _tiles(n, step):
    return [(a, min(a + step, n)) for a in range(0, n, step)]


def build_nc(debug=()):
    nc = bass.Bass("TRN2", target_bir_lowering=False)

    def din(name, shape, dt=F32):
        return nc.dram_tensor(name, shape, dt, kind="ExternalInput").ap()

    xT = din("xT", [D, NTA])
    wa_qkg = din("wa_qkg", [8, 128, KC * 384])
    wa_v = din("wa_v", [4, 128, KC * 256])
    wa_out = din("wa_out", [128, KC * 1024])
    gains = din("gains", [128, 32])
    biasA = din("biasA", [8, 128, 1280])
    validA = din("validA", [128, NKB_A * 64], BF16)
    kvw = din("kvw", [128, KC * 384])
    wb_qg = din("wb_qg", [8, 128, KC * 256])
    wb_out = din("wb_out", [128, KC * 1024])
    biasB = din("biasB", [8, 128, 512])
    sinkB = din("sinkB", [128, 8])
    validB = din("validB", [128, NKB_B * 64], BF16)
    outT = nc.dram_tensor("outT", [D, NW], F32, kind="ExternalOutput").ap()
    dbg_out = {}
    dbg_shapes = {"xTn": [128, KC * NTA], "attnA": [128, KC * NU], "h1T": [128, KC * NU],
                  "h1n": [128, KC * NU], "attnB": [128, KC * NW], "QT0": [128, NU], "KT0": [128, NTA],
                  "SG0": [128, NU], "V0": [128, NKB_A * 256]}
    for name in debug:
        dbg_out[name] = nc.dram_tensor("dbg_" + name, dbg_shapes[name], F32, kind="ExternalOutput").ap()

    xT_v = xT.rearrange("(kc p) t -> p kc t", p=128)
    outT_v = outT.rearrange("(kc p) t -> p kc t", p=128)

    with ExitStack() as es0:
        S = Sched(nc, es0)
        out_tokens = []

        def sbuf(es, name, shape, dt):
            return es.enter_context(nc.sbuf_tensor(name, shape, dt))

        psum = es0.enter_context(nc.psum_tensor("psum", [128, 8, 512], F32))

        def bk(b):
            return "bk%d" % b

        ones_bf = sbuf(es0, "ones_bf", [128, 128], BF16)
        eps_t = sbuf(es0, "eps_t", [128, 1], F32)
        one_t = sbuf(es0, "one_t", [128, 1], F32)
        eps60 = sbuf(es0, "eps60", [128, 1], F32)
        gains_sb = sbuf(es0, "gains_sb", [128, 32], F32)
        sink_sb = sbuf(es0, "sink_sb", [128, 8], F32)
        esink = sbuf(es0, "esink", [128, 8], F32)
        validB_sb = sbuf(es0, "validB_sb", [128, NKB_B, 64], BF16)
        attnT = sbuf(es0, "attnT", [128, KC, NU], BF16)
        dbgf = sbuf(es0, "dbgf", [128, 512], F32) if debug else None

        def dbg_dump(name, src_fn, ncols, key_reads):
            if name not in dbg_out:
                return
            for (c0, c1) in _tiles(ncols, 512):
                S.op("dve", lambda e, c0=c0, c1=c1: e.tensor_copy(out=dbgf[:, 0:c1 - c0], in_=src_fn(c0, c1)),
                     reads=key_reads, writes=["dbgf"])
                tok = S.op("sp", lambda e, c0=c0, c1=c1: e.dma_start(out=dbg_out[name][:, c0:c1], in_=dbgf[:, 0:c1 - c0]),
                           reads=["dbgf"], dma="dbg")
                out_tokens.append(tok)

        S.op("pool", lambda e: e.memset(ones_bf[:], 1.0), writes=["ones"])
        S.op("pool", lambda e: e.memset(eps_t[:], EPS), writes=["eps"])
        S.op("pool", lambda e: e.memset(one_t[:], 1.0), writes=["one"])
        S.op("pool", lambda e: e.memset(eps60[:], 2.0 ** -60), writes=["eps60"])
        S.op("sp", lambda e: e.dma_start(out=gains_sb[:], in_=gains[:]), writes=["gains"], dma="c0")
        S.op("sp", lambda e: e.dma_start(out=sink_sb[:], in_=sinkB[:]), writes=["sink"], dma="c1")
        S.op("sp", lambda e: e.dma_start(out=validB_sb[:].rearrange("p a b -> p (a b)"), in_=validB[:]),
             writes=["validB"], dma="c2")
        S.op("act", lambda e: e.activation(out=esink[:], in_=sink_sb[:], func=AF.Exp), reads=["sink"], writes=["esink"])

        A_G, KV_G, B_G, F_G = 0, 8, 16, 24

        with ExitStack() as es1:
            xT_bf = sbuf(es1, "xT_bf", [128, KC, NTA], BF16)
            validA_sb = sbuf(es1, "validA_sb", [128, NKB_A, 64], BF16)
            stg = [sbuf(es1, "stg%d" % i, [128, KC * TILE], F32) for i in range(2)]
            sq = sbuf(es1, "sq", [128, KC, TILE], BF16)
            rt = sbuf(es1, "rt", [128, TILE], F32)
            rstd = sbuf(es1, "rstd", [128, TILE], F32)
            bstg = sbuf(es1, "bstg", [128, 1280], F32)
            E_sb = [sbuf(es1, "E%d" % i, [128, 2, 640], BF16) for i in range(2)]
            wqkg_bf = [sbuf(es1, "wqkg%d" % i, [128, KC, 384], BF16) for i in range(2)]
            wv_bf = sbuf(es1, "wv_bf", [128, KC, 256], BF16)
            QT = sbuf(es1, "QT", [128, NU], BF16)
            KT = sbuf(es1, "KT", [128, NTA], BF16)
            SG = sbuf(es1, "SG", [128, NU], BF16)
            V_sb = sbuf(es1, "V_sb", [128, NKB_A, 256], BF16)
            PT = [[sbuf(es1, "PT%d_%d" % (h, s), [128, 640], BF16) for s in range(RING_A)] for h in range(2)]
            Ttmp = [sbuf(es1, "Ttmp%d" % i, [128, 512], F32) for i in range(2)]
            Rt = [sbuf(es1, "Rt%d" % i, [128, 128], F32) for i in range(2)]
            Ot = [sbuf(es1, "Ot%d" % i, [128, 128], F32) for i in range(2)]

            S.op("sp", lambda e: e.dma_start(out=validA_sb[:].rearrange("p a b -> p (a b)"), in_=validA[:]),
                 writes=["validA"], dma="c3")

            stg_ctr = [0]

            def next_stg():
                s = stg_ctr[0] % 2
                stg_ctr[0] += 1
                return s

            XT_TILES = _tiles(NTA, TILE)

            def xk(a, b_):
                return ["xT_%d" % i for i, (t0, t1) in enumerate(XT_TILES) if t0 < b_ and t1 > a]
            ALLX = xk(0, NTA)

            stat_bank = [4]

            def preamble(after_tile=None):
              for ti_, (t0, t1) in enumerate(XT_TILES):
                n = t1 - t0
                s = next_stg()
                sv = stg[s][:, 0:KC * n].rearrange("p (kc t) -> p kc t", kc=KC)
                S.op("sp", lambda e, sv=sv, t0=t0, t1=t1: e.dma_start(out=sv, in_=xT_v[:, :, t0:t1]),
                     writes=["stg%d" % s], dma="stg%d" % s)
                S.op("act", lambda e, sv=sv, n=n: e.activation(out=sq[:, :, 0:n], in_=sv, func=AF.Square),
                     reads=["stg%d" % s], writes=["sq"])
                b = stat_bank[0]
                stat_bank[0] = 4 + (stat_bank[0] - 3) % 4

                def stat_mm(e, b=b, n=n):
                    ins = None
                    for kc in range(KC):
                        ins = e.matmul(psum[:, b, 0:n], lhsT=ones_bf[:, :], rhs=sq[:, kc, 0:n],
                                       start=(kc == 0), stop=(kc == KC - 1))
                    return ins
                S.op("pe", stat_mm, reads=["sq", "ones"], writes=[bk(b)])
                S.op("act", lambda e, b=b, n=n: e.activation(out=rt[:, 0:n], in_=psum[:, b, 0:n], func=AF.Ln,
                                                             scale=1.0 / D, bias=eps_t[:, 0:1]),
                     reads=[bk(b), "eps"], writes=["rt"])
                S.op("act", lambda e, n=n: e.activation(out=rstd[:, 0:n], in_=rt[:, 0:n], func=AF.Exp, scale=-0.5),
                     reads=["rt"], writes=["rstd"])
                S.op("dve", lambda e, sv=sv, t0=t0, t1=t1, n=n: e.tensor_tensor(
                    out=xT_bf[:, :, t0:t1], in0=sv, in1=rstd[:, 0:n].unsqueeze(1).to_broadcast([128, KC, n]), op=ALU.mult),
                    reads=["stg%d" % s, "rstd"], writes=["xT_%d" % ti_])
                if after_tile is not None:
                    after_tile(t1)

            def prefetch_A(hp):
                s = next_stg()
                slot = hp % 2
                S.op("sp", lambda e, s=s, hp=hp: e.dma_start(out=stg[s][:, 0:KC * 384], in_=wa_qkg[hp]),
                     writes=["stg%d" % s], dma="stg%d" % s)

                def cast(e, s=s, slot=slot):
                    ins = None
                    sv = stg[s][:, 0:KC * 384].rearrange("p (kc c) -> p kc c", kc=KC)
                    for kc in range(KC):
                        ins = e.activation(out=wqkg_bf[slot][:, kc, :], in_=sv[:, kc, :], func=AF.Identity,
                                           scale=gains_sb[:, A_G + kc:A_G + kc + 1])
                    return ins
                S.op("act", cast, reads=["stg%d" % s, "gains"], writes=["wqkg%d" % slot])
                if hp % 2 == 0:
                    s2 = next_stg()
                    S.op("sp", lambda e, s2=s2, hp=hp: e.dma_start(out=stg[s2][:, 0:KC * 256], in_=wa_v[hp // 2]),
                         writes=["stg%d" % s2], dma="stg%d" % s2)

                    def castv(e, s2=s2):
                        ins = None
                        sv = stg[s2][:, 0:KC * 256].rearrange("p (kc c) -> p kc c", kc=KC)
                        for kc in range(KC):
                            ins = e.activation(out=wv_bf[:, kc, :], in_=sv[:, kc, :], func=AF.Identity,
                                               scale=gains_sb[:, A_G + kc:A_G + kc + 1])
                        return ins
                    S.op("act", castv, reads=["stg%d" % s2, "gains"], writes=["wv"])
                S.op("sp", lambda e, hp=hp: e.dma_start(out=bstg[:, :], in_=biasA[hp]), writes=["bstg"], dma="bstg")
                S.op("act", lambda e, slot=slot: e.activation(out=E_sb[slot][:].rearrange("p a b -> p (a b)"), in_=bstg[:, :], func=AF.Exp),
                     reads=["bstg"], writes=["E%d" % slot])

                def corners(e, slot=slot):
                    ins = None
                    for h in range(2):
                        e.memset(E_sb[slot][64:128, h, 0:64], 0.0)
                        ins = e.memset(E_sb[slot][0:64, h, 576:640], 0.0)
                    return ins
                S.op("pool", corners, reads=[], writes=["E%d" % slot])

            proj_bank = [0]
            proj_nbanks = [4]

            def next_pbank():
                b = proj_bank[0] % proj_nbanks[0]
                proj_bank[0] = (b + 1) % proj_nbanks[0]
                return b

            def proj_fm(wslot_ap_fn, rhs_fn, ncols_tiles, evac):
                for (c0, c1) in ncols_tiles:
                    b = next_pbank()
                    n = c1 - c0

                    def mm(e, b=b, c0=c0, c1=c1, n=n):
                        ins = None
                        for kc in range(KC):
                            ins = e.matmul(psum[:, b, 0:n], lhsT=wslot_ap_fn(kc), rhs=rhs_fn(kc, c0, c1),
                                           start=(kc == 0), stop=(kc == KC - 1))
                        return ins
                    yield b, c0, c1, n, mm

            def projection_items(hp):
                slot = hp % 2
                wkey = "wqkg%d" % slot
                items = []
                tcount = [0]

                def fm_tile(kind, c0, c1):
                    n = c1 - c0
                    wcol = {"g": 256, "q": 0, "k": 128}[kind]
                    xoff = 0 if kind == "k" else 512

                    def emit():
                        b = next_pbank()

                        def mm(e):
                            ins = None
                            for kc in range(KC):
                                ins = e.matmul(psum[:, b, 0:n], lhsT=wqkg_bf[slot][:, kc, wcol:wcol + 128],
                                               rhs=xT_bf[:, kc, xoff + c0:xoff + c1], start=(kc == 0), stop=(kc == KC - 1))
                            return ins
                        S.op("pe", mm, reads=[wkey] + xk(xoff + c0, xoff + c1), writes=[bk(b)])
                        if kind == "g":
                            tt = tcount[0] % 2
                            tcount[0] += 1
                            S.op("act", lambda e: e.activation(out=Ttmp[tt][:, 0:n], in_=psum[:, b, 0:n], func=AF.Exp, scale=-1.0),
                                 reads=[bk(b)], writes=["Ttmp%d" % tt])
                            S.op("act", lambda e: e.activation(out=Ttmp[tt][:, 0:n], in_=Ttmp[tt][:, 0:n], func=AF.Ln, bias=one_t[:, 0:1]),
                                 reads=["Ttmp%d" % tt, "one"], writes=["Ttmp%d" % tt])
                            S.op("act", lambda e: e.activation(out=Ttmp[tt][:, 0:n], in_=Ttmp[tt][:, 0:n], func=AF.Exp, scale=-1.0),
                                 reads=["Ttmp%d" % tt], writes=["Ttmp%d" % tt])
                            S.op("dve", lambda e: e.tensor_tensor(out=SG[:, c0:c1], in0=psum[:, b, 0:n], in1=Ttmp[tt][:, 0:n], op=ALU.mult),
                                 reads=[bk(b), "Ttmp%d" % tt], writes=["SG"])
                        elif kind == "q":
                            S.op("dve", lambda e: e.tensor_copy(out=QT[:, c0:c1], in_=psum[:, b, 0:n]), reads=[bk(b)], writes=["QT"])
                        else:
                            S.op("dve", lambda e: e.tensor_copy(out=KT[:, c0:c1], in_=psum[:, b, 0:n]), reads=[bk(b)], writes=["KT"])
                    return (xoff + c1, emit)

                def v_tile(tb0):
                    tbs = [tb for tb in (tb0, tb0 + 1) if tb < NKB_A]
                    nt = len(tbs)

                    def emit():
                        b = next_pbank()

                        def mmv(e):
                            ins = None
                            for i, tb in enumerate(tbs):
                                for kc in range(KC):
                                    ins = e.matmul(psum[:, b, i * 256:(i + 1) * 256], lhsT=xT_bf[:, kc, tb * 128:(tb + 1) * 128],
                                                   rhs=wv_bf[:, kc, :], start=(kc == 0), stop=(kc == KC - 1))
                            return ins
                        S.op("pe", mmv, reads=["wv"] + xk(tbs[0] * 128, (tbs[-1] + 1) * 128), writes=[bk(b)])
                        S.op("act", lambda e: e.activation(
                            out=V_sb[:, tb0:tb0 + nt, :], in_=psum[:, b, 0:nt * 256].rearrange("p (a c) -> p a c", a=nt), func=AF.Copy),
                            reads=[bk(b)], writes=["V"])
                    return ((tbs[-1] + 1) * 128, emit)

                for (c0, c1) in _tiles(NU, 512):
                    items.append(fm_tile("g", c0, c1))
                for (c0, c1) in _tiles(NU, 512):
                    items.append(fm_tile("q", c0, c1))
                for (c0, c1) in _tiles(NTA, 512):
                    items.append(fm_tile("k", c0, c1))
                if hp % 2 == 0:
                    for tb0 in range(0, NKB_A, 2):
                        items.append(v_tile(tb0))
                return items

            def projection_A(hp):
                for need, emit in projection_items(hp):
                    emit()

            nd_ctr = [0]

            def attention_A(hp):
                slot = hp % 2
                hpl = hp % 2
                ekey = "E%d" % slot

                def pv(ju):
                    i = nd_ctr[0] % 2
                    nd_ctr[0] += 1
                    nb, db = 4 + i, 6 + i

                    def mm(e, ju=ju, nb=nb, db=db):
                        ins = None
                        kbs = list(range(ju, ju + 5))
                        for idx, kb2 in enumerate(kbs):
                            col = (ju + 4 - kb2) * 128
                            st, sp_ = (idx == 0), (idx == len(kbs) - 1)
                            r = kb2 % RING_A
                            for h in range(2):
                                e.matmul(psum[h * 64:(h + 1) * 64, nb, 0:128],
                                         lhsT=V_sb[:, kb2, hpl * 128 + h * 64:hpl * 128 + (h + 1) * 64],
                                         rhs=PT[h][r][:, col:col + 128], start=st, stop=sp_)
                            for h in range(2):
                                ins = e.matmul(psum[h * 64:(h + 1) * 64, db, 0:128], lhsT=validA_sb[:, kb2, :],
                                               rhs=PT[h][r][:, col:col + 128], start=st, stop=sp_)
                        return ins
                    rd = ["V", "validA"] + ["PT%d_%d" % (h, kb2 % RING_A) for h in range(2) for kb2 in range(ju, ju + 5)]
                    S.op("pe", mm, reads=rd, writes=[bk(nb), bk(db)])
                    S.op("act", lambda e, i=i, db=db: e.activation(out=Rt[i][:, :], in_=psum[:, db, 0:128], func=AF.Ln, bias=eps60[:, 0:1]),
                         reads=[bk(db), "eps60"], writes=["Rt%d" % i])
                    S.op("act", lambda e, i=i: e.activation(out=Rt[i][:, :], in_=Rt[i][:, :], func=AF.Exp, scale=-1.0),
                         reads=["Rt%d" % i], writes=["Rt%d" % i])
                    S.op("dve", lambda e, i=i, nb=nb: e.tensor_tensor(out=Ot[i][:, :], in0=psum[:, nb, 0:128], in1=Rt[i][:, :], op=ALU.mult),
                         reads=[bk(nb), "Rt%d" % i], writes=["Ot%d" % i])
                    S.op("pool", lambda e, i=i, ju=ju: e.tensor_tensor(out=attnT[:, hp, ju * 128:(ju + 1) * 128], in0=Ot[i][:, :],
                                                                       in1=SG[:, ju * 128:(ju + 1) * 128], op=ALU.mult),
                         reads=["Ot%d" % i, "SG"], writes=["attnT"])

                for kb in range(NKB_A):
                    jmin, jmax = max(kb, 4), min(kb + 4, 20)
                    c0, c1 = (jmin - kb) * 128, (jmax - kb + 1) * 128
                    u0 = (jmin - 4) * 128
                    r = kb % RING_A
                    for h in range(2):
                        def mm(e, h=h, kb=kb, c0=c0, c1=c1, u0=u0):
                            ins = None
                            a = c0
                            while a < c1:
                                bnk = 2 * h + (a // 512)
                                bend = min(c1, (a // 512 + 1) * 512)
                                ins = e.matmul(psum[:, bnk, a % 512:a % 512 + (bend - a)],
                                               lhsT=KT[h * 64:(h + 1) * 64, kb * 128:(kb + 1) * 128],
                                               rhs=QT[h * 64:(h + 1) * 64, u0 + (a - c0):u0 + (bend - c0)], start=True, stop=True)
                                a = bend
                            return ins
                        S.op("pe", mm, reads=["QT", "KT"], writes=[bk(2 * h), bk(2 * h + 1)])
                        psS = psum[:, 2 * h:2 * h + 2, :].rearrange("p a b -> p (a b)")
                        S.op("act", lambda e, h=h, r=r, c0=c0, c1=c1, psS=psS: e.activation(
                            out=PT[h][r][:, c0:c1], in_=psS[:, c0:c1], func=AF.Exp, scale=0.125),
                            reads=[bk(2 * h), bk(2 * h + 1)], writes=["PT%d_%d" % (h, r)])
                        S.op("dve", lambda e, h=h, r=r, c0=c0, c1=c1, slot=slot: e.tensor_tensor(
                            out=PT[h][r][:, c0:c1], in0=PT[h][r][:, c0:c1], in1=E_sb[slot][:, h, c0:c1], op=ALU.mult),
                            reads=["PT%d_%d" % (h, r), ekey], writes=["PT%d_%d" % (h, r)])
                    ju = kb - 5
                    if ju >= 0:
                        pv(ju)
                for ju in range(NKB_A - 5, NQB_A):
                    pv(ju)

            prefetch_A(0)
            items0 = projection_items(0)

            prev_t1 = [0]

            def after_tile(t1):
                lim, prev_t1[0] = prev_t1[0], t1
                rest = []
                for need, emit in items0:
                    if need <= lim:
                        emit()
                    else:
                        rest.append((need, emit))
                items0[:] = rest
            preamble(after_tile)
            proj_nbanks[0] = 8
            after_tile(NTA)
            assert not items0
            prefetch_A(1)
            dbg_dump("xTn", lambda c0, c1: xT_bf[:].rearrange("p a b -> p (a b)")[:, c0:c1], KC * NTA, ALLX)
            for hp in range(8):
                if hp >= 1 and hp + 1 < 8:
                    prefetch_A(hp + 1)
                if hp >= 1:
                    projection_A(hp)
                if hp == 0:
                    dbg_dump("QT0", lambda c0, c1: QT[:, c0:c1], NU, ["QT"])
                    dbg_dump("KT0", lambda c0, c1: KT[:, c0:c1], NTA, ["KT"])
                    dbg_dump("SG0", lambda c0, c1: SG[:, c0:c1], NU, ["SG"])
                    dbg_dump("V0", lambda c0, c1: V_sb[:].rearrange("p a b -> p (a b)")[:, c0:c1], NKB_A * 256, ["V"])
                attention_A(hp)
            dbg_dump("attnA", lambda c0, c1: attnT[:].rearrange("p a b -> p (a b)")[:, c0:c1], KC * NU, ["attnT"])
            S.flush()

        with ExitStack() as es2:
            h1T = sbuf(es2, "h1T", [128, KC, NU], F32)
            h1n = sbuf(es2, "h1n", [128, KC, NU], BF16)
            kvw_bf = sbuf(es2, "kvw_bf", [128, KC, 384], BF16)

            def load_wout(es, wsrc, stgs, wout_bf):
                for i in range(4):
                    s = i % 2
                    S.op("sp", lambda e, s=s, i=i: e.dma_start(out=stgs[s][:, 0:2048], in_=wsrc[:, i * 2048:(i + 1) * 2048]),
                         writes=["wstg%d" % s], dma="wstg%d" % s)
                    S.op("act", lambda e, s=s, i=i: e.activation(
                        out=wout_bf[:, :, i * 256:(i + 1) * 256], in_=stgs[s][:, 0:2048].rearrange("p (kc c) -> p kc c", kc=KC), func=AF.Copy),
                        reads=["wstg%d" % s], writes=["wout%d" % i])

            def stats_a(src_ap_fn, src_key, n, sq):
                S.op("act", lambda e, n=n: e.activation(out=sq[:, :, 0:n], in_=src_ap_fn(), func=AF.Square),
                     reads=[src_key], writes=["sq"])

            def stats_b(n, sq, rt, rstd, bank_ctr):
                b = 7
                bank_ctr[0] += 1

                def stat_mm(e, b=b, n=n):
                    ins = None
                    for kc in range(KC):
                        ins = e.matmul(psum[:, b, 0:n], lhsT=ones_bf[:, :], rhs=sq[:, kc, 0:n],
                                       start=(kc == 0), stop=(kc == KC - 1))
                    return ins
                S.op("pe", stat_mm, reads=["sq", "ones"], writes=[bk(b)])
                S.op("act", lambda e, b=b, n=n: e.activation(out=rt[:, 0:n], in_=psum[:, b, 0:n], func=AF.Ln,
                                                             scale=1.0 / D, bias=eps_t[:, 0:1]),
                     reads=[bk(b), "eps"], writes=["rt"])
                S.op("act", lambda e, n=n: e.activation(out=rstd[:, 0:n], in_=rt[:, 0:n], func=AF.Exp, scale=-0.5),
                     reads=["rt"], writes=["rstd"])

            with ExitStack() as es2a:
                stg2 = [sbuf(es2a, "stg2_%d" % i, [128, KC * TILE], F32) for i in range(2)]
                wout_bf = sbuf(es2a, "woutA_bf", [128, KC, 1024], BF16)
                sq = sbuf(es2a, "sq2", [128, KC, TILE], BF16)
                rt = sbuf(es2a, "rt2", [128, TILE], F32)
                rstd = sbuf(es2a, "rstd2", [128, TILE], F32)
                load_wout(es2a, wa_out, stg2, wout_bf)
                kvstg = sbuf(es2a, "kvstg", [128, KC * 128], F32)

                def prefetch_kvw():
                    for part in range(3):
                        src = kvw[:].rearrange("p (kc c) -> p kc c", kc=KC)[:, :, part * 128:(part + 1) * 128]
                        S.op("sp", lambda e, src=src: e.dma_start(out=kvstg[:, :].rearrange("p (kc c) -> p kc c", kc=KC), in_=src),
                             writes=["kvstg"], dma="kvstg")

                        def castkv(e, part=part):
                            ins = None
                            sv = kvstg[:, :].rearrange("p (kc c) -> p kc c", kc=KC)
                            for kc in range(KC):
                                ins = e.activation(out=kvw_bf[:, kc, part * 128:(part + 1) * 128], in_=sv[:, kc, :], func=AF.Identity,
                                                   scale=gains_sb[:, KV_G + kc:KV_G + kc + 1])
                            return ins
                        S.op("act", castkv, reads=["kvstg", "gains"], writes=["kvw"])
                bctr = [0]
                wb = [0]
                pendingA = []
                for ti, (u0, u1) in enumerate(_tiles(NU, TILE)):
                    n = u1 - u0
                    s = ti % 2
                    sv = stg2[s][:, 0:KC * n].rearrange("p (kc t) -> p kc t", kc=KC)
                    S.op("sp", lambda e, sv=sv, u0=u0, u1=u1: e.dma_start(out=sv, in_=xT_v[:, :, 512 + u0:512 + u1]),
                         writes=["wstg%d" % s], dma="wstg%d" % s)
                    for oc in range(KC):
                        b = wb[0] % 7
                        wb[0] += 1

                        def mm(e, b=b, oc=oc, u0=u0, u1=u1, n=n):
                            ins = None
                            for kc in range(KC):
                                ins = e.matmul(psum[:, b, 0:n], lhsT=wout_bf[:, kc, oc * 128:(oc + 1) * 128],
                                               rhs=attnT[:, kc, u0:u1], start=(kc == 0), stop=(kc == KC - 1))
                            return ins
                        S.op("pe", mm, reads=["wout%d" % (oc // 2), "attnT"], writes=[bk(b)])
                        S.op("dve", lambda e, b=b, oc=oc, u0=u0, u1=u1, n=n, sv=sv: e.tensor_tensor(
                            out=h1T[:, oc, u0:u1], in0=psum[:, b, 0:n], in1=sv[:, oc, :], op=ALU.add),
                            reads=[bk(b), "wstg%d" % s], writes=["h1T_%d" % ti])
                    def finish(ti=ti, u0=u0, u1=u1, n=n):
                        stats_b(n, sq, rt, rstd, bctr)
                        S.op("dve", lambda e: e.tensor_tensor(
                            out=h1n[:, :, u0:u1], in0=h1T[:, :, u0:u1], in1=rstd[:, 0:n].unsqueeze(1).to_broadcast([128, KC, n]), op=ALU.mult),
                            reads=["h1T_%d" % ti, "rstd"], writes=["h1n"])
                    if ti == 3:
                        prefetch_kvw()
                    if pendingA:
                        pendingA.pop()()
                    stats_a(lambda u0=u0, u1=u1: h1T[:, :, u0:u1], "h1T_%d" % ti, n, sq)
                    pendingA.append(finish)
                pendingA.pop()()
                dbg_dump("h1T", lambda c0, c1: h1T[:].rearrange("p a b -> p (a b)")[:, c0:c1], KC * NU, ["h1n"])
                dbg_dump("h1n", lambda c0, c1: h1n[:].rearrange("p a b -> p (a b)")[:, c0:c1], KC * NU, ["h1n"])
                S.flush()

            with ExitStack() as es2b:
                stgB = [sbuf(es2b, "stgB%d" % i, [128, KC * 256], F32) for i in range(2)]
                wqg_bf = [sbuf(es2b, "wqg%d" % i, [128, KC, 256], BF16) for i in range(2)]
                KshT = [sbuf(es2b, "KshT%d" % g, [128, NU], BF16) for g in range(2)]
                Vsh = sbuf(es2b, "Vsh", [128, NKB_B, 128], BF16)
                QTb = sbuf(es2b, "QTb", [128, NW], BF16)
                SGb = sbuf(es2b, "SGb", [128, NW], BF16)
                PTb = [[sbuf(es2b, "PTb%d_%d" % (h, s), [128, 256], BF16) for s in range(RING_B)] for h in range(2)]
                bstgB = sbuf(es2b, "bstgB", [128, 512], F32)
                EB = [sbuf(es2b, "EB%d" % i, [128, 2, 256], BF16) for i in range(2)]
                TtmpB = [sbuf(es2b, "TtmpB%d" % i, [128, 512], F32) for i in range(2)]
                RtB = [sbuf(es2b, "RtB%d" % i, [128, 128], F32) for i in range(2)]
                OtB = [sbuf(es2b, "OtB%d" % i, [128, 128], F32) for i in range(2)]
                sctr = [0]

                def next_stgB():
                    s = sctr[0] % 2
                    sctr[0] += 1
                    return s

                def prefetch_B(hp):
                    s = next_stgB()
                    slot = hp % 2
                    S.op("sp", lambda e, s=s, hp=hp: e.dma_start(out=stgB[s][:, :], in_=wb_qg[hp]),
                         writes=["stgB%d" % s], dma="stgB%d" % s)

                    def cast(e, s=s, slot=slot):
                        ins = None
                        sv = stgB[s][:, :].rearrange("p (kc c) -> p kc c", kc=KC)
                        for kc in range(KC):
                            ins = e.activation(out=wqg_bf[slot][:, kc, :], in_=sv[:, kc, :], func=AF.Identity,
                                               scale=gains_sb[:, B_G + kc:B_G + kc + 1])
                        return ins
                    S.op("act", cast, reads=["stgB%d" % s, "gains"], writes=["wqg%d" % slot])
                    S.op("sp", lambda e, hp=hp: e.dma_start(out=bstgB[:, :], in_=biasB[hp]), writes=["bstgB"], dma="bstgB")
                    S.op("act", lambda e, slot=slot: e.activation(out=EB[slot][:].rearrange("p a b -> p (a b)"), in_=bstgB[:, :], func=AF.Exp),
                         reads=["bstgB"], writes=["EB%d" % slot])

                    def corners(e, slot=slot):
                        ins = None
                        for h in range(2):
                            e.memset(EB[slot][64:128, h, 0:64], 0.0)
                            ins = e.memset(EB[slot][0:64, h, 192:256], 0.0)
                        return ins
                    S.op("pool", corners, reads=[], writes=["EB%d" % slot])

                pbank = [0]

                def next_pb():
                    b = pbank[0]
                    pbank[0] = (b + 1) % 8
                    return b

                for g in range(2):
                    for (c0, c1) in _tiles(NU, 512):
                        b = next_pb()
                        n = c1 - c0

                        def mm(e, b=b, g=g, c0=c0, c1=c1, n=n):
                            ins = None
                            for kc in range(KC):
                                ins = e.matmul(psum[:, b, 0:n], lhsT=kvw_bf[:, kc, g * 128:(g + 1) * 128], rhs=h1n[:, kc, c0:c1],
                                               start=(kc == 0), stop=(kc == KC - 1))
                            return ins
                        S.op("pe", mm, reads=["kvw", "h1n"], writes=[bk(b)])
                        S.op("dve", lambda e, b=b, g=g, c0=c0, c1=c1, n=n: e.tensor_copy(out=KshT[g][:, c0:c1], in_=psum[:, b, 0:n]),
                             reads=[bk(b)], writes=["KshT%d" % g])
                for tb0 in range(0, NKB_B, 4):
                    tbs = [tb for tb in range(tb0, tb0 + 4) if tb < NKB_B]
                    b = next_pb()

                    def mmv(e, b=b, tbs=tbs):
                        ins = None
                        for i, tb in enumerate(tbs):
                            for kc in range(KC):
                                ins = e.matmul(psum[:, b, i * 128:(i + 1) * 128], lhsT=h1n[:, kc, tb * 128:(tb + 1) * 128],
                                               rhs=kvw_bf[:, kc, 256:384], start=(kc == 0), stop=(kc == KC - 1))
                        return ins
                    S.op("pe", mmv, reads=["kvw", "h1n"], writes=[bk(b)])
                    nt = len(tbs)
                    S.op("act", lambda e, b=b, tb0=tb0, nt=nt: e.activation(
                        out=Vsh[:, tb0:tb0 + nt, :], in_=psum[:, b, 0:nt * 128].rearrange("p (a c) -> p a c", a=nt), func=AF.Copy),
                        reads=[bk(b)], writes=["Vsh"])

                def projection_B(hp):
                    slot = hp % 2
                    wkey = "wqg%d" % slot
                    ti = 0
                    for (c0, c1) in _tiles(NW, 512):
                        b = next_pb()
                        n = c1 - c0

                        def mm(e, b=b, c0=c0, c1=c1, n=n):
                            ins = None
                            for kc in range(KC):
                                ins = e.matmul(psum[:, b, 0:n], lhsT=wqg_bf[slot][:, kc, 128:256], rhs=h1n[:, kc, 128 + c0:128 + c1],
                                               start=(kc == 0), stop=(kc == KC - 1))
                            return ins
                        S.op("pe", mm, reads=[wkey, "h1n"], writes=[bk(b)])
                        tt = ti % 2
                        ti += 1
                        S.op("act", lambda e, b=b, n=n, tt=tt: e.activation(out=TtmpB[tt][:, 0:n], in_=psum[:, b, 0:n], func=AF.Exp, scale=-1.0),
                             reads=[bk(b)], writes=["TtmpB%d" % tt])
                        S.op("act", lambda e, n=n, tt=tt: e.activation(out=TtmpB[tt][:, 0:n], in_=TtmpB[tt][:, 0:n], func=AF.Ln, bias=one_t[:, 0:1]),
                             reads=["TtmpB%d" % tt, "one"], writes=["TtmpB%d" % tt])
                        S.op("act", lambda e, n=n, tt=tt: e.activation(out=TtmpB[tt][:, 0:n], in_=TtmpB[tt][:, 0:n], func=AF.Exp, scale=-1.0),
                             reads=["TtmpB%d" % tt], writes=["TtmpB%d" % tt])
                        S.op("dve", lambda e, b=b, c0=c0, c1=c1, n=n, tt=tt: e.tensor_tensor(
                            out=SGb[:, c0:c1], in0=psum[:, b, 0:n], in1=TtmpB[tt][:, 0:n], op=ALU.mult),
                            reads=[bk(b), "TtmpB%d" % tt], writes=["SGb"])
                    for (c0, c1) in _tiles(NW, 512):
                        b = next_pb()
                        n = c1 - c0

                        def mm(e, b=b, c0=c0, c1=c1, n=n):
                            ins = None
                            for kc in range(KC):
                                ins = e.matmul(psum[:, b, 0:n], lhsT=wqg_bf[slot][:, kc, 0:128], rhs=h1n[:, kc, 128 + c0:128 + c1],
                                               start=(kc == 0), stop=(kc == KC - 1))
                            return ins
                        S.op("pe", mm, reads=[wkey, "h1n"], writes=[bk(b)])
                        S.op("dve", lambda e, b=b, c0=c0, c1=c1, n=n: e.tensor_copy(out=QTb[:, c0:c1], in_=psum[:, b, 0:n]),
                             reads=[bk(b)], writes=["QTb"])

                ndb = [0]

                def attention_B(hp):
                    slot = hp % 2
                    g = hp // 4
                    ekey = "EB%d" % slot

                    def pv(jw):
                        i = ndb[0] % 2
                        ndb[0] += 1
                        nb, db = 4 + i, 6 + i

                        def mm(e, jw=jw, nb=nb, db=db):
                            ins = None
                            kbs = [jw, jw + 1]
                            for idx, kb2 in enumerate(kbs):
                                col = (jw + 1 - kb2) * 128
                                st, sp_ = (idx == 0), (idx == len(kbs) - 1)
                                r = kb2 % RING_B
                                for h in range(2):
                                    e.matmul(psum[h * 64:(h + 1) * 64, nb, 0:128], lhsT=Vsh[:, kb2, g * 64:(g + 1) * 64],
                                             rhs=PTb[h][r][:, col:col + 128], start=st, stop=sp_)
                                for h in range(2):
                                    ins = e.matmul(psum[h * 64:(h + 1) * 64, db, 0:128], lhsT=validB_sb[:, kb2, :],
                                                   rhs=PTb[h][r][:, col:col + 128], start=st, stop=sp_)
                            return ins
                        rd = ["Vsh", "validB"] + ["PTb%d_%d" % (h, kb2 % RING_B) for h in range(2) for kb2 in (jw, jw + 1)]
                        S.op("pe", mm, reads=rd, writes=[bk(nb), bk(db)])
                        S.op("act", lambda e, i=i, db=db: e.activation(out=RtB[i][:, :], in_=psum[:, db, 0:128], func=AF.Ln, bias=esink[:, hp:hp + 1]),
                             reads=[bk(db), "esink"], writes=["RtB%d" % i])
                        S.op("act", lambda e, i=i: e.activation(out=RtB[i][:, :], in_=RtB[i][:, :], func=AF.Exp, scale=-1.0),
                             reads=["RtB%d" % i], writes=["RtB%d" % i])
                        S.op("dve", lambda e, i=i, nb=nb: e.tensor_tensor(out=OtB[i][:, :], in0=psum[:, nb, 0:128], in1=RtB[i][:, :], op=ALU.mult),
                             reads=[bk(nb), "RtB%d" % i], writes=["OtB%d" % i])
                        S.op("pool", lambda e, i=i, jw=jw: e.tensor_tensor(out=attnT[:, hp, jw * 128:(jw + 1) * 128], in0=OtB[i][:, :],
                                                                           in1=SGb[:, jw * 128:(jw + 1) * 128], op=ALU.mult),
                             reads=["OtB%d" % i, "SGb"], writes=["attnT"])

                    for kb in range(NKB_B):
                        jmin, jmax = max(kb - 1, 0), min(kb, NQB_B - 1)
                        c0, c1 = (jmin + 1 - kb) * 128, (jmax + 1 - kb + 1) * 128
                        w0 = jmin * 128
                        r = kb % RING_B
                        for h in range(2):
                            S.op("pe", lambda e, h=h, kb=kb, c0=c0, c1=c1, w0=w0: e.matmul(
                                psum[:, 2 * h, c0:c1], lhsT=KshT[g][h * 64:(h + 1) * 64, kb * 128:(kb + 1) * 128],
                                rhs=QTb[h * 64:(h + 1) * 64, w0:w0 + (c1 - c0)], start=True, stop=True),
                                reads=["QTb", "KshT%d" % g], writes=[bk(2 * h)])
                            S.op("act", lambda e, h=h, r=r, c0=c0, c1=c1: e.activation(
                                out=PTb[h][r][:, c0:c1], in_=psum[:, 2 * h, c0:c1], func=AF.Exp, scale=0.125),
                                reads=[bk(2 * h)], writes=["PTb%d_%d" % (h, r)])
                            S.op("dve", lambda e, h=h, r=r, c0=c0, c1=c1: e.tensor_tensor(
                                out=PTb[h][r][:, c0:c1], in0=PTb[h][r][:, c0:c1], in1=EB[slot][:, h, c0:c1], op=ALU.mult),
                                reads=["PTb%d_%d" % (h, r), ekey], writes=["PTb%d_%d" % (h, r)])
                        jw = kb - 2
                        if jw >= 0:
                            pv(jw)
                    for jw in range(NKB_B - 2, NQB_B):
                        pv(jw)

                prefetch_B(0)
                for hp in range(8):
                    if hp + 1 < 8:
                        prefetch_B(hp + 1)
                    projection_B(hp)
                    attention_B(hp)
                dbg_dump("attnB", lambda c0, c1: attnT[:, c0 // NW, c0 % NW:c0 % NW + (c1 - c0)], KC * NW, ["attnT"])
                S.flush()

            with ExitStack() as es2c:
                TB = 256
                wstg = [sbuf(es2c, "wstgC%d" % i, [128, 2048], F32) for i in range(2)]
                wout_bf = sbuf(es2c, "woutB_bf", [128, KC, 1024], BF16)
                h2 = [sbuf(es2c, "h2_%d" % i, [128, KC, TB], F32) for i in range(3)]
                sq = sbuf(es2c, "sq3", [128, KC, TB], BF16)
                rt = sbuf(es2c, "rt3", [128, TB], F32)
                rstd = sbuf(es2c, "rstd3", [128, TB], F32)
                load_wout(es2c, wb_out, wstg, wout_bf)
                bctr = [0]
                wb = [0]
                pendingB = []
                for ti, (w0, w1) in enumerate(_tiles(NW, TB)):
                    n = w1 - w0
                    s = ti % 3
                    for oc in range(KC):
                        b = wb[0] % 7
                        wb[0] += 1

                        def mm(e, b=b, oc=oc, w0=w0, w1=w1, n=n):
                            ins = None
                            for kc in range(KC):
                                ins = e.matmul(psum[:, b, 0:n], lhsT=wout_bf[:, kc, oc * 128:(oc + 1) * 128],
                                               rhs=attnT[:, kc, w0:w1], start=(kc == 0), stop=(kc == KC - 1))
                            return ins
                        S.op("pe", mm, reads=["wout%d" % (oc // 2), "attnT"], writes=[bk(b)])
                        S.op("dve", lambda e, b=b, oc=oc, w0=w0, w1=w1, n=n, s=s: e.tensor_tensor(
                            out=h2[s][:, oc, 0:n], in0=psum[:, b, 0:n], in1=h1T[:, oc, 128 + w0:128 + w1], op=ALU.add),
                            reads=[bk(b)], writes=["h2_%d" % s])
                    def finishB(s=s, w0=w0, w1=w1, n=n):
                        stats_b(n, sq, rt, rstd, bctr)

                        def fin(e):
                            ins = None
                            for oc in range(KC):
                                ins = e.scalar_tensor_tensor(out=h2[s][:, oc, 0:n], in0=h2[s][:, oc, 0:n],
                                                             scalar=gains_sb[:, F_G + oc:F_G + oc + 1], in1=rstd[:, 0:n],
                                                             op0=ALU.mult, op1=ALU.mult)
                            return ins
                        S.op("dve", fin, reads=["h2_%d" % s, "rstd", "gains"], writes=["h2_%d" % s])
                        tok = S.op("sp", lambda e: e.dma_start(out=outT_v[:, :, w0:w1], in_=h2[s][:, :, 0:n]),
                                   reads=["h2_%d" % s], dma="out%d" % s)
                        out_tokens.append(tok)
                    if pendingB:
                        pendingB.pop()()
                    stats_a(lambda s=s, n=n: h2[s][:, :, 0:n], "h2_%d" % s, n, sq)
                    pendingB.append(finishB)
                pendingB.pop()()
                final = {}
                for key, val in out_tokens:
                    final[key] = max(final.get(key, 0), val)
                S.wait_tokens("sp", list(final.items()))
                S.flush()
    return nc


def _t5_bucket_np(rel):
    nb = 16
    max_exact = 8
    ret = np.where(rel > 0, nb, 0)
    n = np.abs(rel)
    nf = np.maximum(n, 1).astype(np.float32)
    large = max_exact + (np.log(nf / np.float32(max_exact)) / np.float32(math.log(128 / max_exact))
                         * np.float32(nb - max_exact)).astype(np.int32)
    large = np.minimum(large, nb - 1)
    return ret + np.where(n < max_exact, n, large)


def _prep_shared(a_norm, a_w_in, a_rel_bias, a_w_out, kv_norm, kv_w, t5_bias, b_norm, b_w_in, b_sinks, b_w_out, final_norm):
    f = np.float32
    w_in = np.asarray(a_w_in[0], f)
    w4 = w_in.reshape(KC, 128, 4, 8, 128)
    wa_qkg = np.ascontiguousarray(np.transpose(w4[:, :, [0, 1, 3]], (3, 1, 0, 2, 4))).reshape(8, 128, KC * 384)
    wv = w_in[:, 2048:3072].reshape(KC, 128, 4, 256)
    wa_v = np.ascontiguousarray(np.transpose(wv, (2, 1, 0, 3))).reshape(4, 128, KC * 256)
    wa_out = np.ascontiguousarray(np.transpose(np.asarray(a_w_out[0], f).reshape(KC, 128, 4, 256), (1, 2, 0, 3))).reshape(128, KC * 1024)
    wb_out = np.ascontiguousarray(np.transpose(np.asarray(b_w_out[0], f).reshape(KC, 128, 4, 256), (1, 2, 0, 3))).reshape(128, KC * 1024)

    def gcol(v):
        return np.asarray(v, f).reshape(KC, 128).T
    gains = np.ascontiguousarray(np.concatenate([gcol(a_norm[0]), gcol(kv_norm), gcol(b_norm[0]), gcol(final_norm)], axis=1))
    k = np.arange(128)[:, None]
    q = np.arange(640)[None, :]
    idxA = np.clip(q - k, -256, 256) + 256
    rb = np.asarray(a_rel_bias[0], f)
    bA = rb[idxA]
    biasA = np.ascontiguousarray(np.transpose(bA.reshape(128, 640, 8, 2), (2, 0, 3, 1))).reshape(8, 128, 1280)
    kvw_ = np.asarray(kv_w, f).reshape(KC, 128, 256)
    kcat = np.concatenate([kvw_[:, :, 0:64], kvw_[:, :, 0:64], kvw_[:, :, 64:128], kvw_[:, :, 64:128], kvw_[:, :, 128:256]], axis=2)
    kvw = np.ascontiguousarray(np.transpose(kcat, (1, 0, 2))).reshape(128, KC * 384)
    wb = np.asarray(b_w_in[0], f).reshape(KC, 128, 2, 8, 128)
    wb_qg = np.ascontiguousarray(np.transpose(wb, (3, 1, 0, 2, 4))).reshape(8, 128, KC * 256)
    qb = np.arange(256)[None, :]
    bucket = _t5_bucket_np((k - qb).astype(np.int32))
    tb = np.asarray(t5_bias, f)[bucket]
    biasB = np.ascontiguousarray(np.transpose(tb.reshape(128, 256, 8, 2), (2, 0, 3, 1))).reshape(8, 128, 512)
    sk = np.asarray(b_sinks[0], f).reshape(8, 2)
    sinkB = np.ascontiguousarray(np.repeat(sk.T, 64, axis=0))
    return dict(wa_qkg=wa_qkg, wa_v=wa_v, wa_out=wa_out, gains=gains, biasA=biasA, kvw=kvw, wb_qg=wb_qg,
                wb_out=wb_out, biasB=biasB, sinkB=sinkB)


def _prep_core(x, c):
    b, half = c // 2, c % 2
    T0 = half * NOWN
    lo = T0 - HALO
    xe = np.zeros((NTA, D), np.float32)
    src0 = max(lo, 0)
    xe[src0 - lo:, :] = x[b, src0:T0 + NOWN, :]
    xTc = np.ascontiguousarray(xe.T)
    tpos = lo + np.arange(NTA)
    vA = (tpos >= 0).astype(np.float32).reshape(NKB_A, 128).T
    validA = np.ascontiguousarray(np.repeat(vA[:, :, None], 64, axis=2)).reshape(128, NKB_A * 64).astype(ml_dtypes.bfloat16)
    upos = T0 - 128 + np.arange(NU)
    vB = (upos >= 0).astype(np.float32).reshape(NKB_B, 128).T
    validB = np.ascontiguousarray(np.repeat(vB[:, :, None], 64, axis=2)).reshape(128, NKB_B * 64).astype(ml_dtypes.bfloat16)
    return dict(xT=xTc, validA=validA, validB=validB)


_NC_CACHE = {}


def kernel(x, a_norm, a_w_in, a_rel_bias, a_w_out, kv_norm, kv_w, t5_bias, b_norm, b_w_in, b_sinks, b_w_out, final_norm,
           _debug=()):
    x = np.asarray(x, np.float32)
    shared = _prep_shared(np.asarray(a_norm), np.asarray(a_w_in), np.asarray(a_rel_bias), np.asarray(a_w_out),
                          np.asarray(kv_norm), np.asarray(kv_w), np.asarray(t5_bias), np.asarray(b_norm),
                          np.asarray(b_w_in), np.asarray(b_sinks), np.asarray(b_w_out), np.asarray(final_norm))
    in_maps = []
    for c in range(N_CORES):
        m = dict(shared)
        m.update(_prep_core(x, c))
        in_maps.append(m)
    key = tuple(_debug)
    if key not in _NC_CACHE:
        _NC_CACHE[key] = build_nc(debug=key)
    nc = _NC_CACHE[key]
    res = run_bass_kernel_spmd(nc, in_maps, core_ids=list(range(N_CORES)))
    out = np.empty((4, SEQ, D), np.float32)
    for c in range(N_CORES):
        b, half = c // 2, c % 2
        out[b, half * NOWN:(half + 1) * NOWN, :] = np.asarray(res.results[c]["outT"]).T
    if _debug:
        return out, res.results
    return out
```
